# Optimizing a Trainium2 kernel written in Bass

```python
import jax, jax.numpy as jnp
from jax import lax
import numpy as np

D_MODEL = 1024
BATCH = 32
SEQ = 256
DEPTH = 2
DEC_BATCH = 2
DEC_SEQ = 4096
PAST_LEN = 512

GRID_W = 64
N_EVEN = (DEPTH + 1) // 2
N_ODD = DEPTH // 2
FN_GROUPS = 4
FN_CH = D_MODEL // 2 // FN_GROUPS
FN_WIDTH = FN_GROUPS * FN_CH
NA_HEADS = 8
NA_HEAD_DIM = D_MODEL // 2 // NA_HEADS
NA_WIDTH = NA_HEADS * NA_HEAD_DIM
WIN_R = 8
WIN_C = 16
Q_BLOCK = 128
EVEN_IN = 2 * FN_WIDTH + 4 * NA_WIDTH
RW_HEAD_DIM = 64
RW_HEADS = D_MODEL // RW_HEAD_DIM
DECAY_LORA = 64
AAA_LORA = 64
GATE_LORA = 128
N_LERP = 6
DECAY_SCALE = 0.606531
ALPHA = (2 * DEPTH) ** 0.25
BETA = (8 * DEPTH) ** -0.25
LN_EPS = 1e-6
GN_EPS = 64e-5
NEG_INF = -1e30

kernel_name = 'hybrid_fnet_natten_rwkv7_diffusion_step'


def layer_norm(x, g=None, b=None, eps=LN_EPS):
    xf = x.astype(jnp.float32)
    mu = xf.mean(-1, keepdims=True)
    var = jnp.square(xf - mu).mean(-1, keepdims=True)
    y = (xf - mu) * lax.rsqrt(var + eps)
    if g is not None:
        y = y * g.astype(jnp.float32) + b.astype(jnp.float32)
    return y.astype(x.dtype)


def modulation(cvec, w, b):
    m = jax.nn.silu(cvec) @ w + b
    shift, scale, gate = jnp.split(m[:, None, :], 3, axis=-1)
    return shift, scale, gate


def fourier_mix(a):
    B, L, _ = a.shape
    af = a.reshape(B, L, FN_GROUPS, FN_CH).astype(jnp.float32)
    f = jnp.fft.fft2(af, axes=(1, 3), norm='ortho').real
    return f.astype(a.dtype)


def ctx_attention(q, k, v):
    B, S, H, d = q.shape
    nb = S // Q_BLOCK
    qb = jnp.moveaxis(q.reshape(B, nb, Q_BLOCK, H, d), 1, 0)
    scale = d ** -0.5

    def block(qi):
        s = jnp.einsum('bqhd,bkhd->bhqk', qi, k).astype(jnp.float32) * scale
        p = jax.nn.softmax(s, axis=-1).astype(v.dtype)
        return jnp.einsum('bhqk,bkhd->bqhd', p, v)

    o = lax.map(block, qb)
    return jnp.moveaxis(o, 0, 1).reshape(B, S, H, d)


def neighbourhood_attention(q, k, v, k_ctx, v_ctx, rpb):
    B, L, H, d = q.shape
    rows = L // GRID_W
    wr = min(WIN_R, rows)
    r_q = jnp.arange(rows)
    row_idx = jnp.clip(r_q - wr // 2, 0, rows - wr)[:, None] + jnp.arange(wr)[None, :]
    c_q = jnp.arange(GRID_W)
    c_start = jnp.clip(c_q - WIN_C // 2, 0, GRID_W - WIN_C)
    col_in = (c_q[None, :] >= c_start[:, None]) & (c_q[None, :] < c_start[:, None] + WIN_C)
    dr = row_idx - r_q[:, None] + (WIN_R - 1)
    dc = jnp.clip(c_q[None, :] - c_q[:, None] + (WIN_C - 1), 0, 2 * WIN_C - 2)
    bias = rpb.astype(jnp.float32)[:, dr[:, None, :, None], dc[None, :, None, :]]
    bias = jnp.where(col_in[None, None, :, None, :], bias, NEG_INF)
    qg = q.reshape(B, rows, GRID_W, H, d)
    kg = k.reshape(B, rows, GRID_W, H, d)[:, row_idx]
    vg = v.reshape(B, rows, GRID_W, H, d)[:, row_idx]
    scale = d ** -0.5
    s_loc = jnp.einsum('brchd,brwkhd->bhrcwk', qg, kg).astype(jnp.float32) * scale + bias[None]
    s_ctx = jnp.einsum('brchd,bphd->bhrcp', qg, k_ctx).astype(jnp.float32) * scale
    n_loc = wr * GRID_W
    s = jnp.concatenate([s_loc.reshape(B, H, rows, GRID_W, n_loc), s_ctx], axis=-1)
    p = jax.nn.softmax(s, axis=-1).astype(v.dtype)
    p_loc = p[..., :n_loc].reshape(B, H, rows, GRID_W, wr, GRID_W)
    o = (jnp.einsum('bhrcwk,brwkhd->brchd', p_loc, vg)
         + jnp.einsum('bhrcp,bphd->brchd', p[..., n_loc:], v_ctx))
    return o.reshape(B, L, H, d)


def even_mixer(u, w_in, w_fnet, rpb, w_out, ctx_kv):
    B, L, _ = u.shape
    h = u @ w_in
    splits = np.cumsum([FN_WIDTH, FN_WIDTH, NA_WIDTH, NA_WIDTH, NA_WIDTH]).tolist()
    a, za, q, k, v, zb = jnp.split(h, splits, axis=-1)
    a = jnp.einsum('blgc,gce->blge', fourier_mix(a), w_fnet).reshape(B, L, FN_WIDTH) * jax.nn.silu(za)
    q = q.reshape(B, L, NA_HEADS, NA_HEAD_DIM)
    k = k.reshape(B, L, NA_HEADS, NA_HEAD_DIM)
    v = v.reshape(B, L, NA_HEADS, NA_HEAD_DIM)
    if ctx_kv is None:
        o = ctx_attention(q, k, v)
        kv = (k, v)
    else:
        o = neighbourhood_attention(q, k, v, ctx_kv[0], ctx_kv[1], rpb)
        kv = None
    o = o.reshape(B, L, NA_WIDTH) * jax.nn.silu(zb)
    return jnp.concatenate([a, o], axis=-1) @ w_out, kv


def rwkv_mixer(u, rw, init_state):
    B, L, D = u.shape
    H, N = RW_HEADS, RW_HEAD_DIM
    f32 = jnp.float32
    prev = jnp.pad(u, ((0, 0), (1, 0), (0, 0)))[:, :L]
    nxt = jnp.pad(u, ((0, 0), (0, 1), (0, 0)))[:, 1:]
    dx = 0.5 * (prev + nxt) - u
    xr, xw, xk, xv, xa, xg = [u + dx * rw['mu'][i] for i in range(N_LERP)]
    r = xr @ rw['w_rkvz'][0]
    k = xk @ rw['w_rkvz'][1]
    v = xv @ rw['w_rkvz'][2]
    z = u @ rw['w_rkvz'][3]
    wl = rw['w0'][:, None, None, :] + jnp.einsum(
        'eblr,erd->ebld', jnp.tanh(jnp.einsum('bld,edr->eblr', xw, rw['w1'])), rw['w2'])
    decay = jnp.exp(-DECAY_SCALE * jax.nn.sigmoid(wl.astype(f32)))
    a = jax.nn.sigmoid((rw['a0'][:, None, None, :] + jnp.einsum(
        'eblr,erd->ebld', jnp.einsum('bld,edr->eblr', xa, rw['a1']), rw['a2'])).astype(f32))
    g = jax.nn.sigmoid(xg @ rw['g1']) @ rw['g2']
    kf = k.astype(f32)
    rf = r.astype(f32)
    vf = v.astype(f32)

    def heads(t):
        return t.reshape(t.shape[:-1] + (H, N))

    kk = heads(kf * rw['k_k'].astype(f32))
    kk = kk * lax.rsqrt(jnp.sum(kk * kk, axis=-1, keepdims=True) + 1e-12)
    kd = kf[None] * (1.0 + (a - 1.0) * rw['k_a'].astype(f32))

    def both(t):
        return jnp.moveaxis(jnp.stack([t, jnp.flip(t, 1)]), 2, 0)

    def per_dir(t):
        return jnp.moveaxis(jnp.stack([t[0], jnp.flip(t[1], 1)]), 2, 0)

    xs = (both(heads(rf)), per_dir(heads(decay)), per_dir(heads(kd)),
          both(heads(vf)), both(kk), per_dir(heads(a)))

    def step(S, inp):
        r_t, w_t, k_t, v_t, kk_t, a_t = inp
        sa = jnp.einsum('ebhvk,ebhk->ebhv', S, kk_t)
        S = (S * w_t[..., None, :] - sa[..., None] * (kk_t * a_t)[..., None, :]
             + v_t[..., None] * k_t[..., None, :])
        return S, jnp.einsum('ebhvk,ebhk->ebhv', S, r_t)

    if init_state is None:
        S0 = jnp.zeros((2, B, H, N, N), f32)
    else:
        S0 = init_state.astype(f32)
    S_fin, y = lax.scan(step, S0, xs)
    y = jnp.moveaxis(y[:, 0] + jnp.flip(y[:, 1], 0), 0, 1)
    mu = y.mean(-1, keepdims=True)
    var = jnp.square(y - mu).mean(-1, keepdims=True)
    yn = (y - mu) * lax.rsqrt(var + GN_EPS)
    yn = yn * heads(rw['lnx_g'].astype(f32)) + heads(rw['lnx_b'].astype(f32))
    bonus = jnp.sum(heads(rf) * heads(kd.mean(0)) * rw['r_k'].astype(f32), axis=-1, keepdims=True) * heads(vf)
    o = (yn + bonus).reshape(B, L, D).astype(u.dtype) * g * jax.nn.silu(z)
    return o @ rw['w_out'], S_fin


def setup_inputs(seed: int = 0) -> dict:
    key = jax.random.key(seed)
    ks = iter(jax.random.split(key, 40))
    f32 = jnp.float32

    def nrm(shape, std):
        return std * jax.random.normal(next(ks), shape, f32)

    D = D_MODEL
    return {
        'x_prompt': nrm((BATCH, SEQ, D), 1.0),
        'x_sample': nrm((DEC_BATCH, DEC_SEQ, D), 1.0),
        'cache_k': nrm((DEC_BATCH, N_EVEN, PAST_LEN, NA_HEADS, NA_HEAD_DIM), 1.0),
        'cache_v': nrm((DEC_BATCH, N_EVEN, PAST_LEN, NA_HEADS, NA_HEAD_DIM), 1.0),
        'state_rwkv': nrm((DEC_BATCH, N_ODD, 2, RW_HEADS, RW_HEAD_DIM, RW_HEAD_DIM), 0.5),
        'c': nrm((DEC_BATCH, D), 1.0),
        'c_ctx': nrm((D,), 1.0),
        'ada_w': nrm((DEPTH, D, 3 * D), 0.1 * D ** -0.5),
        'ada_b': nrm((DEPTH, 3 * D), 0.01),
        'post_ln_g': 1.0 + nrm((DEPTH, D), 0.01),
        'post_ln_b': nrm((DEPTH, D), 0.01),
        'ev_w_in': nrm((N_EVEN, D, EVEN_IN), D ** -0.5),
        'ev_w_fnet': nrm((N_EVEN, FN_GROUPS, FN_CH, FN_CH), FN_CH ** -0.5),
        'ev_rpb': nrm((N_EVEN, NA_HEADS, 2 * WIN_R - 1, 2 * WIN_C - 1), 0.1),
        'ev_w_out': nrm((N_EVEN, FN_WIDTH + NA_WIDTH, D), BETA * (FN_WIDTH + NA_WIDTH) ** -0.5),
        'rw_mu': jax.random.uniform(next(ks), (N_ODD, N_LERP, D), f32),
        'rw_w_rkvz': nrm((N_ODD, 4, D, D), D ** -0.5),
        'rw_w0': nrm((N_ODD, 2, D), 0.5),
        'rw_w1': nrm((N_ODD, 2, D, DECAY_LORA), D ** -0.5),
        'rw_w2': nrm((N_ODD, 2, DECAY_LORA, D), 0.5 * DECAY_LORA ** -0.5),
        'rw_a0': nrm((N_ODD, 2, D), 0.5),
        'rw_a1': nrm((N_ODD, 2, D, AAA_LORA), D ** -0.5),
        'rw_a2': nrm((N_ODD, 2, AAA_LORA, D), 0.5 * AAA_LORA ** -0.5),
        'rw_g1': nrm((N_ODD, D, GATE_LORA), D ** -0.5),
        'rw_g2': nrm((N_ODD, GATE_LORA, D), GATE_LORA ** -0.5),
        'rw_k_k': 1.0 + nrm((N_ODD, D), 0.1),
        'rw_k_a': 1.0 + nrm((N_ODD, D), 0.1),
        'rw_r_k': nrm((N_ODD, RW_HEADS, RW_HEAD_DIM), 0.1),
        'rw_lnx_g': 1.0 + nrm((N_ODD, D), 0.01),
        'rw_lnx_b': nrm((N_ODD, D), 0.01),
        'rw_w_out': nrm((N_ODD, D, D), BETA * D ** -0.5),
    }


def reference(x_prompt, x_sample, cache_k, cache_v, state_rwkv, c, c_ctx,
              ada_w, ada_b, post_ln_g, post_ln_b,
              ev_w_in, ev_w_fnet, ev_rpb, ev_w_out,
              rw_mu, rw_w_rkvz, rw_w0, rw_w1, rw_w2, rw_a0, rw_a1, rw_a2,
              rw_g1, rw_g2, rw_k_k, rw_k_a, rw_r_k, rw_lnx_g, rw_lnx_b, rw_w_out):

    def rw_params(i):
        return dict(mu=rw_mu[i], w_rkvz=rw_w_rkvz[i], w0=rw_w0[i], w1=rw_w1[i], w2=rw_w2[i],
                    a0=rw_a0[i], a1=rw_a1[i], a2=rw_a2[i], g1=rw_g1[i], g2=rw_g2[i],
                    k_k=rw_k_k[i], k_a=rw_k_a[i], r_k=rw_r_k[i], lnx_g=rw_lnx_g[i],
                    lnx_b=rw_lnx_b[i], w_out=rw_w_out[i])

    def sublayer(l, x, cvec, ctx):
        shift, scale, gate = modulation(cvec, ada_w[l], ada_b[l])
        u = layer_norm(x) * (1 + scale) + shift
        i = l // 2
        if l % 2 == 0:
            out, st = even_mixer(u, ev_w_in[i], ev_w_fnet[i], ev_rpb[i], ev_w_out[i], ctx)
        else:
            out, st = rwkv_mixer(u, rw_params(i), ctx)
        x = layer_norm(ALPHA * x + (1 + gate) * out, post_ln_g[l], post_ln_b[l])
        return x, st

    x = x_prompt
    ks_, vs_, ss_ = [], [], []
    for l in range(DEPTH):
        x, st = sublayer(l, x, c_ctx[None, :], None)
        if l % 2 == 0:
            ks_.append(st[0])
            vs_.append(st[1])
        else:
            ss_.append(jnp.moveaxis(st, 0, 1))
    y_prompt = x
    new_k = jnp.stack(ks_, axis=1)
    new_v = jnp.stack(vs_, axis=1)
    new_state = jnp.stack(ss_, axis=1)

    x = x_sample
    for l in range(DEPTH):
        i = l // 2
        if l % 2 == 0:
            ctx = (cache_k[:, i], cache_v[:, i])
        else:
            ctx = jnp.moveaxis(state_rwkv[:, i], 1, 0)
        x, _ = sublayer(l, x, c, ctx)
    y_sample = x

    return (y_prompt, y_sample, new_k, new_v, new_state)
```

```python
import os
import numpy as np
import ml_dtypes
from contextlib import ExitStack
import concourse.bass as bass
import concourse.mybir as mybir
from concourse.bass_utils import run_bass_kernel_spmd

F32 = mybir.dt.float32
BF16 = mybir.dt.bfloat16
AF = mybir.ActivationFunctionType
ALU = mybir.AluOpType
AX = mybir.AxisListType

NCORES = 8
D = 1024
KC = 8
LN_EPS = 1e-6
ALPHA = 4 ** 0.25
PE, ACT, DVE, POOL, SP = "pe", "act", "dve", "pool", "sp"


class Trk:
    __slots__ = ("w", "r", "name")

    def __init__(self, name=""):
        self.w = None
        self.r = {}
        self.name = name


class T:
    def __init__(self, h, name):
        self.h = h
        self.name = name
        self.trk = Trk(name)
        self.subs = {}

    def sub(self, key):
        if key not in self.subs:
            self.subs[key] = Trk(f"{self.name}.{key}")
        return self.subs[key]

    def __getitem__(self, k):
        return self.h[k]


def _trk(x):
    return x.trk if isinstance(x, T) else x


class KB:
    def __init__(self):
        self.nc = bass.Bass("TRN2", target_bir_lowering=False)
        self.es = ExitStack()
        self.q = {e: [] for e in (PE, ACT, DVE, POOL, SP)}
        self.cnt = {e: 0 for e in self.q}
        self.waited = {e: {} for e in self.q}
        self.sems = {}
        self.dcnt = {}
        for e in self.q:
            self.sems[e] = self.es.enter_context(self.nc.semaphore("s_" + e))
        self.n_ops = 0

    def sb(self, name, shape, dt):
        return T(self.es.enter_context(self.nc.sbuf_tensor(name, list(shape), dt)), name)

    def ps(self, name, shape, dt):
        return T(self.es.enter_context(self.nc.psum_tensor(name, list(shape), dt)), name)

    def dram(self, name, shape, dt, kind="Internal"):
        h = self.nc.dram_tensor(name, list(shape), dt, kind=kind)
        return T(h.ap(), name)

    def init_arena(self, nbytes):
        self.arena = self.es.enter_context(self.nc.sbuf_tensor("arena", [128, nbytes // 2], BF16))
        self.a_off = 0
        self.a_size = nbytes

    def alloc(self, name, shape, dt):
        esz = 4 if dt == F32 else 2
        n = int(np.prod(shape[1:]))
        nb = (n * esz + 63) // 64 * 64
        off = self.a_off
        self.a_off += nb
        assert self.a_off <= self.a_size, (name, self.a_off, self.a_size)
        v = self.arena[0:shape[0], off // 2:off // 2 + n * esz // 2]
        if dt == F32:
            v = v.bitcast(F32)
        if len(shape) > 2:
            names = [f"d{i}" for i in range(len(shape) - 1)]
            v = v.rearrange("p (" + " ".join(names) + ") -> p " + " ".join(names),
                            **{nm: int(sz) for nm, sz in zip(names[1:], shape[2:])})
        return T(v, name)

    def mark(self):
        return self.a_off

    def release(self, mark):
        self.a_off = mark

    def barrier(self):
        snap = [(k, v) for k, v in self.cnt.items() if v > 0] + [(k, v) for k, v in self.dcnt.items() if v > 0]
        sems = self.sems
        for e in self.q:
            waits = []
            for k, v in snap:
                if k == e and e in (PE, SP):
                    continue
                if self.waited[e].get(k, 0) >= v:
                    continue
                self.waited[e][k] = v
                waits.append((k, v))

            def run(E, waits=waits):
                for k, v in waits:
                    E.wait_ge(sems[k], v)
            self.q[e].append(run)

    def dsem(self, key):
        if key not in self.sems:
            self.sems[key] = self.es.enter_context(self.nc.semaphore("d_" + key))
            self.dcnt[key] = 0
        return key

    def _waits(self, eng, r, w):
        need = {}

        def add(ev):
            if ev is None:
                return
            k, v = ev
            if need.get(k, 0) < v:
                need[k] = v
        for t in r:
            add(t.w)
        for t in w:
            add(t.w)
            for k, v in t.r.items():
                add((k, v))
        out = []
        for k, v in need.items():
            if k == eng and eng == PE:
                continue
            if self.waited[eng].get(k, 0) >= v:
                continue
            self.waited[eng][k] = v
            out.append((k, v))
        return out

    def _commit(self, ev, r, w):
        for t in w:
            t.w = ev
            t.r = {}
        for t in r:
            if t.r.get(ev[0], 0) < ev[1]:
                t.r[ev[0]] = ev[1]

    def op(self, eng, fn, r=(), w=()):
        r = [_trk(x) for x in r]
        w = [_trk(x) for x in w]
        waits = self._waits(eng, r, w)
        self.cnt[eng] += 1
        ev = (eng, self.cnt[eng])
        sems = self.sems

        def run(E, waits=waits, fn=fn, s=sems[eng]):
            for k, v in waits:
                E.wait_ge(sems[k], v)
            fn(E).then_inc(s, 1)
        self.q[eng].append(run)
        self._commit(ev, r, w)
        self.n_ops += 1

    def dma(self, eng, out, in_, r=(), w=(), sem=None, **kw):
        r = [_trk(x) for x in r]
        w = [_trk(x) for x in w]
        key = self.dsem(sem)
        waits = self._waits(eng, r, w)
        self.dcnt[key] += 16
        ev = (key, self.dcnt[key])
        sems = self.sems

        def run(E, waits=waits, s=sems[key]):
            for k, v in waits:
                E.wait_ge(sems[k], v)
            E.dma_start(out=out, in_=in_, **kw).then_inc(s, 16)
        self.q[eng].append(run)
        self._commit(ev, r, w)
        self.n_ops += 1

    def collective(self, kind, op, groups, in_ap, out_ap, r, w, sem):
        r = [_trk(x) for x in r]
        w = [_trk(x) for x in w]
        key = self.dsem(sem)
        waits = self._waits(POOL, r, w)
        self.dcnt[key] += 1
        ev = (key, self.dcnt[key])
        sems = self.sems

        def run(E, waits=waits, s=sems[key]):
            for k, v in waits:
                E.wait_ge(sems[k], v)
            E.collective_compute(kind, op, replica_groups=groups, ins=[in_ap], outs=[out_ap]).then_inc(s)
        self.q[POOL].append(run)
        self._commit(ev, r, w)

    def finish(self, final_trks):
        final = [_trk(x) for x in final_trks]
        waits = self._waits(SP, final, final)
        waits += [(k, v) for k, v in self.dcnt.items() if v > 0]
        sems = self.sems

        def run(E):
            for k, v in waits:
                E.wait_ge(sems[k], v)
        self.q[SP].append(run)
        block = self.es.enter_context(self.nc.Block())
        q = self.q

        @block.sync
        def _(E):
            for f in q[SP]:
                f(E)

        @block.scalar
        def _(E):
            for f in q[ACT]:
                f(E)

        @block.vector
        def _(E):
            for f in q[DVE]:
                f(E)

        @block.gpsimd
        def _(E):
            for f in q[POOL]:
                f(E)

        @block.tensor
        def _(E):
            for f in q[PE]:
                f(E)
        self.es.close()
        return self.nc


def build(stage=99):
    kb = KB()
    nc = kb.nc
    op, dma = kb.op, kb.dma
    kb.init_arena(207 * 1024)
    sb = kb.alloc

    def din(name, shape, dt=F32):
        return nc.dram_tensor(name, list(shape), dt, kind="ExternalInput").ap()

    def dout(name, shape, dt=F32):
        return T(nc.dram_tensor(name, list(shape), dt, kind="ExternalOutput").ap(), name)

    xp = din("xp", [1024, D])
    cvT = din("cvT", [128, KC, 2])
    ada_w = din("ada_w", [2, D, 3 * D])
    ada_bT = din("ada_bT", [2, 128, 24])
    ada_b = din("ada_b", [2, 3 * D])
    post_g = din("post_ln_g", [2, D])
    post_b = din("post_ln_b", [2, D])
    w_in = din("ev_w_in", [D, 3 * D])
    w_fnet = din("ev_w_fnet", [4, 128, 128])
    w_out0 = din("ev_w_out", [D, D])
    dftc_d = din("dftc", [128, 256])
    dftl_d = din("dftl256", [256, 512])
    rkvz_d = din("rw_w_rkvz", [4, D, D])
    w1_d = din("rw_w1", [2, D, 64])
    w2_d = din("rw_w2", [2, 64, D])
    a1_d = din("rw_a1", [2, D, 64])
    a2_d = din("rw_a2", [2, 64, D])
    g1_d = din("rw_g1", [D, 128])
    g2_d = din("rw_g2", [128, D])
    w0_d = din("rw_w0", [2, D])
    wout1_d = din("rw_w_out", [D, D])
    vecs_d = din("rw_vecs", [128, 14, KC])
    tri_d = din("tri", [128, 2, 128])
    mask4_d = din("mask4", [128, 2, 512])
    namask_d = din("namask", [128, 2, 256])
    bo_d = din("blockones", [128, 2, 128])
    xs_win = din("xs_win", [1536, D])
    ck_d = din("cache_k", [512, 512])
    cv_d = din("cache_v", [512, 512])
    bb_d = din("na_bias", [8, 128, 30 * 64])
    qmask_d = din("na_qmask", [24, 1024])
    rowoh_d = din("na_rowoh", [24, 1536])
    dfts_d = din("dfts", [4096, 2, 1024], BF16)
    st0_d = din("state0", [2, 16, 64, 64])
    selc_d = din("selc", [64, 2, 4])
    selh_d = din("selh", [8, 2])
    o_ys = dout("o_ys", [1024, D])
    o_yp = dout("o_yp", [1024, D])
    o_state = dout("o_state", [4, 2, 16, 64, 64])
    o_newk = dout("o_newk", [1024, 512])
    o_newv = dout("o_newv", [1024, 512])
    outs_final = [o_newk, o_newv]

    def dump(name, tile_ap, shape, src_trk):
        o = dout("dbg_" + name, shape)
        dma(POOL, o[:], tile_ap, r=(src_trk if isinstance(src_trk, (list, tuple)) else [src_trk]), sem="dbg_" + name)

    w_in_bf = kb.dram("w_in_bf", [D, 3 * D], BF16)
    wbf_d = [kb.dram(f"wbf_d{i}", [D, D], BF16) for i in range(5)]

    pbs = [kb.ps(f"pb{i}", [128, 512], F32) for i in range(6)]
    ptrs = [kb.ps(f"ptr{i}", [128, D], BF16) for i in range(2)]
    pctr = [0, 0]

    def npb():
        pctr[0] += 1
        return pbs[pctr[0] % 6]

    def nptr():
        pctr[1] += 1
        return ptrs[pctr[1] % 2]

    ident = sb("ident", [128, 128], BF16)
    consts = sb("consts", [128, 4], F32)
    ones_bf = sb("ones_bf", [128, 128], BF16)
    ones_row = sb("ones_row", [1, 128], F32)
    row_stage = sb("row_stage", [1, 512], F32)
    modT = [sb(f"modT{l}", [128, 2, 16], F32) for l in range(2)]
    gate1 = [None, sb("gate1_1", [128, 2, D], F32)]
    lng = sb("lng", [128, D], F32)
    lnb = sb("lnb", [128, D], F32)
    xt = [sb(f"xt{i}", [128, D], F32) for i in range(2)]
    xn = [sb("xn0", [128, D], BF16)] * 2
    stats = [sb(f"stats{i}", [128, 2, 6], F32) for i in range(2)]
    mv = [sb(f"mv{i}", [128, 4], F32) for i in range(2)]
    ybuf = [sb(f"ybuf{i}", [128, D], F32) for i in range(2)]
    mark_base = kb.mark()
    gate1[0] = sb("gate1_0", [128, 2, D], F32)
    wout0 = sb("wout0", [128, KC, D], BF16)
    dftc = sb("dftc_sb", [128, 256], BF16)
    wf_sb = sb("wf_sb", [128, 4, 128], BF16)
    csw = sb("csw", [128, 4, 256], BF16)
    mark_l0 = kb.mark()
    kv_out = [sb(f"kv_out{i}", [128, 512], F32) for i in range(2)]
    w_in_sb = sb("w_in_sb", [128, KC, 3 * D], BF16)
    mark_ph = kb.mark()

    op(POOL, lambda E: E.memset(ident[:], 0.0), w=[ident])
    op(POOL, lambda E: E.affine_select(out=ident[:], in_=ident[:], pattern=[[-1, 128]], compare_op=ALU.not_equal,
                                       fill=1.0, base=0, channel_multiplier=1), r=[ident], w=[ident])
    for i, val in enumerate((LN_EPS, 64e-5, 1e-12, 1.0)):
        op(POOL, lambda E, i=i, val=val: E.memset(consts[:, i:i + 1], val), w=[consts])
    op(POOL, lambda E: E.memset(ones_bf[:], 1.0), w=[ones_bf])
    op(POOL, lambda E: E.memset(ones_row[:], 1.0), w=[ones_row])

    def bcast_row(dst, dst_ap, src_row_ap, n):
        for c0 in range(0, n, 512):
            cw = min(512, n - c0)
            dma(SP, row_stage[0:1, 0:cw], src_row_ap[:, c0:c0 + cw], w=[row_stage], sem="row_stage")
            p = npb()
            op(PE, lambda E, cw=cw, p=p: E.matmul(p[:, 0:cw], lhsT=ones_row[0:1, :], rhs=row_stage[0:1, 0:cw],
                                                  start=True, stop=True), r=[ones_row, row_stage], w=[p])
            op(ACT, lambda E, c0=c0, cw=cw, p=p: E.activation(out=dst_ap[:, c0:c0 + cw], in_=p[:, 0:cw], func=AF.Identity),
               r=[p], w=[dst])

    cv_sb = sb("cv_sb", [128, KC, 2], F32)
    sc = sb("sc", [128, KC, 2], BF16)
    sc_rep = sb("sc_rep", [128, KC, 2, 128], BF16)
    adab_sb = sb("adab_sb", [128, 2, 24], F32)
    gate_b = sb("gate_b", [128, D], F32)
    adaw = [sb(f"adaw{i}", [128, KC, 512], BF16) for i in range(2)]
    dma(SP, cv_sb[:], cvT[:, :, :], w=[cv_sb], sem="c_cv")
    op(ACT, lambda E: E.activation(out=sc[:], in_=cv_sb[:], func=AF.Silu), r=[cv_sb], w=[sc])
    for kc in range(KC):
        for v in range(2):
            op(DVE, lambda E, kc=kc, v=v: E.tensor_copy(out=sc_rep[:, kc, v, :],
                                                         in_=sc[:, kc, v:v + 1].to_broadcast([128, 128])),
               r=[sc], w=[sc_rep])
    dma(SP, adab_sb[:], ada_bT.rearrange("l p j -> p l j"), w=[adab_sb], sem="c_adab")
    bcast_row(lng, lng[:], post_g[0:1, :], D)
    bcast_row(lnb, lnb[:], post_b[0:1, :], D)
    for l in range(2):
        bcast_row(gate_b, gate_b[:], ada_b[l:l + 1, 2 * D:3 * D], D)
        ps_modT = npb()
        for nch in range(6):
            wb = adaw[(l * 6 + nch) % 2]
            dma(POOL, wb[:], ada_w[l, :, nch * 512:(nch + 1) * 512].rearrange("(kc p) n -> p kc n", p=128),
                w=[wb], sem=wb.name)
            if nch < 4:
                for jj in range(4):
                    j = nch * 4 + jj
                    for kc in range(KC):
                        op(PE, lambda E, kc=kc, jj=jj, j=j, wb=wb, ps_modT=ps_modT: E.matmul(
                            ps_modT[:, 2 * j:2 * j + 2], lhsT=wb[:, kc, jj * 128:(jj + 1) * 128], rhs=sc[:, kc, :],
                            start=(kc == 0), stop=(kc == KC - 1)), r=[wb, sc], w=[ps_modT])
                if nch == 3:
                    for v in range(2):
                        op(DVE, lambda E, l=l, v=v, ps_modT=ps_modT: E.tensor_tensor(
                            out=modT[l][:, v, :], in0=ps_modT[:, 0:32].rearrange("p (j v) -> p v j", v=2)[:, v, :],
                            in1=adab_sb[:, l, 0:16], op=ALU.add), r=[ps_modT, adab_sb], w=[modT[l]])
                    op(DVE, lambda E, l=l: E.tensor_scalar_add(out=modT[l][:, :, 8:16], in0=modT[l][:, :, 8:16],
                                                               scalar1=1.0), r=[modT[l]], w=[modT[l]])
            else:
                for v in range(2):
                    ps_mod = npb()
                    for kc in range(KC):
                        op(PE, lambda E, kc=kc, v=v, wb=wb, ps_mod=ps_mod: E.matmul(
                            ps_mod[:, :], lhsT=sc_rep[:, kc, v, :], rhs=wb[:, kc, :],
                            start=(kc == 0), stop=(kc == KC - 1)), r=[wb, sc_rep], w=[ps_mod])
                    c0 = (nch - 4) * 512
                    op(DVE, lambda E, l=l, v=v, c0=c0, ps_mod=ps_mod: E.scalar_tensor_tensor(
                        out=gate1[l][:, v, c0:c0 + 512], in0=ps_mod[:, :], scalar=1.0,
                        in1=gate_b[:, c0:c0 + 512], op0=ALU.add, op1=ALU.add),
                       r=[ps_mod, gate_b], w=[gate1[l]])
        if l == 0:
            for nch in range(6):
                dma(POOL, w_in_sb[:, :, nch * 512:(nch + 1) * 512],
                    w_in[:, nch * 512:(nch + 1) * 512].rearrange("(kc p) n -> p kc n", p=128),
                    w=[w_in_sb.sub(nch)], sem=f"w_in{nch}")
            dma(POOL, dftc[:], dftc_d[:, :], w=[dftc], sem="c_dftc")
            dma(POOL, wf_sb[:], w_fnet.rearrange("g c e -> c g e"), w=[wf_sb], sem="c_wf")
            dma(POOL, wout0[:], w_out0.rearrange("(kc p) n -> p kc n", p=128), w=[wout0], sem="wout0")
            dma(POOL, w_in_bf[:, :], w_in[:, :], w=[w_in_bf], sem="wcast0")
            for i5 in range(5):
                dma(POOL, wbf_d[i5][:, :], (wout1_d if i5 == 4 else rkvz_d[i5]), w=[wbf_d[i5]], sem=f"wcast{1 + i5}")
    for half in range(2):
        p = npb()
        for gg in range(2):
            g = half * 2 + gg
            for cs in range(2):
                op(PE, lambda E, p=p, g=g, gg=gg, cs=cs: E.matmul(
                    p[:, gg * 256 + cs * 128:gg * 256 + (cs + 1) * 128], lhsT=dftc[:, cs * 128:(cs + 1) * 128],
                    rhs=wf_sb[:, g, :], start=True, stop=True), r=[dftc, wf_sb], w=[p])
        op(ACT, lambda E, p=p, half=half: E.activation(
            out=csw[:, half * 2:half * 2 + 2, :].rearrange("p g e -> p (g e)"), in_=p[:, :], func=AF.Identity),
           r=[p], w=[csw])

    if stage == 0:
        dump("modT0", modT[0][:], [128, 2, 16], modT[0])
        dump("gate1_0", gate1[0][:], [128, 2, D], gate1[0])
        dump("gate1_1", gate1[1][:], [128, 2, D], gate1[1])
        return kb.finish(outs_final)
    kb.barrier()
    kb.release(mark_ph)

    lctr = [0]

    def ln_stats(x, st, m, eps_col=0):
        for h in range(2):
            op(DVE, lambda E, h=h: E.bn_stats(out=st[:, h, :], in_=x[:, h * 512:(h + 1) * 512]), r=[x], w=[st])
        op(DVE, lambda E: E.bn_aggr(out=m[:, 0:2], in_=st[:]), r=[st], w=[m])
        op(ACT, lambda E: E.activation(out=m[:, 2:3], in_=m[:, 1:2], func=AF.Sqrt, bias=consts[:, eps_col:eps_col + 1],
                                       scale=1.0), r=[m, consts], w=[m])
        op(DVE, lambda E: E.reciprocal(out=m[:, 2:3], in_=m[:, 2:3]), r=[m], w=[m])
        op(DVE, lambda E: E.scalar_tensor_tensor(out=m[:, 3:4], in0=m[:, 0:1], scalar=-1.0, in1=m[:, 2:3],
                                                 op0=ALU.mult, op1=ALU.mult), r=[m], w=[m])

    def ln_transpose(x_ap, x_deps, mod, v, uT, col0, utrk):
        i = lctr[0] % 2
        lctr[0] += 1
        x, xnb, st, m = xt[i], xn[i], stats[i], mv[i]
        pst = nptr()
        dma(SP, x[:], x_ap, r=x_deps, w=[x], sem=x.name)
        ln_stats(x, st, m)
        op(ACT, lambda E: E.activation(out=xnb[:], in_=x[:], func=AF.Identity, scale=m[:, 2:3], bias=m[:, 3:4]),
           r=[x, m], w=[xnb])
        for kc in range(KC):
            op(PE, lambda E, kc=kc: E.transpose(pst[:, kc * 128:(kc + 1) * 128], xnb[:, kc * 128:(kc + 1) * 128],
                                                ident[:]), r=[xnb, ident], w=[pst])
        for kc in range(KC):
            op(DVE, lambda E, kc=kc: E.tensor_scalar(
                out=uT[:, kc, col0:col0 + 128], in0=pst[:, kc * 128:(kc + 1) * 128],
                scalar1=mod[:, v, 8 + kc:9 + kc], scalar2=mod[:, v, kc:kc + 1], op0=ALU.mult, op1=ALU.add),
               r=[pst, mod], w=[utrk])

    uT_p = sb("uT_p", [128, KC, 1024], BF16)
    ocat = uT_p
    vtok_p = sb("vtok_p", [128, 8, 512], BF16)
    aT = sb("aT", [128, 4, 1024], BF16)
    sza = sb("sza", [128, 4, 1024], BF16)
    qT = sb("qT", [128, 4, 1024], BF16)
    kT = sb("kT", [128, 4, 1024], BF16)
    szb = sb("szb", [128, 4, 1024], BF16)
    A1 = sb("A1", [128, 8, 4, 256], BF16)
    dftl = sb("dftl_sb", [128, 2, 512], BF16)
    pTs = [sb(f"pT{i}", [128, 512], BF16) for i in range(2)]
    recs = [sb(f"rec{i}", [128, 256], F32) for i in range(2)]
    dma(POOL, dftl[:], dftl_d.rearrange("(lt p) m -> p lt m", p=128), w=[dftl], sem="const3")
    for t in range(8):
        ln_transpose(xp[t * 128:(t + 1) * 128, :], [], modT[0], 0, uT_p, t * 128, uT_p.sub(t // 4))

    n_kv = 0
    for t in range(8):
        for which, c0, od in ((0, 1536, o_newk), (1, 2048, o_newv)):
            pa = npb()
            ko = kv_out[n_kv % 2]
            n_kv += 1
            for kc in range(KC):
                op(PE, lambda E, kc=kc, t=t, c0=c0, pa=pa: E.matmul(
                    pa[:, :], lhsT=uT_p[:, kc, t * 128:(t + 1) * 128], rhs=w_in_sb[:, kc, c0:c0 + 512],
                    start=(kc == 0), stop=(kc == KC - 1)),
                   r=[uT_p.sub(t // 4), w_in_sb.sub(c0 // 512)], w=[pa])
            op(ACT, lambda E, pa=pa, ko=ko: E.activation(out=ko[:], in_=pa[:, :], func=AF.Identity), r=[pa], w=[ko])
            if which == 1:
                op(POOL, lambda E, ko=ko, t=t: E.tensor_copy(out=vtok_p[:, t, :], in_=ko[:]),
                   r=[ko], w=[vtok_p.sub(t)])
            dma(SP, od[t * 128:(t + 1) * 128, :], ko[:], r=[ko], sem=ko.name)

    for blk in range(2):
        c0 = blk * 512
        for j in list(range(0, 16)) + list(range(20, 24)):
            p = npb()
            for kc in range(KC):
                op(PE, lambda E, kc=kc, j=j, c0=c0, p=p: E.matmul(
                    p[:, :], lhsT=w_in_sb[:, kc, j * 128:(j + 1) * 128], rhs=uT_p[:, kc, c0:c0 + 512],
                    start=(kc == 0), stop=(kc == KC - 1)), r=[w_in_sb.sub(j // 4), uT_p.sub(blk)], w=[p])
            grp, idx = j // 4, j % 4
            if grp == 0:
                op(DVE, lambda E, p=p, idx=idx, c0=c0: E.tensor_copy(out=aT[:, idx, c0:c0 + 512], in_=p[:, :]),
                   r=[p], w=[aT.sub(blk)])
            elif grp == 1:
                op(ACT, lambda E, p=p, idx=idx, c0=c0: E.activation(out=sza[:, idx, c0:c0 + 512], in_=p[:, :], func=AF.Silu),
                   r=[p], w=[sza.sub(blk)])
            elif grp == 2:
                op(DVE, lambda E, p=p, idx=idx, c0=c0: E.tensor_scalar_mul(out=qT[:, idx, c0:c0 + 512], in0=p[:, :],
                                                                            scalar1=0.125), r=[p], w=[qT.sub(blk)])
            elif grp == 3:
                op(DVE, lambda E, p=p, idx=idx, c0=c0: E.tensor_copy(out=kT[:, idx, c0:c0 + 512], in_=p[:, :]),
                   r=[p], w=[kT.sub(blk)])
            else:
                op(ACT, lambda E, p=p, idx=idx, c0=c0: E.activation(out=szb[:, idx, c0:c0 + 512], in_=p[:, :], func=AF.Silu),
                   r=[p], w=[szb.sub(blk)])

    for t in range(8):
        for half in range(2):
            p = npb()
            for gg in range(2):
                g = half * 2 + gg
                op(PE, lambda E, p=p, g=g, gg=gg, t=t: E.matmul(
                    p[:, gg * 256:(gg + 1) * 256], lhsT=aT[:, g, t * 128:(t + 1) * 128], rhs=csw[:, g, :],
                    start=True, stop=True), r=[aT.sub(t // 4), csw], w=[p])
            eng = ACT if half == 0 else DVE
            if eng == ACT:
                op(ACT, lambda E, p=p, t=t, half=half: E.activation(
                    out=A1[:, t, half * 2:half * 2 + 2, :].rearrange("p g e -> p (g e)"), in_=p[:, :], func=AF.Identity),
                   r=[p], w=[A1.sub(t)])
            else:
                op(DVE, lambda E, p=p, t=t, half=half: E.tensor_copy(
                    out=A1[:, t, half * 2:half * 2 + 2, :].rearrange("p g e -> p (g e)"), in_=p[:, :]),
                   r=[p], w=[A1.sub(t)])
    for s in range(4):
        for g in range(4):
            p = npb()
            n = 0
            for lt in range(2):
                t = 2 * s + lt
                for cs in range(2):
                    op(PE, lambda E, p=p, t=t, g=g, lt=lt, cs=cs, n=n: E.matmul(
                        p[:, 0:256], lhsT=A1[:, t, g, cs * 128:(cs + 1) * 128], rhs=dftl[:, lt, cs * 256:(cs + 1) * 256],
                        start=(n == 0), stop=(n == 3)), r=[A1.sub(t), dftl], w=[p])
                    n += 1
            op(DVE, lambda E, p=p, g=g, s=s: E.tensor_tensor(
                out=ocat[:, g, s * 256:(s + 1) * 256], in0=p[:, 0:256], in1=sza[:, g, s * 256:(s + 1) * 256], op=ALU.mult),
               r=[p, sza.sub(s // 2)], w=[ocat.sub(s // 2)])

    na = 0
    for s in range(4):
        q0 = s * 256
        for hp in range(4):
            po = npb()
            for e2 in range(2):
                h = 2 * hp + e2
                lo = 64 * e2
                pss = npb()
                pT = pTs[na % 2]
                na += 1
                for kc2 in range(2):
                    op(PE, lambda E, pss=pss, kc2=kc2, lo=lo, hp=hp, q0=q0: E.matmul(
                        pss[:, kc2 * 256:(kc2 + 1) * 256], lhsT=kT[lo:lo + 64, hp, q0 + kc2 * 128:q0 + (kc2 + 1) * 128],
                        rhs=qT[lo:lo + 64, hp, q0:q0 + 256], start=True, stop=True),
                       r=[kT.sub(s // 2), qT.sub(s // 2)], w=[pss])
                op(ACT, lambda E, pss=pss, pT=pT: E.activation(out=pT[:], in_=pss[:, :], func=AF.Exp), r=[pss], w=[pT])
                for kc2 in range(2):
                    op(PE, lambda E, po=po, kc2=kc2, lo=lo, h=h, s=s, pT=pT: E.matmul(
                        po[lo:lo + 64, 0:256], lhsT=vtok_p[:, 2 * s + kc2, h * 64:(h + 1) * 64],
                        rhs=pT[:, kc2 * 256:(kc2 + 1) * 256], start=(kc2 == 0), stop=(kc2 == 1)),
                       r=[vtok_p.sub(2 * s + kc2), pT], w=[po])
                for kc2 in range(2):
                    op(PE, lambda E, po=po, kc2=kc2, lo=lo, pT=pT: E.matmul(
                        po[lo:lo + 64, 256:512], lhsT=ones_bf[:, 0:64],
                        rhs=pT[:, kc2 * 256:(kc2 + 1) * 256], start=(kc2 == 0), stop=(kc2 == 1)),
                       r=[ones_bf, pT], w=[po])
            rec = recs[(s * 4 + hp) % 2]
            op(DVE, lambda E, po=po, rec=rec: E.reciprocal(out=rec[:], in_=po[:, 256:512]), r=[po], w=[rec])
            op(POOL, lambda E, rec=rec, hp=hp, q0=q0: E.tensor_mul(out=rec[:], in0=rec[:], in1=szb[:, hp, q0:q0 + 256]),
               r=[rec, szb.sub(s // 2)], w=[rec])
            op(DVE, lambda E, po=po, rec=rec, hp=hp, q0=q0: E.tensor_tensor(
                out=ocat[:, 4 + hp, q0:q0 + 256], in0=po[:, 0:256], in1=rec[:], op=ALU.mult),
               r=[po, rec], w=[ocat.sub(s // 2)])

    if stage == 2:
        x1d = dout("dbg_x1", [2048, D])
    else:
        x1d = kb.dram("x1d", [2048, D], F32)

    def w0fn(nh):
        return wout0, wout0[:, :, nh * 512:(nh + 1) * 512]

    def out_proj_ln(t_glob, x_ap, oc, col0, octrk, wfn, layer, v):
        i = lctr[0] % 2
        lctr[0] += 1
        x, st, m, y = xt[i], stats[i], mv[i], ybuf[i]
        dma(SP, x[:], x_ap, w=[x], sem=x.name)
        for nh in range(2):
            p = npb()
            wt_, wap_ = wfn(nh)
            for kc in range(KC):
                op(PE, lambda E, kc=kc, nh=nh, p=p, wap_=wap_: E.matmul(
                    p[:, :], lhsT=oc[:, kc, col0:col0 + 128], rhs=wap_[:, kc, :],
                    start=(kc == 0), stop=(kc == KC - 1)), r=[octrk, wt_], w=[p])
            op(DVE, lambda E, nh=nh, p=p: E.tensor_tensor(
                out=y[:, nh * 512:(nh + 1) * 512], in0=p[:, :], in1=gate1[layer][:, v, nh * 512:(nh + 1) * 512],
                op=ALU.mult), r=[p, gate1[layer]], w=[y])
        op(DVE, lambda E: E.scalar_tensor_tensor(out=y[:], in0=x[:], scalar=ALPHA, in1=y[:], op0=ALU.mult, op1=ALU.add),
           r=[x, y], w=[y])
        ln_stats(y, st, m)
        op(ACT, lambda E: E.activation(out=y[:], in_=y[:], func=AF.Identity, scale=m[:, 2:3], bias=m[:, 3:4]),
           r=[y, m], w=[y])
        op(POOL, lambda E: E.tensor_mul(out=y[:], in0=y[:], in1=lng[:]), r=[y, lng], w=[y])
        op(POOL, lambda E: E.tensor_add(out=y[:], in0=y[:], in1=lnb[:]), r=[y, lnb], w=[y])
        return y

    for t in range(8):
        y = out_proj_ln(t, xp[t * 128:(t + 1) * 128, :], ocat, t * 128, ocat.sub(t // 4), w0fn, 0, 0)
        dma(SP, x1d[t * 128:(t + 1) * 128, :], y[:], r=[y], w=[x1d.sub(t)], sem=y.name)
    kb.barrier()
    kb.release(mark_l0)
    GROUPS = [[0, 1, 2, 3], [4, 5, 6, 7]]
    gin = [kb.dram(f"gin{i}", [512, 1024], BF16) for i in range(2)]
    gout = [kb.dram(f"gout{i}", [2048, 1024], BF16) for i in range(2)]
    ocs = sb("ocs", [128, KC, 1024], BF16)
    sza_s = sb("sza_s", [128, 4, 1024], BF16)
    szb_s = sb("szb_s", [128, 4, 1024], BF16)
    qTa = sb("qTa", [128, 8, 1024], BF16)
    kTa = sb("kTa", [128, 8, 1536], BF16)
    vwin = sb("vwin", [128, 12, 512], BF16)
    mark_s1 = kb.mark()
    wch = [sb(f"wch{i}", [128, KC, 512], BF16) for i in range(2)]
    uT_s = sb("uT_s", [128, KC, 1536], BF16)
    aTb = sb("aTb", [128, 4, 512], BF16)
    a1st = [sb(f"a1st{i}", [128, 1024], BF16) for i in range(2)]
    for h in range(8):
        dma(POOL, qTa[64:88, h, :], qmask_d[:, :], w=[qTa.sub("aug")], sem="aug_q")
        dma(POOL, kTa[64:88, h, :], rowoh_d[:, :], w=[kTa.sub("aug")], sem="aug_k")
    for t in range(12):
        ln_transpose(xs_win[t * 128:(t + 1) * 128, :], [], modT[0], 1, uT_s, t * 128, uT_s.sub(t // 4))
    SCUT = os.environ.get('SCUT', '')
    if SCUT == 'ln':
        return kb.finish(outs_final)
    wcc = [0]

    def load_wch(nch):
        wb = wch[wcc[0] % 2]
        wcc[0] += 1
        dma(SP, wb[:], w_in_bf[:, nch * 512:(nch + 1) * 512].rearrange("(kc p) n -> p kc n", p=128), r=[w_in_bf], w=[wb],
            sem=wb.name)
        return wb

    def fm4(wb, wblk, evac):
        for jj in range(4):
            p = npb()
            for kc in range(KC):
                op(PE, lambda E, kc=kc, jj=jj, p=p: E.matmul(
                    p[:, :], lhsT=wb[:, kc, jj * 128:(jj + 1) * 128], rhs=uT_s[:, kc, wblk * 512:(wblk + 1) * 512],
                    start=(kc == 0), stop=(kc == KC - 1)), r=[wb, uT_s.sub(wblk)], w=[p])
            evac(jj, p)

    wb = load_wch(0)
    na1 = 0
    for blk in range(2):
        for jj in range(4):
            p = npb()
            for kc in range(KC):
                op(PE, lambda E, kc=kc, jj=jj, p=p, blk=blk, wb=wb: E.matmul(
                    p[:, :], lhsT=wb[:, kc, jj * 128:(jj + 1) * 128], rhs=uT_s[:, kc, 256 + blk * 512:256 + (blk + 1) * 512],
                    start=(kc == 0), stop=(kc == KC - 1)), r=[wb, uT_s.sub(0), uT_s.sub(1), uT_s.sub(2)], w=[p])
            op(DVE, lambda E, jj=jj, p=p: E.tensor_copy(out=aTb[:, jj, :], in_=p[:, :]), r=[p], w=[aTb])
        for tt_ in range(4):
            stg = a1st[na1 % 2]
            na1 += 1
            for half in range(2):
                p = npb()
                for gg in range(2):
                    g = half * 2 + gg
                    op(PE, lambda E, p=p, g=g, gg=gg, tt_=tt_: E.matmul(
                        p[:, gg * 256:(gg + 1) * 256], lhsT=aTb[:, g, tt_ * 128:(tt_ + 1) * 128], rhs=csw[:, g, :],
                        start=True, stop=True), r=[aTb, csw], w=[p])
                op(ACT, lambda E, p=p, half=half, stg=stg: E.activation(out=stg[:, half * 512:(half + 1) * 512], in_=p[:, :],
                                                                        func=AF.Identity), r=[p], w=[stg])
            row0 = tt_ * 128
            dma(SP, gin[blk][row0:row0 + 128, :], stg[:], r=[stg], w=[gin[blk].sub(row0)], sem=stg.name)
        kb.collective("AllGather", ALU.bypass, GROUPS, gin[blk][:, :], gout[blk][:, :],
                      r=[gin[blk].sub(r0) for r0 in range(0, 512, 128)] + ([gout[0]] if blk else []),
                      w=[gout[blk]], sem=f"ag1_{blk}")

    if SCUT == 'a1':
        dump('aTb', aTb[:], [128, 4, 512], [aTb])
        dump('csw', csw[:], [128, 4, 256], [csw])
        dump('gin0', gin[0][:, :], [512, 1024], [gin[0].sub(r0) for r0 in range(0, 512, 128)])
        return kb.finish(outs_final)

    def own_blocks(fn):
        for blk in range(2):
            fn(blk)

    wb = load_wch(1)
    for blk in range(2):
        for jj in range(4):
            p = npb()
            for kc in range(KC):
                op(PE, lambda E, kc=kc, jj=jj, p=p, blk=blk, wb=wb: E.matmul(
                    p[:, :], lhsT=wb[:, kc, jj * 128:(jj + 1) * 128], rhs=uT_s[:, kc, 256 + blk * 512:256 + (blk + 1) * 512],
                    start=(kc == 0), stop=(kc == KC - 1)), r=[wb, uT_s.sub(0), uT_s.sub(1), uT_s.sub(2)], w=[p])
            op(ACT, lambda E, jj=jj, p=p, blk=blk: E.activation(out=sza_s[:, jj, blk * 512:(blk + 1) * 512], in_=p[:, :],
                                                                func=AF.Silu), r=[p], w=[sza_s])
    wb = load_wch(2)
    for blk in range(2):
        for h in range(8):
            p = npb()
            for kc in range(KC):
                op(PE, lambda E, kc=kc, h=h, p=p, blk=blk, wb=wb: E.matmul(
                    p[0:64, :], lhsT=wb[:, kc, h * 64:(h + 1) * 64], rhs=uT_s[:, kc, 256 + blk * 512:256 + (blk + 1) * 512],
                    start=(kc == 0), stop=(kc == KC - 1)), r=[wb, uT_s.sub(0), uT_s.sub(1), uT_s.sub(2)], w=[p])
            op(DVE, lambda E, h=h, p=p, blk=blk: E.tensor_scalar_mul(out=qTa[0:64, h, blk * 512:(blk + 1) * 512],
                                                                      in0=p[0:64, :], scalar1=0.125), r=[p], w=[qTa.sub(h)])
    wb = load_wch(3)
    for wblk in range(3):
        for h in range(8):
            p = npb()
            for kc in range(KC):
                op(PE, lambda E, kc=kc, h=h, p=p, wblk=wblk, wb=wb: E.matmul(
                    p[0:64, :], lhsT=wb[:, kc, h * 64:(h + 1) * 64], rhs=uT_s[:, kc, wblk * 512:(wblk + 1) * 512],
                    start=(kc == 0), stop=(kc == KC - 1)), r=[wb, uT_s.sub(wblk)], w=[p])
            op(ACT, lambda E, h=h, p=p, wblk=wblk: E.activation(out=kTa[0:64, h, wblk * 512:(wblk + 1) * 512], in_=p[0:64, :],
                                                                func=AF.Identity), r=[p], w=[kTa.sub(h)])
    wb = load_wch(4)
    for t in range(12):
        p = npb()
        for kc in range(KC):
            op(PE, lambda E, kc=kc, t=t, p=p, wb=wb: E.matmul(
                p[:, :], lhsT=uT_s[:, kc, t * 128:(t + 1) * 128], rhs=wb[:, kc, :],
                start=(kc == 0), stop=(kc == KC - 1)), r=[wb, uT_s.sub(t // 4)], w=[p])
        op(ACT if t % 2 else DVE, (lambda E, t=t, p=p: E.activation(out=vwin[:, t, :], in_=p[:, :], func=AF.Identity)) if t % 2
           else (lambda E, t=t, p=p: E.tensor_copy(out=vwin[:, t, :], in_=p[:, :])), r=[p], w=[vwin])
    wb = load_wch(5)
    for blk in range(2):
        for jj in range(4):
            p = npb()
            for kc in range(KC):
                op(PE, lambda E, kc=kc, jj=jj, p=p, blk=blk, wb=wb: E.matmul(
                    p[:, :], lhsT=wb[:, kc, jj * 128:(jj + 1) * 128], rhs=uT_s[:, kc, 256 + blk * 512:256 + (blk + 1) * 512],
                    start=(kc == 0), stop=(kc == KC - 1)), r=[wb, uT_s.sub(0), uT_s.sub(1), uT_s.sub(2)], w=[p])
            op(ACT, lambda E, jj=jj, p=p, blk=blk: E.activation(out=szb_s[:, jj, blk * 512:(blk + 1) * 512], in_=p[:, :],
                                                                func=AF.Silu), r=[p], w=[szb_s])
    if SCUT == 'proj':
        dump('gin0', gin[0][:, :], [512, 1024], [gin[0].sub(r0) for r0 in range(0, 512, 128)])
        dump('csw', csw[:], [128, 4, 256], [csw])
        dump('aTb', aTb[:], [128, 4, 512], [aTb])
        return kb.finish(outs_final)
    kb.barrier()
    kb.release(mark_s1)

    BBt = [sb(f"BBt{i}", [128, 30 * 64], BF16) for i in range(2)]
    kc_tok = sb("kc_tok", [128, 4, 512], BF16)
    vctx = sb("vctx", [128, 4, 512], BF16)
    kcT = sb("kcT", [64, 8, 512], BF16)
    pTn = [sb(f"pTn{i}", [128, 512], BF16) for i in range(3)]
    recn = sb("recn", [128, 512], F32)
    a1g = [sb(f"a1g{i}", [128, 4, 256], BF16) for i in range(2)]
    dfs = [sb(f"dfs{i}", [128, 2, 512], BF16) for i in range(2)]
    dma(POOL, kc_tok[:], ck_d.rearrange("(t p) n -> p t n", p=128), w=[kc_tok], sem="ctxk")
    dma(POOL, vctx[:], cv_d.rearrange("(t p) n -> p t n", p=128), w=[vctx], sem="ctxv")
    for h in range(8):
        pt = nptr()
        for t4 in range(4):
            op(PE, lambda E, pt=pt, t4=t4, h=h: E.transpose(pt[0:64, t4 * 128:(t4 + 1) * 128], kc_tok[:, t4, h * 64:(h + 1) * 64],
                                                            ident[:]), r=[kc_tok, ident], w=[pt])
        op(DVE, lambda E, pt=pt, h=h: E.tensor_copy(out=kcT[:, h, :], in_=pt[0:64, 0:512]), r=[pt], w=[kcT])
    npT = 0
    for hp in range(4):
        bbs = []
        for h2 in range(2):
            bt = BBt[h2]
            dma(POOL, bt[:], bb_d[2 * hp + h2], w=[bt], sem=bt.name)
            bbs.append(bt)
        for j in range(2):
            po, pd = pbs[0], pbs[1]
            for h2 in range(2):
                h = 2 * hp + h2
                lo = 64 * h2
                units = [("loc", i) for i in ((range(0, 10)) if j == 0 else range(2, 12))] + [("ctx", t4) for t4 in range(4)]
                for ui, (kind, i) in enumerate(units):
                    pss = pbs[2 + (npT % 4)]
                    pT = pTn[npT % 3]
                    npT += 1
                    if kind == "loc":
                        idx0 = 18 - 2 * i + 8 * j
                        op(PE, lambda E, pss=pss, h=h, i=i, j=j: E.matmul(
                            pss[:, :], lhsT=kTa[0:88, h, i * 128:(i + 1) * 128], rhs=qTa[0:88, h, j * 512:(j + 1) * 512],
                            start=True, stop=False), r=[kTa.sub(h), kTa.sub("aug"), qTa.sub(h), qTa.sub("aug")], w=[pss])
                        op(PE, lambda E, pss=pss, h2=h2, idx0=idx0: E.matmul(
                            pss[:, :], lhsT=ident[:], rhs=bbs[h2][:, idx0 * 64:idx0 * 64 + 512], start=False, stop=True),
                           r=[ident, bbs[h2]], w=[pss])
                        vl = vwin[:, i, h * 64:(h + 1) * 64]
                        vtr = vwin
                    else:
                        op(PE, lambda E, pss=pss, h=h, i=i, j=j: E.matmul(
                            pss[:, :], lhsT=kcT[0:64, h, i * 128:(i + 1) * 128], rhs=qTa[0:64, h, j * 512:(j + 1) * 512],
                            start=True, stop=True), r=[kcT, qTa.sub(h)], w=[pss])
                        vl = vctx[:, i, h * 64:(h + 1) * 64]
                        vtr = vctx
                    op(ACT, lambda E, pss=pss, pT=pT: E.activation(out=pT[:], in_=pss[:, :], func=AF.Exp), r=[pss], w=[pT])
                    op(PE, lambda E, po=po, lo=lo, vl=vl, pT=pT, ui=ui: E.matmul(
                        po[lo:lo + 64, :], lhsT=vl, rhs=pT[:], start=(ui == 0), stop=(ui == len(units) - 1)),
                       r=[vtr, pT], w=[po])
                    op(PE, lambda E, pd=pd, lo=lo, pT=pT, ui=ui: E.matmul(
                        pd[lo:lo + 64, :], lhsT=ones_bf[:, 0:64], rhs=pT[:], start=(ui == 0), stop=(ui == len(units) - 1)),
                       r=[ones_bf, pT], w=[pd])
            op(DVE, lambda E, pd=pd: E.reciprocal(out=recn[:], in_=pd[:, :]), r=[pd], w=[recn])
            op(POOL, lambda E, hp=hp, j=j: E.tensor_mul(out=recn[:], in0=recn[:], in1=szb_s[:, hp, j * 512:(j + 1) * 512]),
               r=[recn, szb_s], w=[recn])
            op(DVE, lambda E, po=po, hp=hp, j=j: E.tensor_tensor(out=ocs[:, 4 + hp, j * 512:(j + 1) * 512], in0=po[:, :], in1=recn[:],
                                                                 op=ALU.mult), r=[po, recn], w=[ocs.sub(j)])

    if SCUT == 'na':
        dump('gin0', gin[0][:, :], [512, 1024], [gin[0].sub(r0) for r0 in range(0, 512, 128)])
        dump('csw', csw[:], [128, 4, 256], [csw])
        return kb.finish(outs_final)
    nst = 0
    for mb in range(2):
        for lt in range(32):
            ag, df = a1g[nst % 2], dfs[nst % 2]
            nst += 1
            gsrc = gout[(lt % 8) // 4]
            grow = (lt // 8) * 512 + (lt % 4) * 128
            dma(SP, ag[:], gsrc[grow:grow + 128, :].rearrange("p (g e) -> p g e", g=4), r=[gsrc], w=[ag], sem=ag.name)
            dma(SP, df[:], dfts_d[lt * 128:(lt + 1) * 128, :, mb * 512:(mb + 1) * 512], w=[df], sem=df.name)
            for g in range(4):
                for cs in range(2):
                    op(PE, lambda E, g=g, cs=cs, ag=ag, df=df, lt=lt: E.matmul(
                        pbs[g][:, :], lhsT=ag[:, g, cs * 128:(cs + 1) * 128], rhs=df[:, cs, :],
                        start=(lt == 0 and cs == 0), stop=(lt == 31 and cs == 1)), r=[ag, df], w=[pbs[g]])
        for g in range(4):
            op(DVE, lambda E, g=g, mb=mb: E.tensor_tensor(out=ocs[:, g, mb * 512:(mb + 1) * 512], in0=pbs[g][:, :],
                                                          in1=sza_s[:, g, mb * 512:(mb + 1) * 512], op=ALU.mult),
               r=[pbs[g], sza_s], w=[ocs.sub(mb)])

    if SCUT == 'fnet':
        return kb.finish(outs_final)
    if stage == 2:
        dump('ocs', ocs[:], [128, KC, 1024], [ocs.sub(0), ocs.sub(1)])
        dump('gout0', gout[0][:, :], [2048, 1024], [gout[0]])
        dump('gin0', gin[0][:, :], [512, 1024], [gin[0].sub(r0) for r0 in range(0, 512, 128)])
        dump('sza', sza_s[:], [128, 4, 1024], [sza_s])
    for t in range(8):
        y = out_proj_ln(t, xs_win[256 + t * 128:256 + (t + 1) * 128, :], ocs, t * 128, ocs.sub(t // 4), w0fn, 0, 1)
        dma(SP, x1d[1024 + t * 128:1024 + (t + 1) * 128, :], y[:], r=[y], w=[x1d.sub(8 + t)], sem=y.name)

    if stage == 2:
        return kb.finish(outs_final)

    kb.barrier()
    kb.release(mark_base)
    NLEV = 2
    CUT = os.environ.get('L1CUT', '')
    NBLK = int(os.environ.get('L1NBLK', '4'))
    wbuf = [sb(f"wbuf{i}", [128, KC, 512], BF16) for i in range(2)]
    w1cat = sb("w1cat", [128, KC, 128], BF16)
    a1cat = sb("a1cat", [128, KC, 128], BF16)
    g1s = sb("g1s", [128, KC, 128], BF16)
    w2cat = sb("w2cat", [128, D], BF16)
    a2cat = sb("a2cat", [128, D], BF16)
    g2s = sb("g2s", [128, D], BF16)
    w0b = sb("w0b", [128, 2, D], F32)
    vecs = sb("vecs", [128, 16, KC], F32)
    tri = sb("tri", [128, 2, 128], F32)
    mask4 = sb("mask4", [128, 2, 512], BF16)
    namask = sb("namask", [128, 2, 256], BF16)
    bo = sb("bo", [128, 2, 128], BF16)
    identf = sb("identf", [128, 128], F32)
    dma(SP, vecs[:, 0:14, :], vecs_d[:, :, :], w=[vecs], sem="c_vecs")
    dma(SP, tri[:], tri_d[:, :, :], w=[tri], sem="c_tri")
    dma(POOL, mask4[:], mask4_d[:, :, :], w=[mask4], sem="c_mask4")
    dma(POOL, namask[:], namask_d[:, :, :], w=[namask], sem="c_namask")
    dma(POOL, bo[:], bo_d[:, :, :], w=[bo], sem="c_bo")
    for e in range(2):
        dma(POOL, w1cat[:, :, e * 64:(e + 1) * 64], w1_d[e].rearrange("(kc p) r -> p kc r", p=128), w=[w1cat], sem="c_w1")
        dma(POOL, a1cat[:, :, e * 64:(e + 1) * 64], a1_d[e].rearrange("(kc p) r -> p kc r", p=128), w=[a1cat], sem="c_a1")
        dma(POOL, w2cat[64 * e:64 * e + 64, :], w2_d[e], w=[w2cat], sem="c_w2")
        dma(POOL, a2cat[64 * e:64 * e + 64, :], a2_d[e], w=[a2cat], sem="c_a2")
        bcast_row(w0b, w0b[:, e, :], w0_d[e:e + 1, :], D)
    dma(POOL, g1s[:], g1_d.rearrange("(kc p) r -> p kc r", p=128), w=[g1s], sem="c_g1")
    dma(POOL, g2s[:], g2_d[:, :], w=[g2s], sem="c_g2")
    bcast_row(lng, lng[:], post_g[1:2, :], D)
    bcast_row(lnb, lnb[:], post_b[1:2, :], D)
    op(POOL, lambda E: E.tensor_copy(out=identf[:], in_=ident[:]), r=[ident], w=[identf])
    V_MU, V_KK, V_KA, V_RK, V_LG, V_LB, V_A0, V_OMKA = 0, 6, 7, 8, 9, 10, 11, 13
    op(DVE, lambda E: E.tensor_scalar(out=vecs[:, V_OMKA, :], in0=vecs[:, V_KA, :], scalar1=-1.0, scalar2=1.0,
                                      op0=ALU.mult, op1=ALU.add), r=[vecs], w=[vecs])
    op(DVE, lambda E: E.tensor_scalar_mul(out=vecs[:, V_RK, :], in0=vecs[:, V_RK, :], scalar1=0.5), r=[vecs], w=[vecs])
    op(DVE, lambda E: E.tensor_scalar_mul(out=vecs[:, 14:16, :], in0=vecs[:, V_A0:V_A0 + 2, :], scalar1=-1.0), r=[vecs], w=[vecs])

    mark_blk = kb.mark()
    u1T_s = sb("u1T_s", [128, KC, 1026], BF16)
    u1T = [T(u1T_s.h[:, :, i * 258:(i + 1) * 258], f"u1T{i}") for i in range(2)]
    vext = [sb(f"vext{c}", [128, 16, 128], BF16) for c in range(2)]
    yst = [sb(f"yst{i}", [128, 2, 256], BF16) for i in range(2)]
    bonst = sb("bonst", [128, 256], BF16)
    ext_d = kb.dram("ext_d", [4, 2, 16, 64, 128], F32)
    yext_d = kb.dram("yext_d", [4, 2, 8, 128, 512], BF16)
    post_d = kb.dram("post_d", [4, 8, 2, 128, 256], BF16)
    dx = sb("dx", [128, KC, 256], BF16)
    xl = sb("xl", [128, KC, 256], BF16)
    hwT = sb("hwT", [128, 256], BF16)
    haT = sb("haT", [128, 256], BF16)
    hgT = sb("hgT", [128, 256], BF16)
    vtok = [sb(f"vtok{c}", [128, D], BF16) for c in range(2)]
    vT = sb("vT", [128, KC, 256], BF16)
    krawT = sb("krawT", [128, KC, 256], BF16)
    rT = sb("rT", [128, KC, 256], BF16)
    gzT = sb("gzT", [128, KC, 256], BF16)
    sig = [[sb(f"sig{c}{e}", [128, D], F32) for e in range(2)] for c in range(2)]
    oT = sb("oT", [128, KC, 256], BF16)
    a_t = [sb(f"a_t{e}", [128, 256], F32) for e in range(2)]
    kk0 = sb("kk0", [128, 256], F32)
    kap = sb("kap", [128, 256], F32)
    kd_t = [sb(f"kd_t{e}", [128, 256], F32) for e in range(2)]
    b_t = [sb(f"b_t{e}", [128, 256], F32) for e in range(2)]
    sqb = sb("sqb", [128, 256], BF16)
    rkr2 = [sb(f"rkr{i}", [128, 256], BF16) for i in range(2)]
    gtmp = kk0
    Einc = [sb(f"Einc{i}", [128, 130], F32) for i in range(4)]
    Enin = [sb(f"Enin{i}", [128, 128], F32) for i in range(4)]
    KR = [sb(f"KR{i}", [128, 2, 128], BF16) for i in range(4)]
    KBt = [sb(f"KB{i}", [128, 2, 128], BF16) for i in range(4)]
    KBbar2 = [sb(f"KBbar{e}", [128, 2, 128], BF16) for e in range(4)]
    KBtok = [sb(f"KBtok{i}", [128, 2, 128], BF16) for i in range(4)]
    AFM = [[sb(f"AFM{i}{h}", [128, 512], BF16) for h in range(2)] for i in range(4)]
    nA = [sb(f"nA{i}", [128, 2, 128], BF16) for i in range(4)]
    T02 = [sb(f"T0{e}", [128, 2, 128], BF16) for e in range(4)]
    BA22 = [sb(f"BA2{e}", [128, 2, 2, 128], BF16) for e in range(4)]
    T12 = [sb(f"T1{e}", [128, 2, 128], BF16) for e in range(4)]
    A42 = [sb(f"A4{e}", [128, 2, 128], BF16) for e in range(4)]
    TT = [sb(f"TT{i}", [128, 2, 128], BF16) for i in range(4)]
    Zb2 = [sb(f"Zb{e}", [128, 2, 128], BF16) for e in range(2)]
    nU2 = [sb(f"nU{e}", [128, 2, 128], BF16) for e in range(2)]
    Sf = [sb(f"Sf{e}", [128, 128], F32) for e in range(2)]
    Sb = [sb(f"Sb{e}", [128, 128], BF16) for e in range(2)]
    yT = sb("yT", [128, 256], F32)
    ybf = sb("ybf", [128, 256], BF16)
    yc = sb("yc", [128, 256], F32)
    sdv = sb("sdv", [128, 256], F32)
    st_out = sb("st_out", [64, 2, 64], F32)
    for i in range(4):
        op(POOL, lambda E, i=i: E.memset(Einc[i][:], 1.0), w=[Einc[i]])
    for i in range(2):
        op(POOL, lambda E, i=i: E.memset(u1T[i][:], 0.0), w=[u1T[i]])
        op(POOL, lambda E, i=i: E.memset(vext[i][:], 0.0), w=[vext[i]])

    wctr = [0]

    def load_wh(i, half):
        wb = wbuf[wctr[0] % 2]
        wctr[0] += 1
        dma(SP, wb[:], wbf_d[i][:, half * 512:(half + 1) * 512].rearrange("(kc p) n -> p kc n", p=128), r=[wbf_d[i]], w=[wb],
            sem=wb.name)
        return wb

    def fm_proj(wi, rhs_fn, rtrk, evac):
        for hw in range(2):
            wb = load_wh(wi, hw)
            for cp in (2 * hw, 2 * hw + 1):
                p = npb()
                for half in range(2):
                    ct = cp * 2 + half
                    for kc in range(KC):
                        op(PE, lambda E, p=p, half=half, ct=ct, kc=kc, wb=wb: E.matmul(
                            p[:, half * 256:(half + 1) * 256], lhsT=wb[:, kc, (ct % 4) * 128:(ct % 4 + 1) * 128], rhs=rhs_fn(kc),
                            start=(kc == 0), stop=(kc == KC - 1)), r=[wb] + rtrk, w=[p])
                evac(cp, p)

    def rwkv_block(blk, x1rows, x1deps, v, yout_rows, state_out, stacked=False):
        VW = 128 if stacked else 64
        if stacked:
            ub = T(u1T_s.h[:, :, blk * 256:blk * 256 + 258], "ubv")
            ub.trk = u1T_s.trk
        else:
            ub = u1T[blk % 2]
            for c in range(2):
                ln_transpose(x1rows[c * 128:(c + 1) * 128, :], x1deps[c], modT[1], v, ub, 1 + c * 128, ub)

        def vsrc(c, h):
            return vext[c][:, h, :] if stacked else vtok[c][:, h * 64:(h + 1) * 64]
        op(DVE, lambda E: E.tensor_tensor(out=dx[:], in0=ub[:, :, 0:256], in1=ub[:, :, 2:258], op=ALU.add),
           r=[ub], w=[dx])
        op(DVE, lambda E: E.scalar_tensor_tensor(out=dx[:], in0=dx[:], scalar=0.5, in1=ub[:, :, 1:257],
                                                 op0=ALU.mult, op1=ALU.subtract), r=[ub, dx], w=[dx])

        def lerp(i):
            for kc in range(KC):
                op(DVE, lambda E, kc=kc: E.scalar_tensor_tensor(
                    out=xl[:, kc, :], in0=dx[:, kc, :], scalar=vecs[:, V_MU + i, kc:kc + 1], in1=ub[:, kc, 1:257],
                    op0=ALU.mult, op1=ALU.add), r=[dx, ub, vecs], w=[xl])

        def hidden(wcat, hT, func):
            p = npb()
            for kc in range(KC):
                op(PE, lambda E, kc=kc: E.matmul(p[:, 0:256], lhsT=wcat[:, kc, :], rhs=xl[:, kc, :],
                                                 start=(kc == 0), stop=(kc == KC - 1)), r=[wcat, xl], w=[p])
            op(ACT, lambda E: E.activation(out=hT[:], in_=p[:, 0:256], func=func), r=[p], w=[hT])

        lerp(1)
        hidden(w1cat, hwT, AF.Tanh)
        lerp(4)
        hidden(a1cat, haT, AF.Identity)
        lerp(5)
        hidden(g1s, hgT, AF.Sigmoid)
        lerp(3)
        for nh in range(2):
            wv = load_wh(2, nh)
            for c in range(2):
                p = npb()
                for kc in range(KC):
                    op(PE, lambda E, kc=kc, c=c, nh=nh, p=p, wv=wv: E.matmul(
                        p[:, :], lhsT=xl[:, kc, c * 128:(c + 1) * 128], rhs=wv[:, kc, :],
                        start=(kc == 0), stop=(kc == KC - 1)), r=[xl, wv], w=[p])
                op(ACT, lambda E, c=c, nh=nh, p=p: E.activation(out=vtok[c][:, nh * 512:(nh + 1) * 512], in_=p[:, :],
                                                                func=AF.Identity), r=[p], w=[vtok[c]])
        for c in range(2):
            pt = nptr()
            for kc in range(KC):
                op(PE, lambda E, kc=kc, c=c, pt=pt: E.transpose(pt[:, kc * 128:(kc + 1) * 128],
                                                                vtok[c][:, kc * 128:(kc + 1) * 128], ident[:]),
                   r=[vtok[c], ident], w=[pt])
            op(DVE, lambda E, c=c, pt=pt: E.tensor_copy(out=vT[:, :, c * 128:(c + 1) * 128],
                                                        in_=pt[:, :].rearrange("p (k t) -> p k t", k=KC)),
               r=[pt], w=[vT])
            if stacked:
                op(POOL, lambda E, c=c: E.tensor_copy(out=vext[c][:, :, 64:128],
                                                      in_=vtok[c][:, :].rearrange("p (h d) -> p h d", h=16)),
                   r=[vtok[c]], w=[vext[c]])
        lerp(2)
        fm_proj(1, lambda kc: xl[:, kc, :], [xl],
                lambda cp, p: op(DVE, lambda E: E.tensor_copy(
                    out=krawT[:, 2 * cp:2 * cp + 2, :], in_=p[:, :].rearrange("p (k t) -> p k t", k=2)),
                    r=[p], w=[krawT]))
        lerp(0)
        fm_proj(0, lambda kc: xl[:, kc, :], [xl],
                lambda cp, p: op(ACT, lambda E: E.activation(
                    out=rT[:, 2 * cp:2 * cp + 2, :], in_=p[:, :].rearrange("p (k t) -> p k t", k=2), func=AF.Identity),
                    r=[p], w=[rT]))
        fm_proj(3, lambda kc: ub[:, kc, 1:257], [ub],
                lambda cp, p: op(ACT, lambda E: E.activation(
                    out=gzT[:, 2 * cp:2 * cp + 2, :], in_=p[:, :].rearrange("p (k t) -> p k t", k=2), func=AF.Silu),
                    r=[p], w=[gzT]))
        for c in range(2):
            for e in range(2):
                for nh in range(2):
                    p = npb()
                    op(PE, lambda E, c=c, e=e, nh=nh, p=p: E.matmul(
                        p[:, :], lhsT=hwT[64 * e:64 * e + 64, c * 128:(c + 1) * 128],
                        rhs=w2cat[64 * e:64 * e + 64, nh * 512:(nh + 1) * 512], start=True, stop=True),
                       r=[hwT, w2cat], w=[p])
                    op(DVE, lambda E, c=c, e=e, nh=nh, p=p: E.tensor_tensor(
                        out=sig[c][e][:, nh * 512:(nh + 1) * 512], in0=p[:, :], in1=w0b[:, e, nh * 512:(nh + 1) * 512],
                        op=ALU.add), r=[p, w0b], w=[sig[c][e]])
                op(ACT, lambda E, c=c, e=e: E.activation(out=sig[c][e][:], in_=sig[c][e][:], func=AF.Sigmoid),
                   r=[sig[c][e]], w=[sig[c][e]])

        if CUT == 'proj':
            return
        pair = [0]
        def ctgen(ct):
            cs_ = slice(ct * 128, (ct + 1) * 128)
            rkr = rkr2[ct % 2]
            p = npb()
            op(PE, lambda E, p=p, cs_=cs_: E.matmul(p[:, 0:256], lhsT=g2s[:, cs_], rhs=hgT[:], start=True, stop=True),
               r=[g2s, hgT], w=[p])
            op(DVE, lambda E, p=p, ct=ct: E.tensor_tensor(out=gzT[:, ct, :], in0=p[:, 0:256], in1=gzT[:, ct, :], op=ALU.mult),
               r=[p, gzT], w=[gzT])
            if CUT == 'c1':
                return
            for e in range(2):
                p = npb()
                op(PE, lambda E, p=p, e=e, cs_=cs_: E.matmul(
                    p[:, 0:256], lhsT=a2cat[64 * e:64 * e + 64, cs_], rhs=haT[64 * e:64 * e + 64, :],
                    start=True, stop=True), r=[a2cat, haT], w=[p])
                op(ACT, lambda E, p=p, e=e, ct=ct: E.activation(
                    out=a_t[e][:], in_=p[:, 0:256], func=AF.Exp,
                    bias=vecs[:, 14 + e, ct:ct + 1], scale=-1.0), r=[p, vecs], w=[a_t[e]])
                op(ACT, lambda E, e=e: E.activation(out=a_t[e][:], in_=a_t[e][:], func=AF.Ln, bias=consts[:, 3:4], scale=1.0),
                   r=[a_t[e], consts], w=[a_t[e]])
                op(ACT, lambda E, e=e: E.activation(out=a_t[e][:], in_=a_t[e][:], func=AF.Exp, scale=-1.0),
                   r=[a_t[e]], w=[a_t[e]])
            if CUT == 'c2':
                return
            op(DVE, lambda E, ct=ct: E.tensor_scalar_mul(out=kk0[:], in0=krawT[:, ct, :], scalar1=vecs[:, V_KK, ct:ct + 1]),
               r=[krawT, vecs], w=[kk0])
            op(POOL, lambda E: E.tensor_mul(out=sqb[:], in0=kk0[:], in1=kk0[:]), r=[kk0], w=[sqb])
            p = npb()
            op(PE, lambda E, p=p: E.matmul(p[:, 0:256], lhsT=bo[:, 0, :], rhs=sqb[:], start=True, stop=True),
               r=[bo, sqb], w=[p])
            op(ACT, lambda E, p=p: E.activation(out=kap[:], in_=p[:, 0:256], func=AF.Ln, bias=consts[:, 2:3], scale=1.0),
               r=[p, consts], w=[kap])
            op(ACT, lambda E: E.activation(out=kap[:], in_=kap[:], func=AF.Exp, scale=-0.5), r=[kap], w=[kap])
            op(POOL, lambda E: E.tensor_mul(out=kap[:], in0=kap[:], in1=kk0[:]), r=[kap, kk0], w=[kap])
            if CUT == 'c3':
                return
            for e in range(2):
                op(DVE, lambda E, e=e, ct=ct: E.tensor_scalar(
                    out=kd_t[e][:], in0=a_t[e][:], scalar1=vecs[:, V_KA, ct:ct + 1], scalar2=vecs[:, V_OMKA, ct:ct + 1],
                    op0=ALU.mult, op1=ALU.add), r=[a_t[e], vecs], w=[kd_t[e]])
                op(POOL, lambda E, e=e, ct=ct: E.tensor_mul(out=kd_t[e][:], in0=kd_t[e][:], in1=krawT[:, ct, :]),
                   r=[kd_t[e], krawT], w=[kd_t[e]])
                op(POOL, lambda E, e=e: E.tensor_mul(out=b_t[e][:], in0=kap[:], in1=a_t[e][:]), r=[kap, a_t[e]], w=[b_t[e]])
            if CUT == 'c4':
                return
            op(POOL, lambda E: E.tensor_add(out=gtmp[:], in0=kd_t[0][:], in1=kd_t[1][:]), r=[kd_t[0], kd_t[1]], w=[gtmp])
            op(POOL, lambda E, ct=ct: E.tensor_mul(out=gtmp[:], in0=gtmp[:], in1=rT[:, ct, :]), r=[gtmp, rT], w=[gtmp])
            op(DVE, lambda E, ct=ct: E.tensor_scalar_mul(out=rkr[:], in0=gtmp[:], scalar1=vecs[:, V_RK, ct:ct + 1]),
               r=[gtmp, vecs], w=[rkr])

            if CUT == 'ctprep':
                return
            yield
            op(POOL, lambda E: E.memset(yT[:], 0.0), w=[yT])

            def prep(e, ci, c):
                i = 2 * e + ci
                KBbar, T0, BA2, T1, A4 = KBbar2[i], T02[i], BA22[i], T12[i], A42[i]
                Zb, nU = Zb2[e], nU2[e]
                tsl = slice(c * 128, (c + 1) * 128)
                ei, en, kr, kbt, kbtok, af, na_, tt = Einc[i], Enin[i], KR[i], KBt[i], KBtok[i], AFM[i], nA[i], TT[i]
                ex = ei[:, 0:128] if e == 0 else ei[:, 2:130]
                wc = ei[:, 128:129] if e == 0 else ei[:, 1:2]
                p = npb()
                op(PE, lambda E, p=p, c=c, e=e, cs_=cs_: E.matmul(p[:, 0:128], lhsT=sig[c][e][:, cs_], rhs=tri[:, e, :],
                                                                  start=True, stop=True), r=[sig[c][e], tri], w=[p])
                op(ACT, lambda E, p=p, ei=ei: E.activation(out=ei[:, 1:129], in_=p[:, 0:128], func=AF.Exp), r=[p], w=[ei])
                op(ACT, lambda E, p=p, en=en: E.activation(out=en[:], in_=p[:, 0:128], func=AF.Exp, scale=-1.0), r=[p], w=[en])
                yield
                op(DVE, lambda E, kr=kr, ex=ex, tsl=tsl: E.tensor_tensor(out=kr[:, 0, :], in0=kap[:, tsl], in1=ex, op=ALU.mult),
                   r=[kap, ei], w=[kr])
                op(POOL, lambda E, kr=kr, ei=ei, tsl=tsl, ct=ct: E.tensor_mul(out=kr[:, 1, :], in0=rT[:, ct, tsl], in1=ei[:, 1:129]),
                   r=[rT, ei], w=[kr])
                op(DVE, lambda E, kbt=kbt, en=en, tsl=tsl, e=e: E.tensor_tensor(out=kbt[:, 0, :], in0=kd_t[e][:, tsl], in1=en[:], op=ALU.mult),
                   r=[kd_t[e], en], w=[kbt])
                op(POOL, lambda E, kbt=kbt, en=en, tsl=tsl, e=e: E.tensor_mul(out=kbt[:, 1, :], in0=b_t[e][:, tsl], in1=en[:]),
                   r=[b_t[e], en], w=[kbt])
                op(ACT, lambda E, kbt=kbt, wc=wc: E.activation(out=KBbar[:], in_=kbt[:], func=AF.Identity, scale=wc),
                   r=[kbt, ei], w=[KBbar])
                yield
                pt = nptr()
                for j in range(2):
                    op(PE, lambda E, pt=pt, j=j: E.transpose(pt[:, j * 128:(j + 1) * 128], KBbar[:, j, :], ident[:]),
                       r=[KBbar, ident], w=[pt])
                op(DVE, lambda E, pt=pt, kbtok=kbtok: E.tensor_copy(
                    out=kbtok[:], in_=pt[:, 0:256].rearrange("p (j c) -> p j c", j=2)), r=[pt], w=[kbtok])
                yield
                if CUT == 'tilde':
                    return
                for h2 in range(2):
                    lo = 64 * h2
                    pa = npb()
                    pn = npb()
                    op(PE, lambda E, pa=pa, lo=lo, kbt=kbt, kr=kr: E.matmul(
                        pa[:, 0:256], lhsT=kbt[lo:lo + 64, 1, :], rhs=kr[lo:lo + 64, :, :], start=True, stop=True),
                       r=[kbt, kr], w=[pa])
                    op(PE, lambda E, pa=pa, lo=lo, kbt=kbt, kr=kr: E.matmul(
                        pa[:, 256:512], lhsT=kbt[lo:lo + 64, 0, :], rhs=kr[lo:lo + 64, :, :], start=True, stop=True),
                       r=[kbt, kr], w=[pa])
                    op(PE, lambda E, pn=pn, lo=lo, h2=h2, kbt=kbt, kr=kr: E.matmul(
                        pn[:, 0:128], lhsT=kr[lo:lo + 64, 0, :], rhs=kbt[lo:lo + 64, 1, :],
                        start=True, stop=True), r=[kbt, kr], w=[pn])
                    op(DVE, lambda E, pa=pa, h2=h2, af=af, e=e: E.tensor_tensor(
                        out=af[h2][:], in0=pa[:, :], in1=mask4[:, e, :], op=ALU.mult), r=[pa, mask4], w=[af[h2]])
                    op(DVE, lambda E, pn=pn, na_=na_, e=e, h2=h2: E.tensor_tensor(
                        out=na_[:, h2, :], in0=pn[:, 0:128], in1=namask[:, e, 0:128], op=ALU.mult),
                       r=[pn, namask], w=[na_])
                yield
                if CUT == 'aforms':
                    return
                for h2 in range(2):
                    op(POOL, lambda E, h2=h2, af=af: E.tensor_add(out=T0[:, h2, :], in0=ident[:], in1=af[h2][:, 0:128]),
                       r=[ident, af[h2]], w=[T0])
                p1 = npb()
                for h2 in range(2):
                    op(PE, lambda E, p1=p1, h2=h2, af=af, na_=na_: E.matmul(
                        p1[:, h2 * 256:h2 * 256 + 128], lhsT=na_[:, h2, :], rhs=af[h2][:, 0:128], start=True, stop=True),
                       r=[na_, af[h2]], w=[p1])
                    op(PE, lambda E, p1=p1, h2=h2, af=af, na_=na_: E.matmul(
                        p1[:, h2 * 256 + 128:h2 * 256 + 256], lhsT=af[h2][:, 0:128], rhs=na_[:, h2, :], start=True, stop=True),
                       r=[na_, af[h2]], w=[p1])
                op(ACT, lambda E, p1=p1: E.activation(out=BA2[:].rearrange("p h j t -> p (h j t)"), in_=p1[:, :],
                                                      func=AF.Identity), r=[p1], w=[BA2])
                yield
                p2 = npb()
                for h2 in range(2):
                    op(PE, lambda E, p2=p2, h2=h2: E.matmul(p2[:, h2 * 128:(h2 + 1) * 128], lhsT=ident[:], rhs=T0[:, h2, :],
                                                            start=True, stop=False), r=[ident, T0], w=[p2])
                    op(PE, lambda E, p2=p2, h2=h2: E.matmul(p2[:, h2 * 128:(h2 + 1) * 128], lhsT=BA2[:, h2, 1, :], rhs=T0[:, h2, :],
                                                            start=False, stop=True), r=[BA2, T0], w=[p2])
                    op(PE, lambda E, p2=p2, h2=h2: E.matmul(p2[:, 256 + h2 * 128:256 + (h2 + 1) * 128], lhsT=BA2[:, h2, 0, :],
                                                            rhs=BA2[:, h2, 1, :], start=True, stop=True), r=[BA2], w=[p2])
                op(ACT, lambda E, p2=p2: E.activation(out=T1[:].rearrange("p h t -> p (h t)"), in_=p2[:, 0:256],
                                                      func=AF.Identity), r=[p2], w=[T1])
                op(ACT, lambda E, p2=p2: E.activation(out=A4[:].rearrange("p h t -> p (h t)"), in_=p2[:, 256:512],
                                                      func=AF.Identity), r=[p2], w=[A4])
                yield
                p3 = npb()
                for h2 in range(2):
                    op(PE, lambda E, p3=p3, h2=h2: E.matmul(p3[:, h2 * 128:(h2 + 1) * 128], lhsT=ident[:], rhs=T1[:, h2, :],
                                                            start=True, stop=False), r=[ident, T1], w=[p3])
                    op(PE, lambda E, p3=p3, h2=h2: E.matmul(p3[:, h2 * 128:(h2 + 1) * 128], lhsT=A4[:, h2, :], rhs=T1[:, h2, :],
                                                            start=False, stop=True), r=[A4, T1], w=[p3])
                op(ACT, lambda E, p3=p3, tt=tt: E.activation(out=tt[:].rearrange("p h t -> p (h t)"), in_=p3[:, 0:256],
                                                             func=AF.Identity), r=[p3], w=[tt])
                yield
                yield

            def serial(e, ci, c):
                i = 2 * e + ci
                KBbar, T0, BA2, T1, A4 = KBbar2[i], T02[i], BA22[i], T12[i], A42[i]
                Zb, nU = Zb2[e], nU2[e]
                tsl = slice(c * 128, (c + 1) * 128)
                ei, en, kr, kbt, kbtok, af, na_, tt = Einc[i], Enin[i], KR[i], KBt[i], KBtok[i], AFM[i], nA[i], TT[i]
                ex = ei[:, 0:128] if e == 0 else ei[:, 2:130]
                wc = ei[:, 128:129] if e == 0 else ei[:, 1:2]
                if CUT == 'tchain':
                    return
                first = (ci == 0) and not stacked
                for h2 in range(2):
                    lo = 64 * h2
                    h = 2 * ct + h2
                    pz = npb()
                    if not first:
                        op(PE, lambda E, pz=pz, h2=h2, lo=lo, kr=kr, e=e: E.matmul(
                            pz[:, 0:VW], lhsT=kr[lo:lo + 64, 0, :], rhs=Sb[e][lo:lo + 64, 0:VW],
                            start=True, stop=False), r=[kr, Sb[e]], w=[pz])
                    op(PE, lambda E, pz=pz, h2=h2, h=h, af=af, c=c, first=first: E.matmul(
                        pz[:, 0:VW], lhsT=af[h2][:, 256:384], rhs=vsrc(c, h),
                        start=first, stop=True), r=[af[h2], vtok[c], vext[c]], w=[pz])
                    op(ACT, lambda E, pz=pz, h2=h2: E.activation(out=Zb[:, h2, 0:VW], in_=pz[:, 0:VW],
                                                                 func=AF.Identity), r=[pz], w=[Zb])
                yield
                pu = npb()
                for h2 in range(2):
                    op(PE, lambda E, pu=pu, h2=h2, tt=tt: E.matmul(pu[:, h2 * VW:(h2 + 1) * VW], lhsT=tt[:, h2, :], rhs=Zb[:, h2, 0:VW],
                                                                    start=True, stop=True), r=[tt, Zb], w=[pu])
                op(ACT, lambda E, pu=pu: E.activation(out=nU[:, :, 0:VW], in_=pu[:, 0:2 * VW].rearrange("p (h v) -> p h v", h=2),
                                                      func=AF.Identity, scale=-1.0), r=[pu], w=[nU])
                yield
                for h2 in range(2):
                    lo = 64 * h2
                    h = 2 * ct + h2
                    py = npb()
                    ylo = 0 if stacked else lo
                    if not first:
                        op(PE, lambda E, py=py, lo=lo, ylo=ylo, kr=kr, e=e: E.matmul(
                            py[ylo:ylo + VW, 0:128], lhsT=Sb[e][lo:lo + 64, 0:VW], rhs=kr[lo:lo + 64, 1, :],
                            start=True, stop=False), r=[kr, Sb[e]], w=[py])
                    op(PE, lambda E, py=py, ylo=ylo, h=h, h2=h2, af=af, c=c, first=first: E.matmul(
                        py[ylo:ylo + VW, 0:128], lhsT=vsrc(c, h), rhs=af[h2][:, 384:512],
                        start=first, stop=False), r=[af[h2], vtok[c], vext[c]], w=[py])
                    op(PE, lambda E, py=py, ylo=ylo, h2=h2, af=af: E.matmul(
                        py[ylo:ylo + VW, 0:128], lhsT=nU[:, h2, 0:VW], rhs=af[h2][:, 128:256],
                        start=False, stop=True), r=[af[h2], nU], w=[py])
                    if stacked:
                        ys_ = yst[e]
                        op(ACT, lambda E, py=py, tsl=tsl, h2=h2, ys_=ys_: E.activation(out=ys_[:, h2, tsl], in_=py[:, 0:128],
                                                                                      func=AF.Identity), r=[py], w=[ys_])
                    else:
                        op(DVE, lambda E, py=py, tsl=tsl, lo=lo: E.tensor_tensor(
                            out=yT[lo:lo + 64, tsl], in0=py[lo:lo + 64, 0:128], in1=yT[lo:lo + 64, tsl], op=ALU.add),
                           r=[py, yT], w=[yT])
                yield
                pss = npb()
                for h2 in range(2):
                    lo = 64 * h2
                    h = 2 * ct + h2
                    op(PE, lambda E, pss=pss, lo=lo, h=h, kbtok=kbtok, c=c: E.matmul(
                        pss[lo:lo + 64, 0:VW], lhsT=kbtok[:, 0, lo:lo + 64], rhs=vsrc(c, h),
                        start=True, stop=False), r=[kbtok, vtok[c], vext[c]], w=[pss])
                    op(PE, lambda E, pss=pss, lo=lo, h2=h2, kbtok=kbtok: E.matmul(
                        pss[lo:lo + 64, 0:VW], lhsT=kbtok[:, 1, lo:lo + 64], rhs=nU[:, h2, 0:VW],
                        start=False, stop=True), r=[kbtok, nU], w=[pss])
                op(DVE, lambda E, pss=pss, wc=wc, e=e: E.scalar_tensor_tensor(
                    out=Sf[e][:, 0:VW], in0=Sf[e][:, 0:VW], scalar=wc, in1=pss[:, 0:VW], op0=ALU.mult, op1=ALU.add),
                   r=[pss, ei, Sf[e]], w=[Sf[e]])
                op(POOL, lambda E, e=e: E.tensor_copy(out=Sb[e][:, 0:VW], in_=Sf[e][:, 0:VW]), r=[Sf[e]], w=[Sb[e]])
                yield

            def chain(e):
                op(POOL, lambda E, e=e: E.memset(Sf[e][:], 0.0), w=[Sf[e]])
                if stacked:
                    op(POOL, lambda E, e=e: E.tensor_add(out=Sf[e][:, 0:64], in0=identf[:, 0:64], in1=identf[:, 64:128]),
                       r=[identf, Sf[e]], w=[Sf[e]])
                op(POOL, lambda E, e=e: E.tensor_copy(out=Sb[e][:], in_=Sf[e][:]), r=[Sf[e]], w=[Sb[e]])
                for ci, c in enumerate((0, 1) if e == 0 else (1, 0)):
                    yield from serial(e, ci, c)
                if stacked:
                    dma(SP, ext_d[blk, e, 2 * ct:2 * ct + 2, :, :].rearrange("h k n -> (h k) n"), Sf[e][:], r=[Sf[e]],
                        w=[ext_d], sem=f"Sfo{e}")
                    dma(SP, yext_d[blk, e, ct, :, :], yst[e][:].rearrange("p h t -> p (h t)"), r=[yst[e]], w=[yext_d],
                        sem=f"yst{e}")
                elif state_out is not None and CUT not in ('tilde', 'aforms', 'tchain', 'nostate'):
                    pst = npb()
                    op(PE, lambda E, pst=pst, e=e: E.matmul(pst[0:64, 0:128], lhsT=Sf[e][:, 0:64], rhs=identf[:, :], start=True, stop=True),
                       r=[Sf[e], identf], w=[pst])
                    op(ACT, lambda E, pst=pst: E.activation(out=st_out[:].rearrange("v h k -> v (h k)"), in_=pst[0:64, 0:128],
                                                            func=AF.Identity), r=[pst], w=[st_out])
                    dma(SP, state_out[e, 2 * ct:2 * ct + 2, :, :].rearrange("h v k -> v h k"), st_out[:], r=[st_out], sem="st_out")

            def rr(gens):
                while gens:
                    for g_ in list(gens):
                        try:
                            next(g_)
                        except StopIteration:
                            gens.remove(g_)

            rr([prep(0, 0, 0), prep(1, 0, 1), prep(0, 1, 1), prep(1, 1, 0)])
            rr([chain(0), chain(1)])
            yield
            if CUT in ('tilde', 'aforms', 'tchain', 'serial'):
                return
            if stacked:
                pbn = npb()
                op(PE, lambda E, pbn=pbn: E.matmul(pbn[:, 0:256], lhsT=bo[:, 0, :], rhs=rkr[:], start=True, stop=True), r=[bo, rkr], w=[pbn])
                op(DVE, lambda E, pbn=pbn, ct=ct: E.tensor_tensor(out=bonst[:], in0=pbn[:, 0:256], in1=vT[:, ct, :], op=ALU.mult),
                   r=[pbn, vT], w=[bonst])
                dma(SP, post_d[blk, ct, 1, :, :], bonst[:], r=[bonst], w=[post_d], sem="bonst")
                return
            post_ct(ct, None, (yT, ybf, yc, sdv, gzT, oT), rkr)
        cgs = [ctgen(ct) for ct in range(KC)]
        next(cgs[0], None)
        for ct in range(KC):
            next(cgs[ct], None)
            if ct + 1 < KC:
                next(cgs[ct + 1], None)
            next(cgs[ct], None)
        if stacked:
            dma(SP, post_d[blk, :, 0, :, :].rearrange("c p t -> p c t"), gzT[:], r=[gzT], w=[post_d], sem="gzo")
            return
        if CUT:
            return
        wo = [load_wh(4, 0), load_wh(4, 1)]
        for c in range(2):
            y = out_proj_ln(0, x1rows[c * 128:(c + 1) * 128, :], oT, c * 128, oT, lambda nh: (wo[nh], wo[nh][:, :, :]), 1, v)
            dma(SP, yout_rows[c * 128:(c + 1) * 128, :], y[:], r=[y], sem=y.name)

    def post_ct(ct, bon_ap, bufs, rkr=None):
        yT, ybf, yc, sdv, gzT, oT = bufs
        if True:
            op(POOL, lambda E: E.tensor_copy(out=ybf[:], in_=yT[:]), r=[yT], w=[ybf])
            pm = npb()
            op(PE, lambda E, pm=pm: E.matmul(pm[:, 0:256], lhsT=bo[:, 1, :], rhs=ybf[:], start=True, stop=True), r=[bo, ybf], w=[pm])
            op(DVE, lambda E, pm=pm: E.tensor_tensor(out=yc[:], in0=yT[:], in1=pm[:, 0:256], op=ALU.subtract), r=[yT, pm], w=[yc])
            op(POOL, lambda E: E.tensor_mul(out=ybf[:], in0=yc[:], in1=yc[:]), r=[yc], w=[ybf])
            pv = npb()
            op(PE, lambda E, pv=pv: E.matmul(pv[:, 0:256], lhsT=bo[:, 1, :], rhs=ybf[:], start=True, stop=True), r=[bo, ybf], w=[pv])
            op(ACT, lambda E, pv=pv: E.activation(out=sdv[:], in_=pv[:, 0:256], func=AF.Ln, bias=consts[:, 1:2], scale=1.0),
               r=[pv, consts], w=[sdv])
            op(ACT, lambda E: E.activation(out=sdv[:], in_=sdv[:], func=AF.Exp, scale=-0.5), r=[sdv], w=[sdv])
            op(POOL, lambda E: E.tensor_mul(out=yc[:], in0=yc[:], in1=sdv[:]), r=[yc, sdv], w=[yc])
            op(ACT, lambda E, ct=ct: E.activation(out=yc[:], in_=yc[:], func=AF.Identity, scale=vecs[:, V_LG, ct:ct + 1],
                                                  bias=vecs[:, V_LB, ct:ct + 1]), r=[yc, vecs], w=[yc])
            if bon_ap is None:
                pbn = npb()
                op(PE, lambda E, pbn=pbn: E.matmul(pbn[:, 0:256], lhsT=bo[:, 0, :], rhs=rkr[:], start=True, stop=True), r=[bo, rkr], w=[pbn])
                op(DVE, lambda E, pbn=pbn, ct=ct: E.tensor_tensor(out=sdv[:], in0=pbn[:, 0:256], in1=vT[:, ct, :], op=ALU.mult),
                   r=[pbn, vT], w=[sdv])
                op(POOL, lambda E: E.tensor_add(out=yc[:], in0=yc[:], in1=sdv[:]), r=[yc, sdv], w=[yc])
            else:
                op(POOL, lambda E: E.tensor_add(out=yc[:], in0=yc[:], in1=bon_ap[0]), r=[yc, bon_ap[1]], w=[yc])
            op(POOL, lambda E, ct=ct: E.tensor_mul(out=oT[:, ct, :], in0=yc[:], in1=gzT[:, ct, :]), r=[yc, gzT], w=[oT])

    for s4 in range(NBLK):
        rwkv_block(s4, x1d[s4 * 256:(s4 + 1) * 256, :], [[x1d.sub(2 * s4)], [x1d.sub(2 * s4 + 1)]], 0,
                   o_yp[s4 * 256:(s4 + 1) * 256, :], o_state[s4])
    if stage == 3:
        return kb.finish(outs_final)

    kb.barrier()
    for t in range(8):
        ln_transpose(x1d[1024 + t * 128:1024 + (t + 1) * 128, :], [x1d.sub(8 + t)], modT[1], 1, u1T_s, 1 + t * 128, u1T_s)
    hin = kb.dram("hin", [2, D], BF16)
    hout = kb.dram("hout", [8, D], BF16)
    hsb = T(dx.h[0:8, 0:4, :].rearrange("p a b -> p (a b)"), "hsb")
    hsb.trk = dx.trk
    selh = sb("selh", [8, 2], BF16)
    selc = sb("selc", [64, 2, 4], F32)
    dma(POOL, selh[:], selh_d[:, :], w=[selh], sem="c3")
    dma(SP, selc[:], selc_d[:, :, :], w=[selc], sem="c3s")
    for w_, col in ((0, 1), (1, 1024)):
        dma(SP, hin[w_, :].rearrange("(kc p) -> p kc", p=128), u1T_s[:, :, col], r=[u1T_s], w=[hin], sem="hin",
            allow_slow_non_contiguous=True)
    kb.collective("AllGather", ALU.bypass, GROUPS, hin[:, :], hout[:, :], r=[hin], w=[hout], sem="ag2")
    dma(SP, hsb[:], hout[:, :], r=[hout], w=[hsb], sem="hsb")
    ph = npb()
    for kc in range(KC):
        op(PE, lambda E, kc=kc: E.matmul(ph[:, 2 * kc:2 * kc + 2], lhsT=hsb[0:8, kc * 128:(kc + 1) * 128], rhs=selh[0:8, :],
                                         start=True, stop=True), r=[hsb, selh], w=[ph])
    for w_, col in ((0, 0), (1, 1025)):
        op(DVE, lambda E, w_=w_, col=col: E.tensor_copy(out=u1T_s[:, :, col],
                                                        in_=ph[:, 0:16].rearrange("p (k w) -> p k w", w=2)[:, :, w_]),
           r=[ph], w=[u1T_s])
    for sblk in range(4):
        rwkv_block(sblk, None, None, 1, None, None, stacked=True)

    kb.barrier()
    kb.release(mark_blk)
    yT_x = sb("yT2", [128, 256], F32)
    ybf_x = sb("ybf2", [128, 256], BF16)
    yc_x = sb("yc2", [128, 256], F32)
    sdv_x = sb("sdv2", [128, 256], F32)
    gzT_x = sb("gzT2", [128, KC, 256], BF16)
    oT_x = sb("oT2", [128, KC, 256], BF16)
    EXTs = sb("EXTs", [64, 4, 8, 128], F32)
    QTt = sb("QTt", [64, 4, 8, 64], F32)
    Xab = [sb(f"Xab{i}", [64, 8, 128], F32) for i in range(2)]
    CE = sb("CE", [64, 4, 8, 128], F32)
    QcT = sb("QcT", [64, 4, 8, 64], F32)
    S0v = sb("S0v", [64, 8, 64], F32)
    Bab = [sb(f"Bab{i}", [64, 8, 64], F32) for i in range(2)]
    accS = sb("accS", [64, 8, 64], F32)
    FIN = sb("FIN", [128, 4, 2, 16, 64], BF16)
    yx = [sb(f"yx{i}", [128, 2, 512], BF16) for i in range(2)]
    pl = [sb(f"pl{i}", [128, 2, 256], BF16) for i in range(2)]
    cin = kb.dram("cin", [2, 16, 64, 128], F32)
    cout = kb.dram("cout", [4, 2, 16, 64, 128], F32)
    op(POOL, lambda E: E.memset(FIN[:], 0.0), w=[FIN])
    for j in range(4):
        for e in range(2):
            op(POOL, lambda E, j=j, e=e: E.tensor_copy(
                out=FIN[64:128, j, e, :, :], in_=ident[64:128, 64:128].unsqueeze(1).to_broadcast([64, 16, 64])),
               r=[ident, FIN], w=[FIN])

    def load_ext(e, hg):
        for j in range(4):
            dma(SP, EXTs[:, j, :, :], ext_d[j, e, hg * 8:(hg + 1) * 8, :, :].rearrange("h k n -> k h n"),
                r=[ext_d], w=[EXTs], sem="EXTs")
        for j in range(4):
            p = npb()
            for hh in range(8):
                op(PE, lambda E, p=p, j=j, hh=hh: E.matmul(p[0:64, hh * 64:(hh + 1) * 64], lhsT=EXTs[:, j, hh, 0:64],
                                                           rhs=identf[0:64, 0:64], start=True, stop=True), r=[EXTs, identf], w=[p])
            op(ACT, lambda E, p=p, j=j: E.activation(out=QTt[:, j, :, :].rearrange("p h k -> p (h k)"), in_=p[0:64, :],
                                                     func=AF.Identity), r=[p], w=[QTt])

    for e in range(2):
        order = [0, 1, 2, 3] if e == 0 else [3, 2, 1, 0]
        for hg in range(2):
            load_ext(e, hg)
            cur_ap = EXTs[:, order[0], :, :]
            cur_trk = EXTs
            for step, j in enumerate(order[1:]):
                xn = Xab[step % 2]
                for half in range(2):
                    p = npb()
                    for h4 in range(4):
                        hh = half * 4 + h4
                        op(PE, lambda E, p=p, h4=h4, hh=hh, j=j, cur_ap=cur_ap: E.matmul(
                            p[0:64, h4 * 128:(h4 + 1) * 128], lhsT=QTt[:, j, hh, :], rhs=cur_ap[:, hh, :], start=True, stop=True),
                           r=[QTt, cur_trk], w=[p])
                    pv3 = p[0:64, :].rearrange("p (h n) -> p h n", h=4)
                    op(ACT, lambda E, pv3=pv3, xn=xn, half=half: E.activation(out=xn[:, half * 4:half * 4 + 4, 0:64], in_=pv3[:, :, 0:64],
                                                                              func=AF.Identity), r=[p], w=[xn])
                    op(DVE, lambda E, pv3=pv3, xn=xn, half=half, j=j: E.tensor_tensor(
                        out=xn[:, half * 4:half * 4 + 4, 64:128], in0=pv3[:, :, 64:128], in1=EXTs[:, j, half * 4:half * 4 + 4, 64:128],
                        op=ALU.add), r=[p, EXTs], w=[xn])
                cur_ap, cur_trk = xn[:], xn
            dma(SP, cin[e, hg * 8:(hg + 1) * 8, :, :].rearrange("h k n -> k h n"), cur_ap, r=[cur_trk], w=[cin], sem="cin")
    kb.collective("AllGather", ALU.bypass, GROUPS, cin[:].rearrange("e h k n -> (e h k) n"),
                  cout[:].rearrange("r e h k n -> (r e h k) n"), r=[cin], w=[cout], sem="ag3")

    for e in range(2):
        order = [0, 1, 2, 3] if e == 0 else [3, 2, 1, 0]
        for hg in range(2):
            load_ext(e, hg)
            for r_ in range(4):
                dma(SP, CE[:, r_, :, :], cout[r_, e, hg * 8:(hg + 1) * 8, :, :].rearrange("h k n -> k h n"),
                    r=[cout], w=[CE], sem="CE")
            for r_ in range(4):
                p = npb()
                for hh in range(8):
                    op(PE, lambda E, p=p, r_=r_, hh=hh: E.matmul(p[0:64, hh * 64:(hh + 1) * 64], lhsT=CE[:, r_, hh, 0:64],
                                                                 rhs=identf[0:64, 0:64], start=True, stop=True), r=[CE, identf], w=[p])
                op(ACT, lambda E, p=p, r_=r_: E.activation(out=QcT[:, r_, :, :].rearrange("p h k -> p (h k)"), in_=p[0:64, :],
                                                           func=AF.Identity), r=[p], w=[QcT])
            dma(SP, S0v[:], st0_d[e, hg * 8:(hg + 1) * 8, :, :].rearrange("h v k -> v h k"), w=[S0v], sem="S0v")
            p = npb()
            for hh in range(8):
                op(PE, lambda E, p=p, hh=hh: E.matmul(p[0:64, hh * 64:(hh + 1) * 64], lhsT=S0v[:, hh, :], rhs=identf[0:64, 0:64],
                                                      start=True, stop=True), r=[S0v, identf], w=[p])
            bcur = Bab[0]
            op(ACT, lambda E, p=p, bcur=bcur: E.activation(out=bcur[:].rearrange("p h v -> p (h v)"), in_=p[0:64, :],
                                                           func=AF.Identity), r=[p], w=[bcur])
            op(DVE, lambda E, bcur=bcur, e=e: E.tensor_scalar_mul(out=accS[:], in0=bcur[:], scalar1=selc[:, e, 0:1]),
               r=[bcur, selc], w=[accS])
            for i in range(3):
                r_ = order[i]
                bn = Bab[(i + 1) % 2]
                p = npb()
                for hh in range(8):
                    op(PE, lambda E, p=p, hh=hh, r_=r_, bcur=bcur: E.matmul(
                        p[0:64, hh * 64:(hh + 1) * 64], lhsT=QcT[:, r_, hh, :], rhs=bcur[:, hh, :], start=True, stop=True),
                       r=[QcT, bcur], w=[p])
                op(DVE, lambda E, p=p, bn=bn, r_=r_: E.tensor_tensor(
                    out=bn[:], in0=p[0:64, :].rearrange("p (h v) -> p h v", h=8), in1=CE[:, r_, :, 64:128], op=ALU.add),
                   r=[p, CE], w=[bn])
                op(DVE, lambda E, bn=bn, e=e, i=i: E.scalar_tensor_tensor(
                    out=accS[:], in0=bn[:], scalar=selc[:, e, i + 1:i + 2], in1=accS[:], op0=ALU.mult, op1=ALU.add),
                   r=[bn, selc, accS], w=[accS])
                bcur = bn
            scur = accS
            for idx, j in enumerate(order):
                op(ACT, lambda E, scur=scur, j=j, e=e, hg=hg: E.activation(
                    out=FIN[0:64, j, e, hg * 8:(hg + 1) * 8, :], in_=scur[:], func=AF.Identity), r=[scur], w=[FIN])
                if idx < 3:
                    sn = Bab[idx % 2]
                    p = npb()
                    for hh in range(8):
                        op(PE, lambda E, p=p, hh=hh, j=j, scur=scur: E.matmul(
                            p[0:64, hh * 64:(hh + 1) * 64], lhsT=QTt[:, j, hh, :], rhs=scur[:, hh, :], start=True, stop=True),
                           r=[QTt, scur], w=[p])
                    op(DVE, lambda E, p=p, sn=sn, j=j: E.tensor_tensor(
                        out=sn[:], in0=p[0:64, :].rearrange("p (h v) -> p h v", h=8), in1=EXTs[:, j, :, 64:128], op=ALU.add),
                       r=[p, EXTs], w=[sn])
                    scur = sn

    nfx = 0
    for j in range(4):
        for ct in range(KC):
            yx_, pl_ = yx[nfx % 2], pl[nfx % 2]
            nfx += 1
            dma(SP, yx_[:], yext_d[j, :, ct, :, :].rearrange("e p n -> p e n"), r=[yext_d], w=[yx_], sem=yx_.name)
            dma(SP, pl_[:], post_d[j, ct, :, :, :].rearrange("w p t -> p w t"), r=[post_d], w=[pl_], sem=pl_.name)
            py = npb()
            for h2 in range(2):
                lo = 64 * h2
                for e in range(2):
                    op(PE, lambda E, py=py, lo=lo, h2=h2, e=e, j=j, ct=ct, yx_=yx_: E.matmul(
                        py[lo:lo + 64, 0:256], lhsT=FIN[:, j, e, 2 * ct + h2, :], rhs=yx_[:, e, h2 * 256:(h2 + 1) * 256],
                        start=(e == 0), stop=(e == 1)), r=[FIN, yx_], w=[py])
            op(ACT, lambda E, py=py: E.activation(out=yT_x[:], in_=py[:, 0:256], func=AF.Identity), r=[py], w=[yT_x])
            op(POOL, lambda E, ct=ct, pl_=pl_: E.tensor_copy(out=gzT_x[:, ct, :], in_=pl_[:, 0, :]), r=[pl_], w=[gzT_x])
            post_ct(ct, (pl_[:, 1, :], pl_), (yT_x, ybf_x, yc_x, sdv_x, gzT_x, oT_x))
        wo = [load_wh(4, 0), load_wh(4, 1)]
        for c in range(2):
            r0 = 1024 + j * 256 + c * 128
            y = out_proj_ln(0, x1d[r0:r0 + 128, :], oT_x, c * 128, oT_x, (lambda nh, wo=wo: (wo[nh], wo[nh][:, :, :])), 1, 1)
            dma(SP, o_ys[j * 256 + c * 128:j * 256 + (c + 1) * 128, :], y[:], r=[y], sem=y.name)

    return kb.finish(outs_final)


_NC_CACHE = {}


def _dft_tables():
    i = np.arange(128, dtype=np.float64)
    ang = 2 * np.pi * np.outer(i, i) / 128.0
    dftc = np.concatenate([np.cos(ang), -np.sin(ang)], 1) / np.sqrt(128.0)
    l = np.arange(256, dtype=np.float64)
    ang = 2 * np.pi * np.outer(l, l) / 256.0
    dftl = np.concatenate([np.cos(ang), np.sin(ang)], 1) / 16.0
    return dftc.astype(np.float32), dftl.astype(np.float32)


_DFTC, _DFTL256 = _dft_tables()


def _na_tables():
    half = np.arange(128)[:, None, None] // 64
    ck = np.arange(128)[:, None, None] % 64
    idx = np.arange(30)[None, :, None]
    cq = np.arange(64)[None, None, :]
    dr = 14 - idx + half + 0 * cq
    dc = np.clip(ck - cq + 15, 0, 30) + 0 * idx
    cstart = np.clip(cq - 8, 0, 48)
    col_in = (ck >= cstart) & (ck < cstart + 16)
    valid = (np.abs(dr) <= 7) & col_in
    dri = np.clip(dr + 7, 0, 14)
    qmask = np.zeros((4, 24, 1024), np.float32)
    for qd in range(4):
        for q in range(1024):
            r = 16 * qd + q // 64
            st = min(max(r - 4, 0), 56)
            for j in range(24):
                R = 16 * qd - 4 + j
                if not (st <= R < st + 8):
                    qmask[qd, j, q] = -1e30
    rowoh = (np.arange(1536)[None, :] // 64 == np.arange(24)[:, None]).astype(np.float32)
    return dri.astype(np.int64), dc.astype(np.int64), valid, qmask, rowoh


_BB_DR, _BB_DC, _BB_VALID, _QMASK, _ROWOH = _na_tables()
_DFTS_CACHE = {}


def _dfts(qd):
    if qd not in _DFTS_CACHE:
        l = np.arange(4096, dtype=np.int64)[:, None]
        m = (1024 * qd + np.arange(1024, dtype=np.int64))[None, :]
        ang = 2 * np.pi * ((l * m) % 4096).astype(np.float64) / 4096.0
        t = np.stack([np.cos(ang), np.sin(ang)], 1) / 64.0
        _DFTS_CACHE[qd] = np.ascontiguousarray(t.astype(np.float32).astype(ml_dtypes.bfloat16))
    return _DFTS_CACHE[qd]


def _scan_tables():
    i = np.arange(128)
    s_, t_ = i[:, None], i[None, :]
    us, ls = (s_ < t_).astype(np.float32), (s_ > t_).astype(np.float32)
    ui, li = (s_ <= t_).astype(np.float32), (s_ >= t_).astype(np.float32)
    tri = np.stack([-0.606531 * ui, -0.606531 * li], 1)
    mask4 = np.stack([np.concatenate([-us, ui, us, ui], 1), np.concatenate([-ls, li, ls, li], 1)], 1)
    namask = np.stack([np.concatenate([-ls, -ls], 1), np.concatenate([-us, -us], 1)], 1)
    blk = (s_ // 64 == t_ // 64).astype(np.float32)
    bo = np.stack([blk, blk / 64.0], 1)
    c = np.ascontiguousarray
    return c(tri.astype(np.float32)), c(mask4), c(namask), c(bo.astype(np.float32))


_TRI, _MASK4, _NAMASK, _BO = _scan_tables()


def _rw_vecs(inp):
    f = lambda a: np.asarray(a, dtype=np.float32)
    rows = [f(inp["rw_mu"])[0][i] for i in range(6)]
    rows += [f(inp["rw_k_k"])[0], f(inp["rw_k_a"])[0], f(inp["rw_r_k"])[0].reshape(-1), f(inp["rw_lnx_g"])[0],
             f(inp["rw_lnx_b"])[0], f(inp["rw_a0"])[0][0], f(inp["rw_a0"])[0][1], np.zeros(1024, np.float32)]
    v = np.stack(rows, 0)
    return np.ascontiguousarray(v.reshape(14, KC, 128).transpose(2, 0, 1))


def _prep_inputs(inp):
    f = lambda a: np.ascontiguousarray(np.asarray(a, dtype=np.float32))
    x_prompt = f(inp["x_prompt"])
    c = f(inp["c"])
    c_ctx = f(inp["c_ctx"])
    x_sample = f(inp["x_sample"])
    cache_k = f(inp["cache_k"])
    cache_v = f(inp["cache_v"])
    rpb = f(inp["ev_rpb"])[0]
    state_rwkv = f(inp["state_rwkv"])
    na_bias = np.where(_BB_VALID[None], rpb[:, _BB_DR, _BB_DC], np.float32(-1e30)).astype(np.float32).reshape(8, 128, 30 * 64)
    ada_b = f(inp["ada_b"])
    common = {
        "ada_w": f(inp["ada_w"]),
        "ada_b": ada_b,
        "ada_bT": np.ascontiguousarray(ada_b.reshape(2, 24, 128).transpose(0, 2, 1)),
        "ev_w_in": f(inp["ev_w_in"])[0],
        "ev_w_fnet": f(inp["ev_w_fnet"])[0],
        "ev_w_out": f(inp["ev_w_out"])[0],
        "post_ln_g": f(inp["post_ln_g"]),
        "post_ln_b": f(inp["post_ln_b"]),
        "rw_w_rkvz": f(inp["rw_w_rkvz"])[0],
        "rw_w1": f(inp["rw_w1"])[0], "rw_w2": f(inp["rw_w2"])[0],
        "rw_a1": f(inp["rw_a1"])[0], "rw_a2": f(inp["rw_a2"])[0],
        "rw_g1": f(inp["rw_g1"])[0], "rw_g2": f(inp["rw_g2"])[0],
        "rw_w0": f(inp["rw_w0"])[0], "rw_w_out": f(inp["rw_w_out"])[0],
        "rw_vecs": _rw_vecs(inp),
        "tri": _TRI, "mask4": _MASK4, "namask": _NAMASK, "blockones": _BO,
        "dftc": _DFTC,
        "dftl256": _DFTL256,
    }
    maps = []
    for core in range(NCORES):
        b = core // 4
        cv = np.stack([c_ctx, c[b]], 0)
        m = dict(common)
        m["xp"] = x_prompt[4 * core:4 * core + 4].reshape(1024, D)
        qd = core % 4
        win = np.zeros((24, 64, D), np.float32)
        r0 = 16 * qd - 4
        lo_, hi_ = max(r0, 0), min(r0 + 24, 64)
        win[lo_ - r0:hi_ - r0] = x_sample[b].reshape(64, 64, D)[lo_:hi_]
        m["xs_win"] = win.reshape(1536, D)
        m["cache_k"] = cache_k[b, 0].reshape(512, 512)
        m["cache_v"] = cache_v[b, 0].reshape(512, 512)
        m["na_bias"] = na_bias
        m["na_qmask"] = _QMASK[qd]
        m["na_rowoh"] = _ROWOH
        m["dfts"] = _dfts(qd)
        m["state0"] = state_rwkv[b, 0]
        selc = np.zeros((64, 2, 4), np.float32)
        selc[:, 0, qd] = 1.0
        selc[:, 1, 3 - qd] = 1.0
        m["selc"] = selc
        selh = np.zeros((8, 2), np.float32)
        if qd > 0:
            selh[2 * (qd - 1) + 1, 0] = 1.0
        if qd < 3:
            selh[2 * (qd + 1), 1] = 1.0
        m["selh"] = selh
        m["cvT"] = np.ascontiguousarray(cv.reshape(2, KC, 128).transpose(2, 1, 0))
        maps.append(m)
    return maps


def kernel(_stage=99, _raw=False, **inputs):
    if _stage not in _NC_CACHE:
        _NC_CACHE[_stage] = build(_stage)
    nc = _NC_CACHE[_stage]
    maps = _prep_inputs(inputs)
    res = run_bass_kernel_spmd(nc, maps, core_ids=list(range(NCORES)))
    R = res.results
    if _raw:
        return R
    y_prompt = np.concatenate([R[c]["o_yp"].reshape(4, 256, D) for c in range(NCORES)], 0)
    y_sample = np.concatenate([R[c]["o_ys"] for c in range(NCORES)], 0).reshape(2, 4096, D)
    new_k = np.concatenate([R[c]["o_newk"].reshape(4, 1, 256, 8, 64) for c in range(NCORES)], 0)
    new_v = np.concatenate([R[c]["o_newv"].reshape(4, 1, 256, 8, 64) for c in range(NCORES)], 0)
    new_state = np.concatenate([R[c]["o_state"].reshape(4, 1, 2, 16, 64, 64) for c in range(NCORES)], 0)
    return (y_prompt, y_sample, new_k, new_v, new_state)
```

```python
import os
import numpy as np
import ml_dtypes
from contextlib import ExitStack
import concourse.bass as bass
import concourse.mybir as mybir
from concourse.bass_utils import run_bass_kernel_spmd

F32 = mybir.dt.float32
BF16 = mybir.dt.bfloat16
AF = mybir.ActivationFunctionType
ALU = mybir.AluOpType
AX = mybir.AxisListType

NCORES = 8
D = 1024
KC = 8
LN_EPS = 1e-6
ALPHA = 4 ** 0.25
PE, ACT, DVE, POOL, SP = "pe", "act", "dve", "pool", "sp"


class Trk:
    __slots__ = ("w", "r", "name")

    def __init__(self, name=""):
        self.w = None
        self.r = {}
        self.name = name


class T:
    def __init__(self, h, name):
        self.h = h
        self.name = name
        self.trk = Trk(name)
        self.subs = {}

    def sub(self, key):
        if key not in self.subs:
            self.subs[key] = Trk(f"{self.name}.{key}")
        return self.subs[key]

    def __getitem__(self, k):
        return self.h[k]


def _trk(x):
    return x.trk if isinstance(x, T) else x


class KB:
    def __init__(self):
        self.nc = bass.Bass("TRN2", target_bir_lowering=False)
        self.es = ExitStack()
        self.q = {e: [] for e in (PE, ACT, DVE, POOL, SP)}
        self.cnt = {e: 0 for e in self.q}
        self.waited = {e: {} for e in self.q}
        self.sems = {}
        self.dcnt = {}
        for e in self.q:
            self.sems[e] = self.es.enter_context(self.nc.semaphore("s_" + e))
        self.n_ops = 0

    def sb(self, name, shape, dt):
        return T(self.es.enter_context(self.nc.sbuf_tensor(name, list(shape), dt)), name)

    def ps(self, name, shape, dt):
        return T(self.es.enter_context(self.nc.psum_tensor(name, list(shape), dt)), name)

    def dram(self, name, shape, dt, kind="Internal"):
        h = self.nc.dram_tensor(name, list(shape), dt, kind=kind)
        return T(h.ap(), name)

    def init_arena(self, nbytes):
        self.arena = self.es.enter_context(self.nc.sbuf_tensor("arena", [128, nbytes // 2], BF16))
        self.a_off = 0
        self.a_size = nbytes

    def alloc(self, name, shape, dt):
        esz = 4 if dt == F32 else 2
        n = int(np.prod(shape[1:]))
        nb = (n * esz + 63) // 64 * 64
        off = self.a_off
        self.a_off += nb
        assert self.a_off <= self.a_size, (name, self.a_off, self.a_size)
        v = self.arena[0:shape[0], off // 2:off // 2 + n * esz // 2]
        if dt == F32:
            v = v.bitcast(F32)
        if len(shape) > 2:
            names = [f"d{i}" for i in range(len(shape) - 1)]
            v = v.rearrange("p (" + " ".join(names) + ") -> p " + " ".join(names),
                            **{nm: int(sz) for nm, sz in zip(names[1:], shape[2:])})
        return T(v, name)

    def mark(self):
        return self.a_off

    def release(self, mark):
        self.a_off = mark

    def barrier(self):
        snap = [(k, v) for k, v in self.cnt.items() if v > 0] + [(k, v) for k, v in self.dcnt.items() if v > 0]
        sems = self.sems
        for e in self.q:
            waits = []
            for k, v in snap:
                if k == e and e in (PE, SP):
                    continue
                if self.waited[e].get(k, 0) >= v:
                    continue
                self.waited[e][k] = v
                waits.append((k, v))

            def run(E, waits=waits):
                for k, v in waits:
                    E.wait_ge(sems[k], v)
            self.q[e].append(run)

    def dsem(self, key):
        if key not in self.sems:
            self.sems[key] = self.es.enter_context(self.nc.semaphore("d_" + key))
            self.dcnt[key] = 0
        return key

    def _waits(self, eng, r, w):
        need = {}

        def add(ev):
            if ev is None:
                return
            k, v = ev
            if need.get(k, 0) < v:
                need[k] = v
        for t in r:
            add(t.w)
        for t in w:
            add(t.w)
            for k, v in t.r.items():
                add((k, v))
        out = []
        for k, v in need.items():
            if k == eng and eng == PE:
                continue
            if self.waited[eng].get(k, 0) >= v:
                continue
            self.waited[eng][k] = v
            out.append((k, v))
        return out

    def _commit(self, ev, r, w):
        for t in w:
            t.w = ev
            t.r = {}
        for t in r:
            if t.r.get(ev[0], 0) < ev[1]:
                t.r[ev[0]] = ev[1]

    def op(self, eng, fn, r=(), w=()):
        r = [_trk(x) for x in r]
        w = [_trk(x) for x in w]
        waits = self._waits(eng, r, w)
        self.cnt[eng] += 1
        ev = (eng, self.cnt[eng])
        sems = self.sems

        def run(E, waits=waits, fn=fn, s=sems[eng]):
            for k, v in waits:
                E.wait_ge(sems[k], v)
            fn(E).then_inc(s, 1)
        self.q[eng].append(run)
        self._commit(ev, r, w)
        self.n_ops += 1

    def dma(self, eng, out, in_, r=(), w=(), sem=None, **kw):
        r = [_trk(x) for x in r]
        w = [_trk(x) for x in w]
        key = self.dsem(sem)
        waits = self._waits(eng, r, w)
        self.dcnt[key] += 16
        ev = (key, self.dcnt[key])
        sems = self.sems

        def run(E, waits=waits, s=sems[key]):
            for k, v in waits:
                E.wait_ge(sems[k], v)
            E.dma_start(out=out, in_=in_, **kw).then_inc(s, 16)
        self.q[eng].append(run)
        self._commit(ev, r, w)
        self.n_ops += 1

    def collective(self, kind, op, groups, in_ap, out_ap, r, w, sem):
        r = [_trk(x) for x in r]
        w = [_trk(x) for x in w]
        key = self.dsem(sem)
        waits = self._waits(POOL, r, w)
        self.dcnt[key] += 1
        ev = (key, self.dcnt[key])
        sems = self.sems

        def run(E, waits=waits, s=sems[key]):
            for k, v in waits:
                E.wait_ge(sems[k], v)
            E.collective_compute(kind, op, replica_groups=groups, ins=[in_ap], outs=[out_ap]).then_inc(s)
        self.q[POOL].append(run)
        self._commit(ev, r, w)

    def finish(self, final_trks):
        final = [_trk(x) for x in final_trks]
        waits = self._waits(SP, final, final)
        waits += [(k, v) for k, v in self.dcnt.items() if v > 0]
        sems = self.sems

        def run(E):
            for k, v in waits:
                E.wait_ge(sems[k], v)
        self.q[SP].append(run)
        block = self.es.enter_context(self.nc.Block())
        q = self.q

        @block.sync
        def _(E):
            for f in q[SP]:
                f(E)

        @block.scalar
        def _(E):
            for f in q[ACT]:
                f(E)

        @block.vector
        def _(E):
            for f in q[DVE]:
                f(E)

        @block.gpsimd
        def _(E):
            for f in q[POOL]:
                f(E)

        @block.tensor
        def _(E):
            for f in q[PE]:
                f(E)
        self.es.close()
        return self.nc


def build(stage=99):
    kb = KB()
    nc = kb.nc
    op, dma = kb.op, kb.dma
    kb.init_arena(207 * 1024)
    sb = kb.alloc

    def din(name, shape, dt=F32):
        return nc.dram_tensor(name, list(shape), dt, kind="ExternalInput").ap()

    def dout(name, shape, dt=F32):
        return T(nc.dram_tensor(name, list(shape), dt, kind="ExternalOutput").ap(), name)

    xp = din("xp", [1024, D])
    cvT = din("cvT", [128, KC, 2])
    ada_w = din("ada_w", [2, D, 3 * D])
    ada_bT = din("ada_bT", [2, 128, 24])
    ada_b = din("ada_b", [2, 3 * D])
    post_g = din("post_ln_g", [2, D])
    post_b = din("post_ln_b", [2, D])
    w_in = din("ev_w_in", [D, 3 * D])
    w_fnet = din("ev_w_fnet", [4, 128, 128])
    w_out0 = din("ev_w_out", [D, D])
    dftc_d = din("dftc", [128, 256])
    dftl_d = din("dftl256", [256, 512])
    rkvz_d = din("rw_w_rkvz", [4, D, D])
    w1_d = din("rw_w1", [2, D, 64])
    w2_d = din("rw_w2", [2, 64, D])
    a1_d = din("rw_a1", [2, D, 64])
    a2_d = din("rw_a2", [2, 64, D])
    g1_d = din("rw_g1", [D, 128])
    g2_d = din("rw_g2", [128, D])
    w0_d = din("rw_w0", [2, D])
    wout1_d = din("rw_w_out", [D, D])
    vecs_d = din("rw_vecs", [128, 14, KC])
    tri_d = din("tri", [128, 2, 128])
    mask4_d = din("mask4", [128, 2, 512])
    namask_d = din("namask", [128, 2, 256])
    bo_d = din("blockones", [128, 2, 128])
    xs_win = din("xs_win", [1536, D])
    ck_d = din("cache_k", [512, 512])
    cv_d = din("cache_v", [512, 512])
    bb_d = din("na_bias", [8, 128, 30 * 64])
    qmask_d = din("na_qmask", [24, 1024])
    rowoh_d = din("na_rowoh", [24, 1536])
    dfts_d = din("dfts", [4096, 2, 1024], BF16)
    st0_d = din("state0", [2, 16, 64, 64])
    selc_d = din("selc", [64, 2, 4])
    selh_d = din("selh", [8, 2])
    o_ys = dout("o_ys", [1024, D])
    o_yp = dout("o_yp", [1024, D])
    o_state = dout("o_state", [4, 2, 16, 64, 64])
    o_newk = dout("o_newk", [1024, 512])
    o_newv = dout("o_newv", [1024, 512])
    outs_final = [o_newk, o_newv]

    def dump(name, tile_ap, shape, src_trk):
        o = dout("dbg_" + name, shape)
        dma(POOL, o[:], tile_ap, r=(src_trk if isinstance(src_trk, (list, tuple)) else [src_trk]), sem="dbg_" + name)

    w_in_bf = kb.dram("w_in_bf", [D, 3 * D], BF16)
    wbf_d = [kb.dram(f"wbf_d{i}", [D, D], BF16) for i in range(5)]

    pbs = [kb.ps(f"pb{i}", [128, 512], F32) for i in range(6)]
    ptrs = [kb.ps(f"ptr{i}", [128, D], BF16) for i in range(2)]
    pctr = [0, 0]

    def npb():
        pctr[0] += 1
        return pbs[pctr[0] % 6]

    def nptr():
        pctr[1] += 1
        return ptrs[pctr[1] % 2]

    ident = sb("ident", [128, 128], BF16)
    consts = sb("consts", [128, 4], F32)
    ones_bf = sb("ones_bf", [128, 128], BF16)
    ones_row = sb("ones_row", [1, 128], F32)
    row_stage = sb("row_stage", [1, 512], F32)
    modT = [sb(f"modT{l}", [128, 2, 16], F32) for l in range(2)]
    gate1 = [None, sb("gate1_1", [128, 2, D], F32)]
    lng = sb("lng", [128, D], F32)
    lnb = sb("lnb", [128, D], F32)
    xt = [sb(f"xt{i}", [128, D], F32) for i in range(2)]
    xn = [sb("xn0", [128, D], BF16)] * 2
    stats = [sb(f"stats{i}", [128, 2, 6], F32) for i in range(2)]
    mv = [sb(f"mv{i}", [128, 4], F32) for i in range(2)]
    ybuf = [sb(f"ybuf{i}", [128, D], F32) for i in range(2)]
    mark_base = kb.mark()
    gate1[0] = sb("gate1_0", [128, 2, D], F32)
    wout0 = sb("wout0", [128, KC, D], BF16)
    dftc = sb("dftc_sb", [128, 256], BF16)
    wf_sb = sb("wf_sb", [128, 4, 128], BF16)
    csw = sb("csw", [128, 4, 256], BF16)
    mark_l0 = kb.mark()
    kv_out = [sb(f"kv_out{i}", [128, 512], F32) for i in range(2)]
    w_in_sb = sb("w_in_sb", [128, KC, 3 * D], BF16)
    mark_ph = kb.mark()

    op(POOL, lambda E: E.memset(ident[:], 0.0), w=[ident])
    op(POOL, lambda E: E.affine_select(out=ident[:], in_=ident[:], pattern=[[-1, 128]], compare_op=ALU.not_equal,
                                       fill=1.0, base=0, channel_multiplier=1), r=[ident], w=[ident])
    for i, val in enumerate((LN_EPS, 64e-5, 1e-12, 1.0)):
        op(POOL, lambda E, i=i, val=val: E.memset(consts[:, i:i + 1], val), w=[consts])
    op(POOL, lambda E: E.memset(ones_bf[:], 1.0), w=[ones_bf])
    op(POOL, lambda E: E.memset(ones_row[:], 1.0), w=[ones_row])

    def bcast_row(dst, dst_ap, src_row_ap, n):
        for c0 in range(0, n, 512):
            cw = min(512, n - c0)
            dma(SP, row_stage[0:1, 0:cw], src_row_ap[:, c0:c0 + cw], w=[row_stage], sem="row_stage")
            p = npb()
            op(PE, lambda E, cw=cw, p=p: E.matmul(p[:, 0:cw], lhsT=ones_row[0:1, :], rhs=row_stage[0:1, 0:cw],
                                                  start=True, stop=True), r=[ones_row, row_stage], w=[p])
            op(ACT, lambda E, c0=c0, cw=cw, p=p: E.activation(out=dst_ap[:, c0:c0 + cw], in_=p[:, 0:cw], func=AF.Identity),
               r=[p], w=[dst])

    cv_sb = sb("cv_sb", [128, KC, 2], F32)
    sc = sb("sc", [128, KC, 2], BF16)
    sc_rep = sb("sc_rep", [128, KC, 2, 128], BF16)
    adab_sb = sb("adab_sb", [128, 2, 24], F32)
    gate_b = sb("gate_b", [128, D], F32)
    adaw = [sb(f"adaw{i}", [128, KC, 512], BF16) for i in range(2)]
    dma(SP, cv_sb[:], cvT[:, :, :], w=[cv_sb], sem="c_cv")
    op(ACT, lambda E: E.activation(out=sc[:], in_=cv_sb[:], func=AF.Silu), r=[cv_sb], w=[sc])
    for kc in range(KC):
        for v in range(2):
            op(DVE, lambda E, kc=kc, v=v: E.tensor_copy(out=sc_rep[:, kc, v, :],
                                                         in_=sc[:, kc, v:v + 1].to_broadcast([128, 128])),
               r=[sc], w=[sc_rep])
    dma(SP, adab_sb[:], ada_bT.rearrange("l p j -> p l j"), w=[adab_sb], sem="c_adab")
    bcast_row(lng, lng[:], post_g[0:1, :], D)
    bcast_row(lnb, lnb[:], post_b[0:1, :], D)
    for l in range(2):
        bcast_row(gate_b, gate_b[:], ada_b[l:l + 1, 2 * D:3 * D], D)
        ps_modT = npb()
        for nch in range(6):
            wb = adaw[(l * 6 + nch) % 2]
            dma(POOL, wb[:], ada_w[l, :, nch * 512:(nch + 1) * 512].rearrange("(kc p) n -> p kc n", p=128),
                w=[wb], sem=wb.name)
            if nch < 4:
                for jj in range(4):
                    j = nch * 4 + jj
                    for kc in range(KC):
                        op(PE, lambda E, kc=kc, jj=jj, j=j, wb=wb, ps_modT=ps_modT: E.matmul(
                            ps_modT[:, 2 * j:2 * j + 2], lhsT=wb[:, kc, jj * 128:(jj + 1) * 128], rhs=sc[:, kc, :],
                            start=(kc == 0), stop=(kc == KC - 1)), r=[wb, sc], w=[ps_modT])
                if nch == 3:
                    for v in range(2):
                        op(DVE, lambda E, l=l, v=v, ps_modT=ps_modT: E.tensor_tensor(
                            out=modT[l][:, v, :], in0=ps_modT[:, 0:32].rearrange("p (j v) -> p v j", v=2)[:, v, :],
                            in1=adab_sb[:, l, 0:16], op=ALU.add), r=[ps_modT, adab_sb], w=[modT[l]])
                    op(DVE, lambda E, l=l: E.tensor_scalar_add(out=modT[l][:, :, 8:16], in0=modT[l][:, :, 8:16],
                                                               scalar1=1.0), r=[modT[l]], w=[modT[l]])
            else:
                for v in range(2):
                    ps_mod = npb()
                    for kc in range(KC):
                        op(PE, lambda E, kc=kc, v=v, wb=wb, ps_mod=ps_mod: E.matmul(
                            ps_mod[:, :], lhsT=sc_rep[:, kc, v, :], rhs=wb[:, kc, :],
                            start=(kc == 0), stop=(kc == KC - 1)), r=[wb, sc_rep], w=[ps_mod])
                    c0 = (nch - 4) * 512
                    op(DVE, lambda E, l=l, v=v, c0=c0, ps_mod=ps_mod: E.scalar_tensor_tensor(
                        out=gate1[l][:, v, c0:c0 + 512], in0=ps_mod[:, :], scalar=1.0,
                        in1=gate_b[:, c0:c0 + 512], op0=ALU.add, op1=ALU.add),
                       r=[ps_mod, gate_b], w=[gate1[l]])
        if l == 0:
            for nch in range(6):
                dma(POOL, w_in_sb[:, :, nch * 512:(nch + 1) * 512],
                    w_in[:, nch * 512:(nch + 1) * 512].rearrange("(kc p) n -> p kc n", p=128),
                    w=[w_in_sb.sub(nch)], sem=f"w_in{nch}")
            dma(POOL, dftc[:], dftc_d[:, :], w=[dftc], sem="c_dftc")
            dma(POOL, wf_sb[:], w_fnet.rearrange("g c e -> c g e"), w=[wf_sb], sem="c_wf")
            dma(POOL, wout0[:], w_out0.rearrange("(kc p) n -> p kc n", p=128), w=[wout0], sem="wout0")
            dma(POOL, w_in_bf[:, :], w_in[:, :], w=[w_in_bf], sem="wcast0")
            for i5 in range(5):
                dma(POOL, wbf_d[i5][:, :], (wout1_d if i5 == 4 else rkvz_d[i5]), w=[wbf_d[i5]], sem=f"wcast{1 + i5}")
    for half in range(2):
        p = npb()
        for gg in range(2):
            g = half * 2 + gg
            for cs in range(2):
                op(PE, lambda E, p=p, g=g, gg=gg, cs=cs: E.matmul(
                    p[:, gg * 256 + cs * 128:gg * 256 + (cs + 1) * 128], lhsT=dftc[:, cs * 128:(cs + 1) * 128],
                    rhs=wf_sb[:, g, :], start=True, stop=True), r=[dftc, wf_sb], w=[p])
        op(ACT, lambda E, p=p, half=half: E.activation(
            out=csw[:, half * 2:half * 2 + 2, :].rearrange("p g e -> p (g e)"), in_=p[:, :], func=AF.Identity),
           r=[p], w=[csw])

    if stage == 0:
        dump("modT0", modT[0][:], [128, 2, 16], modT[0])
        dump("gate1_0", gate1[0][:], [128, 2, D], gate1[0])
        dump("gate1_1", gate1[1][:], [128, 2, D], gate1[1])
        return kb.finish(outs_final)
    kb.barrier()
    kb.release(mark_ph)

    lctr = [0]

    def ln_stats(x, st, m, eps_col=0):
        for h in range(2):
            op(DVE, lambda E, h=h: E.bn_stats(out=st[:, h, :], in_=x[:, h * 512:(h + 1) * 512]), r=[x], w=[st])
        op(DVE, lambda E: E.bn_aggr(out=m[:, 0:2], in_=st[:]), r=[st], w=[m])
        op(ACT, lambda E: E.activation(out=m[:, 2:3], in_=m[:, 1:2], func=AF.Sqrt, bias=consts[:, eps_col:eps_col + 1],
                                       scale=1.0), r=[m, consts], w=[m])
        op(DVE, lambda E: E.reciprocal(out=m[:, 2:3], in_=m[:, 2:3]), r=[m], w=[m])
        op(DVE, lambda E: E.scalar_tensor_tensor(out=m[:, 3:4], in0=m[:, 0:1], scalar=-1.0, in1=m[:, 2:3],
                                                 op0=ALU.mult, op1=ALU.mult), r=[m], w=[m])

    def ln_transpose(x_ap, x_deps, mod, v, uT, col0, utrk):
        i = lctr[0] % 2
        lctr[0] += 1
        x, xnb, st, m = xt[i], xn[i], stats[i], mv[i]
        pst = nptr()
        dma(SP, x[:], x_ap, r=x_deps, w=[x], sem=x.name)
        ln_stats(x, st, m)
        op(ACT, lambda E: E.activation(out=xnb[:], in_=x[:], func=AF.Identity, scale=m[:, 2:3], bias=m[:, 3:4]),
           r=[x, m], w=[xnb])
        for kc in range(KC):
            op(PE, lambda E, kc=kc: E.transpose(pst[:, kc * 128:(kc + 1) * 128], xnb[:, kc * 128:(kc + 1) * 128],
                                                ident[:]), r=[xnb, ident], w=[pst])
        for kc in range(KC):
            op(DVE, lambda E, kc=kc: E.tensor_scalar(
                out=uT[:, kc, col0:col0 + 128], in0=pst[:, kc * 128:(kc + 1) * 128],
                scalar1=mod[:, v, 8 + kc:9 + kc], scalar2=mod[:, v, kc:kc + 1], op0=ALU.mult, op1=ALU.add),
               r=[pst, mod], w=[utrk])

    uT_p = sb("uT_p", [128, KC, 1024], BF16)
    ocat = uT_p
    vtok_p = sb("vtok_p", [128, 8, 512], BF16)
    aT = sb("aT", [128, 4, 1024], BF16)
    sza = sb("sza", [128, 4, 1024], BF16)
    qT = sb("qT", [128, 4, 1024], BF16)
    kT = sb("kT", [128, 4, 1024], BF16)
    szb = sb("szb", [128, 4, 1024], BF16)
    A1 = sb("A1", [128, 8, 4, 256], BF16)
    dftl = sb("dftl_sb", [128, 2, 512], BF16)
    pTs = [sb(f"pT{i}", [128, 512], BF16) for i in range(2)]
    recs = [sb(f"rec{i}", [128, 256], F32) for i in range(2)]
    dma(POOL, dftl[:], dftl_d.rearrange("(lt p) m -> p lt m", p=128), w=[dftl], sem="const3")
    for t in range(8):
        ln_transpose(xp[t * 128:(t + 1) * 128, :], [], modT[0], 0, uT_p, t * 128, uT_p.sub(t // 4))

    n_kv = 0
    for t in range(8):
        for which, c0, od in ((0, 1536, o_newk), (1, 2048, o_newv)):
            pa = npb()
            ko = kv_out[n_kv % 2]
            n_kv += 1
            for kc in range(KC):
                op(PE, lambda E, kc=kc, t=t, c0=c0, pa=pa: E.matmul(
                    pa[:, :], lhsT=uT_p[:, kc, t * 128:(t + 1) * 128], rhs=w_in_sb[:, kc, c0:c0 + 512],
                    start=(kc == 0), stop=(kc == KC - 1)),
                   r=[uT_p.sub(t // 4), w_in_sb.sub(c0 // 512)], w=[pa])
            op(ACT, lambda E, pa=pa, ko=ko: E.activation(out=ko[:], in_=pa[:, :], func=AF.Identity), r=[pa], w=[ko])
            if which == 1:
                op(POOL, lambda E, ko=ko, t=t: E.tensor_copy(out=vtok_p[:, t, :], in_=ko[:]),
                   r=[ko], w=[vtok_p.sub(t)])
            dma(SP, od[t * 128:(t + 1) * 128, :], ko[:], r=[ko], sem=ko.name)

    for blk in range(2):
        c0 = blk * 512
        for j in list(range(0, 16)) + list(range(20, 24)):
            p = npb()
            for kc in range(KC):
                op(PE, lambda E, kc=kc, j=j, c0=c0, p=p: E.matmul(
                    p[:, :], lhsT=w_in_sb[:, kc, j * 128:(j + 1) * 128], rhs=uT_p[:, kc, c0:c0 + 512],
                    start=(kc == 0), stop=(kc == KC - 1)), r=[w_in_sb.sub(j // 4), uT_p.sub(blk)], w=[p])
            grp, idx = j // 4, j % 4
            if grp == 0:
                op(DVE, lambda E, p=p, idx=idx, c0=c0: E.tensor_copy(out=aT[:, idx, c0:c0 + 512], in_=p[:, :]),
                   r=[p], w=[aT.sub(blk)])
            elif grp == 1:
                op(ACT, lambda E, p=p, idx=idx, c0=c0: E.activation(out=sza[:, idx, c0:c0 + 512], in_=p[:, :], func=AF.Silu),
                   r=[p], w=[sza.sub(blk)])
            elif grp == 2:
                op(DVE, lambda E, p=p, idx=idx, c0=c0: E.tensor_scalar_mul(out=qT[:, idx, c0:c0 + 512], in0=p[:, :],
                                                                            scalar1=0.125), r=[p], w=[qT.sub(blk)])
            elif grp == 3:
                op(DVE, lambda E, p=p, idx=idx, c0=c0: E.tensor_copy(out=kT[:, idx, c0:c0 + 512], in_=p[:, :]),
                   r=[p], w=[kT.sub(blk)])
            else:
                op(ACT, lambda E, p=p, idx=idx, c0=c0: E.activation(out=szb[:, idx, c0:c0 + 512], in_=p[:, :], func=AF.Silu),
                   r=[p], w=[szb.sub(blk)])

    for t in range(8):
        for half in range(2):
            p = npb()
            for gg in range(2):
                g = half * 2 + gg
                op(PE, lambda E, p=p, g=g, gg=gg, t=t: E.matmul(
                    p[:, gg * 256:(gg + 1) * 256], lhsT=aT[:, g, t * 128:(t + 1) * 128], rhs=csw[:, g, :],
                    start=True, stop=True), r=[aT.sub(t // 4), csw], w=[p])
            eng = ACT if half == 0 else DVE
            if eng == ACT:
                op(ACT, lambda E, p=p, t=t, half=half: E.activation(
                    out=A1[:, t, half * 2:half * 2 + 2, :].rearrange("p g e -> p (g e)"), in_=p[:, :], func=AF.Identity),
                   r=[p], w=[A1.sub(t)])
            else:
                op(DVE, lambda E, p=p, t=t, half=half: E.tensor_copy(
                    out=A1[:, t, half * 2:half * 2 + 2, :].rearrange("p g e -> p (g e)"), in_=p[:, :]),
                   r=[p], w=[A1.sub(t)])
    for s in range(4):
        for g in range(4):
            p = npb()
            n = 0
            for lt in range(2):
                t = 2 * s + lt
                for cs in range(2):
                    op(PE, lambda E, p=p, t=t, g=g, lt=lt, cs=cs, n=n: E.matmul(
                        p[:, 0:256], lhsT=A1[:, t, g, cs * 128:(cs + 1) * 128], rhs=dftl[:, lt, cs * 256:(cs + 1) * 256],
                        start=(n == 0), stop=(n == 3)), r=[A1.sub(t), dftl], w=[p])
                    n += 1
            op(DVE, lambda E, p=p, g=g, s=s: E.tensor_tensor(
                out=ocat[:, g, s * 256:(s + 1) * 256], in0=p[:, 0:256], in1=sza[:, g, s * 256:(s + 1) * 256], op=ALU.mult),
               r=[p, sza.sub(s // 2)], w=[ocat.sub(s // 2)])

    na = 0
    for s in range(4):
        q0 = s * 256
        for hp in range(4):
            po = npb()
            for e2 in range(2):
                h = 2 * hp + e2
                lo = 64 * e2
                pss = npb()
                pT = pTs[na % 2]
                na += 1
                for kc2 in range(2):
                    op(PE, lambda E, pss=pss, kc2=kc2, lo=lo, hp=hp, q0=q0: E.matmul(
                        pss[:, kc2 * 256:(kc2 + 1) * 256], lhsT=kT[lo:lo + 64, hp, q0 + kc2 * 128:q0 + (kc2 + 1) * 128],
                        rhs=qT[lo:lo + 64, hp, q0:q0 + 256], start=True, stop=True),
                       r=[kT.sub(s // 2), qT.sub(s // 2)], w=[pss])
                op(ACT, lambda E, pss=pss, pT=pT: E.activation(out=pT[:], in_=pss[:, :], func=AF.Exp), r=[pss], w=[pT])
                for kc2 in range(2):
                    op(PE, lambda E, po=po, kc2=kc2, lo=lo, h=h, s=s, pT=pT: E.matmul(
                        po[lo:lo + 64, 0:256], lhsT=vtok_p[:, 2 * s + kc2, h * 64:(h + 1) * 64],
                        rhs=pT[:, kc2 * 256:(kc2 + 1) * 256], start=(kc2 == 0), stop=(kc2 == 1)),
                       r=[vtok_p.sub(2 * s + kc2), pT], w=[po])
                for kc2 in range(2):
                    op(PE, lambda E, po=po, kc2=kc2, lo=lo, pT=pT: E.matmul(
                        po[lo:lo + 64, 256:512], lhsT=ones_bf[:, 0:64],
                        rhs=pT[:, kc2 * 256:(kc2 + 1) * 256], start=(kc2 == 0), stop=(kc2 == 1)),
                       r=[ones_bf, pT], w=[po])
            rec = recs[(s * 4 + hp) % 2]
            op(DVE, lambda E, po=po, rec=rec: E.reciprocal(out=rec[:], in_=po[:, 256:512]), r=[po], w=[rec])
            op(POOL, lambda E, rec=rec, hp=hp, q0=q0: E.tensor_mul(out=rec[:], in0=rec[:], in1=szb[:, hp, q0:q0 + 256]),
               r=[rec, szb.sub(s // 2)], w=[rec])
            op(DVE, lambda E, po=po, rec=rec, hp=hp, q0=q0: E.tensor_tensor(
                out=ocat[:, 4 + hp, q0:q0 + 256], in0=po[:, 0:256], in1=rec[:], op=ALU.mult),
               r=[po, rec], w=[ocat.sub(s // 2)])

    if stage == 2:
        x1d = dout("dbg_x1", [2048, D])
    else:
        x1d = kb.dram("x1d", [2048, D], F32)

    def w0fn(nh):
        return wout0, wout0[:, :, nh * 512:(nh + 1) * 512]

    def out_proj_ln(t_glob, x_ap, oc, col0, octrk, wfn, layer, v):
        i = lctr[0] % 2
        lctr[0] += 1
        x, st, m, y = xt[i], stats[i], mv[i], ybuf[i]
        dma(SP, x[:], x_ap, w=[x], sem=x.name)
        for nh in range(2):
            p = npb()
            wt_, wap_ = wfn(nh)
            for kc in range(KC):
                op(PE, lambda E, kc=kc, nh=nh, p=p, wap_=wap_: E.matmul(
                    p[:, :], lhsT=oc[:, kc, col0:col0 + 128], rhs=wap_[:, kc, :],
                    start=(kc == 0), stop=(kc == KC - 1)), r=[octrk, wt_], w=[p])
            op(DVE, lambda E, nh=nh, p=p: E.tensor_tensor(
                out=y[:, nh * 512:(nh + 1) * 512], in0=p[:, :], in1=gate1[layer][:, v, nh * 512:(nh + 1) * 512],
                op=ALU.mult), r=[p, gate1[layer]], w=[y])
        op(DVE, lambda E: E.scalar_tensor_tensor(out=y[:], in0=x[:], scalar=ALPHA, in1=y[:], op0=ALU.mult, op1=ALU.add),
           r=[x, y], w=[y])
        ln_stats(y, st, m)
        op(ACT, lambda E: E.activation(out=y[:], in_=y[:], func=AF.Identity, scale=m[:, 2:3], bias=m[:, 3:4]),
           r=[y, m], w=[y])
        op(POOL, lambda E: E.tensor_mul(out=y[:], in0=y[:], in1=lng[:]), r=[y, lng], w=[y])
        op(POOL, lambda E: E.tensor_add(out=y[:], in0=y[:], in1=lnb[:]), r=[y, lnb], w=[y])
        return y

    for t in range(8):
        y = out_proj_ln(t, xp[t * 128:(t + 1) * 128, :], ocat, t * 128, ocat.sub(t // 4), w0fn, 0, 0)
        dma(SP, x1d[t * 128:(t + 1) * 128, :], y[:], r=[y], w=[x1d.sub(t)], sem=y.name)
    kb.barrier()
    kb.release(mark_l0)
    GROUPS = [[0, 1, 2, 3], [4, 5, 6, 7]]
    gin = [kb.dram(f"gin{i}", [512, 1024], BF16) for i in range(2)]
    gout = [kb.dram(f"gout{i}", [2048, 1024], BF16) for i in range(2)]
    ocs = sb("ocs", [128, KC, 1024], BF16)
    sza_s = sb("sza_s", [128, 4, 1024], BF16)
    szb_s = sb("szb_s", [128, 4, 1024], BF16)
    qTa = sb("qTa", [128, 8, 1024], BF16)
    kTa = sb("kTa", [128, 8, 1536], BF16)
    vwin = sb("vwin", [128, 12, 512], BF16)
    mark_s1 = kb.mark()
    wch = [sb(f"wch{i}", [128, KC, 512], BF16) for i in range(2)]
    uT_s = sb("uT_s", [128, KC, 1536], BF16)
    aTb = sb("aTb", [128, 4, 512], BF16)
    a1st = [sb(f"a1st{i}", [128, 1024], BF16) for i in range(2)]
    for h in range(8):
        dma(POOL, qTa[64:88, h, :], qmask_d[:, :], w=[qTa.sub("aug")], sem="aug_q")
        dma(POOL, kTa[64:88, h, :], rowoh_d[:, :], w=[kTa.sub("aug")], sem="aug_k")
    for t in range(12):
        ln_transpose(xs_win[t * 128:(t + 1) * 128, :], [], modT[0], 1, uT_s, t * 128, uT_s.sub(t // 4))
    SCUT = os.environ.get('SCUT', '')
    if SCUT == 'ln':
        return kb.finish(outs_final)
    wcc = [0]

    def load_wch(nch):
        wb = wch[wcc[0] % 2]
        wcc[0] += 1
        dma(SP, wb[:], w_in_bf[:, nch * 512:(nch + 1) * 512].rearrange("(kc p) n -> p kc n", p=128), r=[w_in_bf], w=[wb],
            sem=wb.name)
        return wb

    def fm4(wb, wblk, evac):
        for jj in range(4):
            p = npb()
            for kc in range(KC):
                op(PE, lambda E, kc=kc, jj=jj, p=p: E.matmul(
                    p[:, :], lhsT=wb[:, kc, jj * 128:(jj + 1) * 128], rhs=uT_s[:, kc, wblk * 512:(wblk + 1) * 512],
                    start=(kc == 0), stop=(kc == KC - 1)), r=[wb, uT_s.sub(wblk)], w=[p])
            evac(jj, p)

    wb = load_wch(0)
    na1 = 0
    for blk in range(2):
        for jj in range(4):
            p = npb()
            for kc in range(KC):
                op(PE, lambda E, kc=kc, jj=jj, p=p, blk=blk, wb=wb: E.matmul(
                    p[:, :], lhsT=wb[:, kc, jj * 128:(jj + 1) * 128], rhs=uT_s[:, kc, 256 + blk * 512:256 + (blk + 1) * 512],
                    start=(kc == 0), stop=(kc == KC - 1)), r=[wb, uT_s.sub(0), uT_s.sub(1), uT_s.sub(2)], w=[p])
            op(DVE, lambda E, jj=jj, p=p: E.tensor_copy(out=aTb[:, jj, :], in_=p[:, :]), r=[p], w=[aTb])
        for tt_ in range(4):
            stg = a1st[na1 % 2]
            na1 += 1
            for half in range(2):
                p = npb()
                for gg in range(2):
                    g = half * 2 + gg
                    op(PE, lambda E, p=p, g=g, gg=gg, tt_=tt_: E.matmul(
                        p[:, gg * 256:(gg + 1) * 256], lhsT=aTb[:, g, tt_ * 128:(tt_ + 1) * 128], rhs=csw[:, g, :],
                        start=True, stop=True), r=[aTb, csw], w=[p])
                op(ACT, lambda E, p=p, half=half, stg=stg: E.activation(out=stg[:, half * 512:(half + 1) * 512], in_=p[:, :],
                                                                        func=AF.Identity), r=[p], w=[stg])
            row0 = tt_ * 128
            dma(SP, gin[blk][row0:row0 + 128, :], stg[:], r=[stg], w=[gin[blk].sub(row0)], sem=stg.name)
        kb.collective("AllGather", ALU.bypass, GROUPS, gin[blk][:, :], gout[blk][:, :],
                      r=[gin[blk].sub(r0) for r0 in range(0, 512, 128)] + ([gout[0]] if blk else []),
                      w=[gout[blk]], sem=f"ag1_{blk}")

    if SCUT == 'a1':
        dump('aTb', aTb[:], [128, 4, 512], [aTb])
        dump('csw', csw[:], [128, 4, 256], [csw])
        dump('gin0', gin[0][:, :], [512, 1024], [gin[0].sub(r0) for r0 in range(0, 512, 128)])
        return kb.finish(outs_final)

    def own_blocks(fn):
        for blk in range(2):
            fn(blk)

    wb = load_wch(1)
    for blk in range(2):
        for jj in range(4):
            p = npb()
            for kc in range(KC):
                op(PE, lambda E, kc=kc, jj=jj, p=p, blk=blk, wb=wb: E.matmul(
                    p[:, :], lhsT=wb[:, kc, jj * 128:(jj + 1) * 128], rhs=uT_s[:, kc, 256 + blk * 512:256 + (blk + 1) * 512],
                    start=(kc == 0), stop=(kc == KC - 1)), r=[wb, uT_s.sub(0), uT_s.sub(1), uT_s.sub(2)], w=[p])
            op(ACT, lambda E, jj=jj, p=p, blk=blk: E.activation(out=sza_s[:, jj, blk * 512:(blk + 1) * 512], in_=p[:, :],
                                                                func=AF.Silu), r=[p], w=[sza_s])
    wb = load_wch(2)
    for blk in range(2):
        for h in range(8):
            p = npb()
            for kc in range(KC):
                op(PE, lambda E, kc=kc, h=h, p=p, blk=blk, wb=wb: E.matmul(
                    p[0:64, :], lhsT=wb[:, kc, h * 64:(h + 1) * 64], rhs=uT_s[:, kc, 256 + blk * 512:256 + (blk + 1) * 512],
                    start=(kc == 0), stop=(kc == KC - 1)), r=[wb, uT_s.sub(0), uT_s.sub(1), uT_s.sub(2)], w=[p])
            op(DVE, lambda E, h=h, p=p, blk=blk: E.tensor_scalar_mul(out=qTa[0:64, h, blk * 512:(blk + 1) * 512],
                                                                      in0=p[0:64, :], scalar1=0.125), r=[p], w=[qTa.sub(h)])
    wb = load_wch(3)
    for wblk in range(3):
        for h in range(8):
            p = npb()
            for kc in range(KC):
                op(PE, lambda E, kc=kc, h=h, p=p, wblk=wblk, wb=wb: E.matmul(
                    p[0:64, :], lhsT=wb[:, kc, h * 64:(h + 1) * 64], rhs=uT_s[:, kc, wblk * 512:(wblk + 1) * 512],
                    start=(kc == 0), stop=(kc == KC - 1)), r=[wb, uT_s.sub(wblk)], w=[p])
            op(ACT, lambda E, h=h, p=p, wblk=wblk: E.activation(out=kTa[0:64, h, wblk * 512:(wblk + 1) * 512], in_=p[0:64, :],
                                                                func=AF.Identity), r=[p], w=[kTa.sub(h)])
    wb = load_wch(4)
    for t in range(12):
        p = npb()
        for kc in range(KC):
            op(PE, lambda E, kc=kc, t=t, p=p, wb=wb: E.matmul(
                p[:, :], lhsT=uT_s[:, kc, t * 128:(t + 1) * 128], rhs=wb[:, kc, :],
                start=(kc == 0), stop=(kc == KC - 1)), r=[wb, uT_s.sub(t // 4)], w=[p])
        op(ACT if t % 2 else DVE, (lambda E, t=t, p=p: E.activation(out=vwin[:, t, :], in_=p[:, :], func=AF.Identity)) if t % 2
           else (lambda E, t=t, p=p: E.tensor_copy(out=vwin[:, t, :], in_=p[:, :])), r=[p], w=[vwin])
    wb = load_wch(5)
    for blk in range(2):
        for jj in range(4):
            p = npb()
            for kc in range(KC):
                op(PE, lambda E, kc=kc, jj=jj, p=p, blk=blk, wb=wb: E.matmul(
                    p[:, :], lhsT=wb[:, kc, jj * 128:(jj + 1) * 128], rhs=uT_s[:, kc, 256 + blk * 512:256 + (blk + 1) * 512],
                    start=(kc == 0), stop=(kc == KC - 1)), r=[wb, uT_s.sub(0), uT_s.sub(1), uT_s.sub(2)], w=[p])
            op(ACT, lambda E, jj=jj, p=p, blk=blk: E.activation(out=szb_s[:, jj, blk * 512:(blk + 1) * 512], in_=p[:, :],
                                                                func=AF.Silu), r=[p], w=[szb_s])
    if SCUT == 'proj':
        dump('gin0', gin[0][:, :], [512, 1024], [gin[0].sub(r0) for r0 in range(0, 512, 128)])
        dump('csw', csw[:], [128, 4, 256], [csw])
        dump('aTb', aTb[:], [128, 4, 512], [aTb])
        return kb.finish(outs_final)
    kb.barrier()
    kb.release(mark_s1)

    BBt = [sb(f"BBt{i}", [128, 30 * 64], BF16) for i in range(2)]
    kc_tok = sb("kc_tok", [128, 4, 512], BF16)
    vctx = sb("vctx", [128, 4, 512], BF16)
    kcT = sb("kcT", [64, 8, 512], BF16)
    pTn = [sb(f"pTn{i}", [128, 512], BF16) for i in range(3)]
    recn = sb("recn", [128, 512], F32)
    a1g = [sb(f"a1g{i}", [128, 4, 256], BF16) for i in range(2)]
    dfs = [sb(f"dfs{i}", [128, 2, 512], BF16) for i in range(2)]
    dma(POOL, kc_tok[:], ck_d.rearrange("(t p) n -> p t n", p=128), w=[kc_tok], sem="ctxk")
    dma(POOL, vctx[:], cv_d.rearrange("(t p) n -> p t n", p=128), w=[vctx], sem="ctxv")
    for h in range(8):
        pt = nptr()
        for t4 in range(4):
            op(PE, lambda E, pt=pt, t4=t4, h=h: E.transpose(pt[0:64, t4 * 128:(t4 + 1) * 128], kc_tok[:, t4, h * 64:(h + 1) * 64],
                                                            ident[:]), r=[kc_tok, ident], w=[pt])
        op(DVE, lambda E, pt=pt, h=h: E.tensor_copy(out=kcT[:, h, :], in_=pt[0:64, 0:512]), r=[pt], w=[kcT])
    npT = 0
    for hp in range(4):
        bbs = []
        for h2 in range(2):
            bt = BBt[h2]
            dma(POOL, bt[:], bb_d[2 * hp + h2], w=[bt], sem=bt.name)
            bbs.append(bt)
        for j in range(2):
            po, pd = pbs[0], pbs[1]
            for h2 in range(2):
                h = 2 * hp + h2
                lo = 64 * h2
                units = [("loc", i) for i in ((range(0, 10)) if j == 0 else range(2, 12))] + [("ctx", t4) for t4 in range(4)]
                for ui, (kind, i) in enumerate(units):
                    pss = pbs[2 + (npT % 4)]
                    pT = pTn[npT % 3]
                    npT += 1
                    if kind == "loc":
                        idx0 = 18 - 2 * i + 8 * j
                        op(PE, lambda E, pss=pss, h=h, i=i, j=j: E.matmul(
                            pss[:, :], lhsT=kTa[0:88, h, i * 128:(i + 1) * 128], rhs=qTa[0:88, h, j * 512:(j + 1) * 512],
                            start=True, stop=False), r=[kTa.sub(h), kTa.sub("aug"), qTa.sub(h), qTa.sub("aug")], w=[pss])
                        op(PE, lambda E, pss=pss, h2=h2, idx0=idx0: E.matmul(
                            pss[:, :], lhsT=ident[:], rhs=bbs[h2][:, idx0 * 64:idx0 * 64 + 512], start=False, stop=True),
                           r=[ident, bbs[h2]], w=[pss])
                        vl = vwin[:, i, h * 64:(h + 1) * 64]
                        vtr = vwin
                    else:
                        op(PE, lambda E, pss=pss, h=h, i=i, j=j: E.matmul(
                            pss[:, :], lhsT=kcT[0:64, h, i * 128:(i + 1) * 128], rhs=qTa[0:64, h, j * 512:(j + 1) * 512],
                            start=True, stop=True), r=[kcT, qTa.sub(h)], w=[pss])
                        vl = vctx[:, i, h * 64:(h + 1) * 64]
                        vtr = vctx
                    op(ACT, lambda E, pss=pss, pT=pT: E.activation(out=pT[:], in_=pss[:, :], func=AF.Exp), r=[pss], w=[pT])
                    op(PE, lambda E, po=po, lo=lo, vl=vl, pT=pT, ui=ui: E.matmul(
                        po[lo:lo + 64, :], lhsT=vl, rhs=pT[:], start=(ui == 0), stop=(ui == len(units) - 1)),
                       r=[vtr, pT], w=[po])
                    op(PE, lambda E, pd=pd, lo=lo, pT=pT, ui=ui: E.matmul(
                        pd[lo:lo + 64, :], lhsT=ones_bf[:, 0:64], rhs=pT[:], start=(ui == 0), stop=(ui == len(units) - 1)),
                       r=[ones_bf, pT], w=[pd])
            op(DVE, lambda E, pd=pd: E.reciprocal(out=recn[:], in_=pd[:, :]), r=[pd], w=[recn])
            op(POOL, lambda E, hp=hp, j=j: E.tensor_mul(out=recn[:], in0=recn[:], in1=szb_s[:, hp, j * 512:(j + 1) * 512]),
               r=[recn, szb_s], w=[recn])
            op(DVE, lambda E, po=po, hp=hp, j=j: E.tensor_tensor(out=ocs[:, 4 + hp, j * 512:(j + 1) * 512], in0=po[:, :], in1=recn[:],
                                                                 op=ALU.mult), r=[po, recn], w=[ocs.sub(j)])

    if SCUT == 'na':
        dump('gin0', gin[0][:, :], [512, 1024], [gin[0].sub(r0) for r0 in range(0, 512, 128)])
        dump('csw', csw[:], [128, 4, 256], [csw])
        return kb.finish(outs_final)
    nst = 0
    for mb in range(2):
        for lt in range(32):
            ag, df = a1g[nst % 2], dfs[nst % 2]
            nst += 1
            gsrc = gout[(lt % 8) // 4]
            grow = (lt // 8) * 512 + (lt % 4) * 128
            dma(SP, ag[:], gsrc[grow:grow + 128, :].rearrange("p (g e) -> p g e", g=4), r=[gsrc], w=[ag], sem=ag.name)
            dma(SP, df[:], dfts_d[lt * 128:(lt + 1) * 128, :, mb * 512:(mb + 1) * 512], w=[df], sem=df.name)
            for g in range(4):
                for cs in range(2):
                    op(PE, lambda E, g=g, cs=cs, ag=ag, df=df, lt=lt: E.matmul(
                        pbs[g][:, :], lhsT=ag[:, g, cs * 128:(cs + 1) * 128], rhs=df[:, cs, :],
                        start=(lt == 0 and cs == 0), stop=(lt == 31 and cs == 1)), r=[ag, df], w=[pbs[g]])
        for g in range(4):
            op(DVE, lambda E, g=g, mb=mb: E.tensor_tensor(out=ocs[:, g, mb * 512:(mb + 1) * 512], in0=pbs[g][:, :],
                                                          in1=sza_s[:, g, mb * 512:(mb + 1) * 512], op=ALU.mult),
               r=[pbs[g], sza_s], w=[ocs.sub(mb)])

    if SCUT == 'fnet':
        return kb.finish(outs_final)
    if stage == 2:
        dump('ocs', ocs[:], [128, KC, 1024], [ocs.sub(0), ocs.sub(1)])
        dump('gout0', gout[0][:, :], [2048, 1024], [gout[0]])
        dump('gin0', gin[0][:, :], [512, 1024], [gin[0].sub(r0) for r0 in range(0, 512, 128)])
        dump('sza', sza_s[:], [128, 4, 1024], [sza_s])
    for t in range(8):
        y = out_proj_ln(t, xs_win[256 + t * 128:256 + (t + 1) * 128, :], ocs, t * 128, ocs.sub(t // 4), w0fn, 0, 1)
        dma(SP, x1d[1024 + t * 128:1024 + (t + 1) * 128, :], y[:], r=[y], w=[x1d.sub(8 + t)], sem=y.name)

    if stage == 2:
        return kb.finish(outs_final)

    kb.barrier()
    kb.release(mark_base)
    NLEV = 2
    CUT = os.environ.get('L1CUT', '')
    NBLK = int(os.environ.get('L1NBLK', '4'))
    wbuf = [sb(f"wbuf{i}", [128, KC, 512], BF16) for i in range(2)]
    w1cat = sb("w1cat", [128, KC, 128], BF16)
    a1cat = sb("a1cat", [128, KC, 128], BF16)
    g1s = sb("g1s", [128, KC, 128], BF16)
    w2cat = sb("w2cat", [128, D], BF16)
    a2cat = sb("a2cat", [128, D], BF16)
    g2s = sb("g2s", [128, D], BF16)
    w0b = sb("w0b", [128, 2, D], F32)
    vecs = sb("vecs", [128, 16, KC], F32)
    tri = sb("tri", [128, 2, 128], F32)
    mask4 = sb("mask4", [128, 2, 512], BF16)
    namask = sb("namask", [128, 2, 256], BF16)
    bo = sb("bo", [128, 2, 128], BF16)
    identf = sb("identf", [128, 128], F32)
    dma(SP, vecs[:, 0:14, :], vecs_d[:, :, :], w=[vecs], sem="c_vecs")
    dma(SP, tri[:], tri_d[:, :, :], w=[tri], sem="c_tri")
    dma(POOL, mask4[:], mask4_d[:, :, :], w=[mask4], sem="c_mask4")
    dma(POOL, namask[:], namask_d[:, :, :], w=[namask], sem="c_namask")
    dma(POOL, bo[:], bo_d[:, :, :], w=[bo], sem="c_bo")
    for e in range(2):
        dma(POOL, w1cat[:, :, e * 64:(e + 1) * 64], w1_d[e].rearrange("(kc p) r -> p kc r", p=128), w=[w1cat], sem="c_w1")
        dma(POOL, a1cat[:, :, e * 64:(e + 1) * 64], a1_d[e].rearrange("(kc p) r -> p kc r", p=128), w=[a1cat], sem="c_a1")
        dma(POOL, w2cat[64 * e:64 * e + 64, :], w2_d[e], w=[w2cat], sem="c_w2")
        dma(POOL, a2cat[64 * e:64 * e + 64, :], a2_d[e], w=[a2cat], sem="c_a2")
        bcast_row(w0b, w0b[:, e, :], w0_d[e:e + 1, :], D)
    dma(POOL, g1s[:], g1_d.rearrange("(kc p) r -> p kc r", p=128), w=[g1s], sem="c_g1")
    dma(POOL, g2s[:], g2_d[:, :], w=[g2s], sem="c_g2")
    bcast_row(lng, lng[:], post_g[1:2, :], D)
    bcast_row(lnb, lnb[:], post_b[1:2, :], D)
    op(POOL, lambda E: E.tensor_copy(out=identf[:], in_=ident[:]), r=[ident], w=[identf])
    V_MU, V_KK, V_KA, V_RK, V_LG, V_LB, V_A0, V_OMKA = 0, 6, 7, 8, 9, 10, 11, 13
    op(DVE, lambda E: E.tensor_scalar(out=vecs[:, V_OMKA, :], in0=vecs[:, V_KA, :], scalar1=-1.0, scalar2=1.0,
                                      op0=ALU.mult, op1=ALU.add), r=[vecs], w=[vecs])
    op(DVE, lambda E: E.tensor_scalar_mul(out=vecs[:, V_RK, :], in0=vecs[:, V_RK, :], scalar1=0.5), r=[vecs], w=[vecs])
    op(DVE, lambda E: E.tensor_scalar_mul(out=vecs[:, 14:16, :], in0=vecs[:, V_A0:V_A0 + 2, :], scalar1=-1.0), r=[vecs], w=[vecs])

    mark_blk = kb.mark()
    u1T_s = sb("u1T_s", [128, KC, 1026], BF16)
    u1T = [T(u1T_s.h[:, :, i * 258:(i + 1) * 258], f"u1T{i}") for i in range(2)]
    vext = [sb(f"vext{c}", [128, 16, 128], BF16) for c in range(2)]
    yst = [sb(f"yst{i}", [128, 2, 256], BF16) for i in range(2)]
    bonst = sb("bonst", [128, 256], BF16)
    ext_d = kb.dram("ext_d", [4, 2, 16, 64, 128], F32)
    yext_d = kb.dram("yext_d", [4, 2, 8, 128, 512], BF16)
    post_d = kb.dram("post_d", [4, 8, 2, 128, 256], BF16)
    dx = sb("dx", [128, KC, 256], BF16)
    xl = sb("xl", [128, KC, 256], BF16)
    hwT = sb("hwT", [128, 256], BF16)
    haT = sb("haT", [128, 256], BF16)
    hgT = sb("hgT", [128, 256], BF16)
    vtok = [sb(f"vtok{c}", [128, D], BF16) for c in range(2)]
    vT = sb("vT", [128, KC, 256], BF16)
    krawT = sb("krawT", [128, KC, 256], BF16)
    rT = sb("rT", [128, KC, 256], BF16)
    gzT = sb("gzT", [128, KC, 256], BF16)
    sig = [[sb(f"sig{c}{e}", [128, D], F32) for e in range(2)] for c in range(2)]
    oT = sb("oT", [128, KC, 256], BF16)
    a_t = [sb(f"a_t{e}", [128, 256], F32) for e in range(2)]
    kk0 = sb("kk0", [128, 256], F32)
    kap = sb("kap", [128, 256], F32)
    kd_t = [sb(f"kd_t{e}", [128, 256], F32) for e in range(2)]
    b_t = [sb(f"b_t{e}", [128, 256], F32) for e in range(2)]
    sqb = sb("sqb", [128, 256], BF16)
    rkr2 = [sb(f"rkr{i}", [128, 256], BF16) for i in range(2)]
    gtmp = kk0
    Einc = [sb(f"Einc{i}", [128, 130], F32) for i in range(4)]
    Enin = [sb(f"Enin{i}", [128, 128], F32) for i in range(4)]
    KR = [sb(f"KR{i}", [128, 2, 128], BF16) for i in range(4)]
    KBt = [sb(f"KB{i}", [128, 2, 128], BF16) for i in range(4)]
    KBbar2 = [sb(f"KBbar{e}", [128, 2, 128], BF16) for e in range(4)]
    KBtok = [sb(f"KBtok{i}", [128, 2, 128], BF16) for i in range(4)]
    AFM = [[sb(f"AFM{i}{h}", [128, 512], BF16) for h in range(2)] for i in range(4)]
    nA = [sb(f"nA{i}", [128, 2, 128], BF16) for i in range(4)]
    T02 = [sb(f"T0{e}", [128, 2, 128], BF16) for e in range(4)]
    BA22 = [sb(f"BA2{e}", [128, 2, 2, 128], BF16) for e in range(4)]
    T12 = [sb(f"T1{e}", [128, 2, 128], BF16) for e in range(4)]
    A42 = [sb(f"A4{e}", [128, 2, 128], BF16) for e in range(4)]
    TT = [sb(f"TT{i}", [128, 2, 128], BF16) for i in range(4)]
    Zb2 = [sb(f"Zb{e}", [128, 2, 128], BF16) for e in range(2)]
    nU2 = [sb(f"nU{e}", [128, 2, 128], BF16) for e in range(2)]
    Sf = [sb(f"Sf{e}", [128, 128], F32) for e in range(2)]
    Sb = [sb(f"Sb{e}", [128, 128], BF16) for e in range(2)]
    yT = sb("yT", [128, 256], F32)
    ybf = sb("ybf", [128, 256], BF16)
    yc = sb("yc", [128, 256], F32)
    sdv = sb("sdv", [128, 256], F32)
    st_out = sb("st_out", [64, 2, 64], F32)
    for i in range(4):
        op(POOL, lambda E, i=i: E.memset(Einc[i][:], 1.0), w=[Einc[i]])
    for i in range(2):
        op(POOL, lambda E, i=i: E.memset(u1T[i][:], 0.0), w=[u1T[i]])
        op(POOL, lambda E, i=i: E.memset(vext[i][:], 0.0), w=[vext[i]])

    wctr = [0]

    def load_wh(i, half):
        wb = wbuf[wctr[0] % 2]
        wctr[0] += 1
        dma(SP, wb[:], wbf_d[i][:, half * 512:(half + 1) * 512].rearrange("(kc p) n -> p kc n", p=128), r=[wbf_d[i]], w=[wb],
            sem=wb.name)
        return wb

    def fm_proj(wi, rhs_fn, rtrk, evac):
        for hw in range(2):
            wb = load_wh(wi, hw)
            for cp in (2 * hw, 2 * hw + 1):
                p = npb()
                for half in range(2):
                    ct = cp * 2 + half
                    for kc in range(KC):
                        op(PE, lambda E, p=p, half=half, ct=ct, kc=kc, wb=wb: E.matmul(
                            p[:, half * 256:(half + 1) * 256], lhsT=wb[:, kc, (ct % 4) * 128:(ct % 4 + 1) * 128], rhs=rhs_fn(kc),
                            start=(kc == 0), stop=(kc == KC - 1)), r=[wb] + rtrk, w=[p])
                evac(cp, p)

    def rwkv_block(blk, x1rows, x1deps, v, yout_rows, state_out, stacked=False):
        VW = 128 if stacked else 64
        if stacked:
            ub = T(u1T_s.h[:, :, blk * 256:blk * 256 + 258], "ubv")
            ub.trk = u1T_s.trk
        else:
            ub = u1T[blk % 2]
            for c in range(2):
                ln_transpose(x1rows[c * 128:(c + 1) * 128, :], x1deps[c], modT[1], v, ub, 1 + c * 128, ub)

        def vsrc(c, h):
            return vext[c][:, h, :] if stacked else vtok[c][:, h * 64:(h + 1) * 64]
        op(DVE, lambda E: E.tensor_tensor(out=dx[:], in0=ub[:, :, 0:256], in1=ub[:, :, 2:258], op=ALU.add),
           r=[ub], w=[dx])
        op(DVE, lambda E: E.scalar_tensor_tensor(out=dx[:], in0=dx[:], scalar=0.5, in1=ub[:, :, 1:257],
                                                 op0=ALU.mult, op1=ALU.subtract), r=[ub, dx], w=[dx])

        def lerp(i):
            for kc in range(KC):
                op(DVE, lambda E, kc=kc: E.scalar_tensor_tensor(
                    out=xl[:, kc, :], in0=dx[:, kc, :], scalar=vecs[:, V_MU + i, kc:kc + 1], in1=ub[:, kc, 1:257],
                    op0=ALU.mult, op1=ALU.add), r=[dx, ub, vecs], w=[xl])

        def hidden(wcat, hT, func):
            p = npb()
            for kc in range(KC):
                op(PE, lambda E, kc=kc: E.matmul(p[:, 0:256], lhsT=wcat[:, kc, :], rhs=xl[:, kc, :],
                                                 start=(kc == 0), stop=(kc == KC - 1)), r=[wcat, xl], w=[p])
            op(ACT, lambda E: E.activation(out=hT[:], in_=p[:, 0:256], func=func), r=[p], w=[hT])

        lerp(1)
        hidden(w1cat, hwT, AF.Tanh)
        lerp(4)
        hidden(a1cat, haT, AF.Identity)
        lerp(5)
        hidden(g1s, hgT, AF.Sigmoid)
        lerp(3)
        for nh in range(2):
            wv = load_wh(2, nh)
            for c in range(2):
                p = npb()
                for kc in range(KC):
                    op(PE, lambda E, kc=kc, c=c, nh=nh, p=p, wv=wv: E.matmul(
                        p[:, :], lhsT=xl[:, kc, c * 128:(c + 1) * 128], rhs=wv[:, kc, :],
                        start=(kc == 0), stop=(kc == KC - 1)), r=[xl, wv], w=[p])
                op(ACT, lambda E, c=c, nh=nh, p=p: E.activation(out=vtok[c][:, nh * 512:(nh + 1) * 512], in_=p[:, :],
                                                                func=AF.Identity), r=[p], w=[vtok[c]])
        for c in range(2):
            pt = nptr()
            for kc in range(KC):
                op(PE, lambda E, kc=kc, c=c, pt=pt: E.transpose(pt[:, kc * 128:(kc + 1) * 128],
                                                                vtok[c][:, kc * 128:(kc + 1) * 128], ident[:]),
                   r=[vtok[c], ident], w=[pt])
            op(DVE, lambda E, c=c, pt=pt: E.tensor_copy(out=vT[:, :, c * 128:(c + 1) * 128],
                                                        in_=pt[:, :].rearrange("p (k t) -> p k t", k=KC)),
               r=[pt], w=[vT])
            if stacked:
                op(POOL, lambda E, c=c: E.tensor_copy(out=vext[c][:, :, 64:128],
                                                      in_=vtok[c][:, :].rearrange("p (h d) -> p h d", h=16)),
                   r=[vtok[c]], w=[vext[c]])
        lerp(2)
        fm_proj(1, lambda kc: xl[:, kc, :], [xl],
                lambda cp, p: op(DVE, lambda E: E.tensor_copy(
                    out=krawT[:, 2 * cp:2 * cp + 2, :], in_=p[:, :].rearrange("p (k t) -> p k t", k=2)),
                    r=[p], w=[krawT]))
        lerp(0)
        fm_proj(0, lambda kc: xl[:, kc, :], [xl],
                lambda cp, p: op(ACT, lambda E: E.activation(
                    out=rT[:, 2 * cp:2 * cp + 2, :], in_=p[:, :].rearrange("p (k t) -> p k t", k=2), func=AF.Identity),
                    r=[p], w=[rT]))
        fm_proj(3, lambda kc: ub[:, kc, 1:257], [ub],
                lambda cp, p: op(ACT, lambda E: E.activation(
                    out=gzT[:, 2 * cp:2 * cp + 2, :], in_=p[:, :].rearrange("p (k t) -> p k t", k=2), func=AF.Silu),
                    r=[p], w=[gzT]))
        for c in range(2):
            for e in range(2):
                for nh in range(2):
                    p = npb()
                    op(PE, lambda E, c=c, e=e, nh=nh, p=p: E.matmul(
                        p[:, :], lhsT=hwT[64 * e:64 * e + 64, c * 128:(c + 1) * 128],
                        rhs=w2cat[64 * e:64 * e + 64, nh * 512:(nh + 1) * 512], start=True, stop=True),
                       r=[hwT, w2cat], w=[p])
                    op(DVE, lambda E, c=c, e=e, nh=nh, p=p: E.tensor_tensor(
                        out=sig[c][e][:, nh * 512:(nh + 1) * 512], in0=p[:, :], in1=w0b[:, e, nh * 512:(nh + 1) * 512],
                        op=ALU.add), r=[p, w0b], w=[sig[c][e]])
                op(ACT, lambda E, c=c, e=e: E.activation(out=sig[c][e][:], in_=sig[c][e][:], func=AF.Sigmoid),
                   r=[sig[c][e]], w=[sig[c][e]])

        if CUT == 'proj':
            return
        pair = [0]
        def ctgen(ct):
            cs_ = slice(ct * 128, (ct + 1) * 128)
            rkr = rkr2[ct % 2]
            p = npb()
            op(PE, lambda E, p=p, cs_=cs_: E.matmul(p[:, 0:256], lhsT=g2s[:, cs_], rhs=hgT[:], start=True, stop=True),
               r=[g2s, hgT], w=[p])
            op(DVE, lambda E, p=p, ct=ct: E.tensor_tensor(out=gzT[:, ct, :], in0=p[:, 0:256], in1=gzT[:, ct, :], op=ALU.mult),
               r=[p, gzT], w=[gzT])
            if CUT == 'c1':
                return
            for e in range(2):
                p = npb()
                op(PE, lambda E, p=p, e=e, cs_=cs_: E.matmul(
                    p[:, 0:256], lhsT=a2cat[64 * e:64 * e + 64, cs_], rhs=haT[64 * e:64 * e + 64, :],
                    start=True, stop=True), r=[a2cat, haT], w=[p])
                op(ACT, lambda E, p=p, e=e, ct=ct: E.activation(
                    out=a_t[e][:], in_=p[:, 0:256], func=AF.Exp,
                    bias=vecs[:, 14 + e, ct:ct + 1], scale=-1.0), r=[p, vecs], w=[a_t[e]])
                op(ACT, lambda E, e=e: E.activation(out=a_t[e][:], in_=a_t[e][:], func=AF.Ln, bias=consts[:, 3:4], scale=1.0),
                   r=[a_t[e], consts], w=[a_t[e]])
                op(ACT, lambda E, e=e: E.activation(out=a_t[e][:], in_=a_t[e][:], func=AF.Exp, scale=-1.0),
                   r=[a_t[e]], w=[a_t[e]])
            if CUT == 'c2':
                return
            op(DVE, lambda E, ct=ct: E.tensor_scalar_mul(out=kk0[:], in0=krawT[:, ct, :], scalar1=vecs[:, V_KK, ct:ct + 1]),
               r=[krawT, vecs], w=[kk0])
            op(POOL, lambda E: E.tensor_mul(out=sqb[:], in0=kk0[:], in1=kk0[:]), r=[kk0], w=[sqb])
            p = npb()
            op(PE, lambda E, p=p: E.matmul(p[:, 0:256], lhsT=bo[:, 0, :], rhs=sqb[:], start=True, stop=True),
               r=[bo, sqb], w=[p])
            op(ACT, lambda E, p=p: E.activation(out=kap[:], in_=p[:, 0:256], func=AF.Ln, bias=consts[:, 2:3], scale=1.0),
               r=[p, consts], w=[kap])
            op(ACT, lambda E: E.activation(out=kap[:], in_=kap[:], func=AF.Exp, scale=-0.5), r=[kap], w=[kap])
            op(POOL, lambda E: E.tensor_mul(out=kap[:], in0=kap[:], in1=kk0[:]), r=[kap, kk0], w=[kap])
            if CUT == 'c3':
                return
            for e in range(2):
                op(DVE, lambda E, e=e, ct=ct: E.tensor_scalar(
                    out=kd_t[e][:], in0=a_t[e][:], scalar1=vecs[:, V_KA, ct:ct + 1], scalar2=vecs[:, V_OMKA, ct:ct + 1],
                    op0=ALU.mult, op1=ALU.add), r=[a_t[e], vecs], w=[kd_t[e]])
                op(POOL, lambda E, e=e, ct=ct: E.tensor_mul(out=kd_t[e][:], in0=kd_t[e][:], in1=krawT[:, ct, :]),
                   r=[kd_t[e], krawT], w=[kd_t[e]])
                op(POOL, lambda E, e=e: E.tensor_mul(out=b_t[e][:], in0=kap[:], in1=a_t[e][:]), r=[kap, a_t[e]], w=[b_t[e]])
            if CUT == 'c4':
                return
            op(POOL, lambda E: E.tensor_add(out=gtmp[:], in0=kd_t[0][:], in1=kd_t[1][:]), r=[kd_t[0], kd_t[1]], w=[gtmp])
            op(POOL, lambda E, ct=ct: E.tensor_mul(out=gtmp[:], in0=gtmp[:], in1=rT[:, ct, :]), r=[gtmp, rT], w=[gtmp])
            op(DVE, lambda E, ct=ct: E.tensor_scalar_mul(out=rkr[:], in0=gtmp[:], scalar1=vecs[:, V_RK, ct:ct + 1]),
               r=[gtmp, vecs], w=[rkr])

            if CUT == 'ctprep':
                return
            yield
            op(POOL, lambda E: E.memset(yT[:], 0.0), w=[yT])

            def prep(e, ci, c):
                i = 2 * e + ci
                KBbar, T0, BA2, T1, A4 = KBbar2[i], T02[i], BA22[i], T12[i], A42[i]
                Zb, nU = Zb2[e], nU2[e]
                tsl = slice(c * 128, (c + 1) * 128)
                ei, en, kr, kbt, kbtok, af, na_, tt = Einc[i], Enin[i], KR[i], KBt[i], KBtok[i], AFM[i], nA[i], TT[i]
                ex = ei[:, 0:128] if e == 0 else ei[:, 2:130]
                wc = ei[:, 128:129] if e == 0 else ei[:, 1:2]
                p = npb()
                op(PE, lambda E, p=p, c=c, e=e, cs_=cs_: E.matmul(p[:, 0:128], lhsT=sig[c][e][:, cs_], rhs=tri[:, e, :],
                                                                  start=True, stop=True), r=[sig[c][e], tri], w=[p])
                op(ACT, lambda E, p=p, ei=ei: E.activation(out=ei[:, 1:129], in_=p[:, 0:128], func=AF.Exp), r=[p], w=[ei])
                op(ACT, lambda E, p=p, en=en: E.activation(out=en[:], in_=p[:, 0:128], func=AF.Exp, scale=-1.0), r=[p], w=[en])
                yield
                op(DVE, lambda E, kr=kr, ex=ex, tsl=tsl: E.tensor_tensor(out=kr[:, 0, :], in0=kap[:, tsl], in1=ex, op=ALU.mult),
                   r=[kap, ei], w=[kr])
                op(POOL, lambda E, kr=kr, ei=ei, tsl=tsl, ct=ct: E.tensor_mul(out=kr[:, 1, :], in0=rT[:, ct, tsl], in1=ei[:, 1:129]),
                   r=[rT, ei], w=[kr])
                op(DVE, lambda E, kbt=kbt, en=en, tsl=tsl, e=e: E.tensor_tensor(out=kbt[:, 0, :], in0=kd_t[e][:, tsl], in1=en[:], op=ALU.mult),
                   r=[kd_t[e], en], w=[kbt])
                op(POOL, lambda E, kbt=kbt, en=en, tsl=tsl, e=e: E.tensor_mul(out=kbt[:, 1, :], in0=b_t[e][:, tsl], in1=en[:]),
                   r=[b_t[e], en], w=[kbt])
                op(ACT, lambda E, kbt=kbt, wc=wc: E.activation(out=KBbar[:], in_=kbt[:], func=AF.Identity, scale=wc),
                   r=[kbt, ei], w=[KBbar])
                yield
                pt = nptr()
                for j in range(2):
                    op(PE, lambda E, pt=pt, j=j: E.transpose(pt[:, j * 128:(j + 1) * 128], KBbar[:, j, :], ident[:]),
                       r=[KBbar, ident], w=[pt])
                op(DVE, lambda E, pt=pt, kbtok=kbtok: E.tensor_copy(
                    out=kbtok[:], in_=pt[:, 0:256].rearrange("p (j c) -> p j c", j=2)), r=[pt], w=[kbtok])
                yield
                if CUT == 'tilde':
                    return
                for h2 in range(2):
                    lo = 64 * h2
                    pa = npb()
                    pn = npb()
                    op(PE, lambda E, pa=pa, lo=lo, kbt=kbt, kr=kr: E.matmul(
                        pa[:, 0:256], lhsT=kbt[lo:lo + 64, 1, :], rhs=kr[lo:lo + 64, :, :], start=True, stop=True),
                       r=[kbt, kr], w=[pa])
                    op(PE, lambda E, pa=pa, lo=lo, kbt=kbt, kr=kr: E.matmul(
                        pa[:, 256:512], lhsT=kbt[lo:lo + 64, 0, :], rhs=kr[lo:lo + 64, :, :], start=True, stop=True),
                       r=[kbt, kr], w=[pa])
                    op(PE, lambda E, pn=pn, lo=lo, h2=h2, kbt=kbt, kr=kr: E.matmul(
                        pn[:, 0:128], lhsT=kr[lo:lo + 64, 0, :], rhs=kbt[lo:lo + 64, 1, :],
                        start=True, stop=True), r=[kbt, kr], w=[pn])
                    op(DVE, lambda E, pa=pa, h2=h2, af=af, e=e: E.tensor_tensor(
                        out=af[h2][:], in0=pa[:, :], in1=mask4[:, e, :], op=ALU.mult), r=[pa, mask4], w=[af[h2]])
                    op(DVE, lambda E, pn=pn, na_=na_, e=e, h2=h2: E.tensor_tensor(
                        out=na_[:, h2, :], in0=pn[:, 0:128], in1=namask[:, e, 0:128], op=ALU.mult),
                       r=[pn, namask], w=[na_])
                yield
                if CUT == 'aforms':
                    return
                for h2 in range(2):
                    op(POOL, lambda E, h2=h2, af=af: E.tensor_add(out=T0[:, h2, :], in0=ident[:], in1=af[h2][:, 0:128]),
                       r=[ident, af[h2]], w=[T0])
                p1 = npb()
                for h2 in range(2):
                    op(PE, lambda E, p1=p1, h2=h2, af=af, na_=na_: E.matmul(
                        p1[:, h2 * 256:h2 * 256 + 128], lhsT=na_[:, h2, :], rhs=af[h2][:, 0:128], start=True, stop=True),
                       r=[na_, af[h2]], w=[p1])
                    op(PE, lambda E, p1=p1, h2=h2, af=af, na_=na_: E.matmul(
                        p1[:, h2 * 256 + 128:h2 * 256 + 256], lhsT=af[h2][:, 0:128], rhs=na_[:, h2, :], start=True, stop=True),
                       r=[na_, af[h2]], w=[p1])
                op(ACT, lambda E, p1=p1: E.activation(out=BA2[:].rearrange("p h j t -> p (h j t)"), in_=p1[:, :],
                                                      func=AF.Identity), r=[p1], w=[BA2])
                yield
                p2 = npb()
                for h2 in range(2):
                    op(PE, lambda E, p2=p2, h2=h2: E.matmul(p2[:, h2 * 128:(h2 + 1) * 128], lhsT=ident[:], rhs=T0[:, h2, :],
                                                            start=True, stop=False), r=[ident, T0], w=[p2])
                    op(PE, lambda E, p2=p2, h2=h2: E.matmul(p2[:, h2 * 128:(h2 + 1) * 128], lhsT=BA2[:, h2, 1, :], rhs=T0[:, h2, :],
                                                            start=False, stop=True), r=[BA2, T0], w=[p2])
                    op(PE, lambda E, p2=p2, h2=h2: E.matmul(p2[:, 256 + h2 * 128:256 + (h2 + 1) * 128], lhsT=BA2[:, h2, 0, :],
                                                            rhs=BA2[:, h2, 1, :], start=True, stop=True), r=[BA2], w=[p2])
                op(ACT, lambda E, p2=p2: E.activation(out=T1[:].rearrange("p h t -> p (h t)"), in_=p2[:, 0:256],
                                                      func=AF.Identity), r=[p2], w=[T1])
                op(ACT, lambda E, p2=p2: E.activation(out=A4[:].rearrange("p h t -> p (h t)"), in_=p2[:, 256:512],
                                                      func=AF.Identity), r=[p2], w=[A4])
                yield
                p3 = npb()
                for h2 in range(2):
                    op(PE, lambda E, p3=p3, h2=h2: E.matmul(p3[:, h2 * 128:(h2 + 1) * 128], lhsT=ident[:], rhs=T1[:, h2, :],
                                                            start=True, stop=False), r=[ident, T1], w=[p3])
                    op(PE, lambda E, p3=p3, h2=h2: E.matmul(p3[:, h2 * 128:(h2 + 1) * 128], lhsT=A4[:, h2, :], rhs=T1[:, h2, :],
                                                            start=False, stop=True), r=[A4, T1], w=[p3])
                op(ACT, lambda E, p3=p3, tt=tt: E.activation(out=tt[:].rearrange("p h t -> p (h t)"), in_=p3[:, 0:256],
                                                             func=AF.Identity), r=[p3], w=[tt])
                yield
                yield

            def serial(e, ci, c):
                i = 2 * e + ci
                KBbar, T0, BA2, T1, A4 = KBbar2[i], T02[i], BA22[i], T12[i], A42[i]
                Zb, nU = Zb2[e], nU2[e]
                tsl = slice(c * 128, (c + 1) * 128)
                ei, en, kr, kbt, kbtok, af, na_, tt = Einc[i], Enin[i], KR[i], KBt[i], KBtok[i], AFM[i], nA[i], TT[i]
                ex = ei[:, 0:128] if e == 0 else ei[:, 2:130]
                wc = ei[:, 128:129] if e == 0 else ei[:, 1:2]
                if CUT == 'tchain':
                    return
                first = (ci == 0) and not stacked
                for h2 in range(2):
                    lo = 64 * h2
                    h = 2 * ct + h2
                    pz = npb()
                    if not first:
                        op(PE, lambda E, pz=pz, h2=h2, lo=lo, kr=kr, e=e: E.matmul(
                            pz[:, 0:VW], lhsT=kr[lo:lo + 64, 0, :], rhs=Sb[e][lo:lo + 64, 0:VW],
                            start=True, stop=False), r=[kr, Sb[e]], w=[pz])
                    op(PE, lambda E, pz=pz, h2=h2, h=h, af=af, c=c, first=first: E.matmul(
                        pz[:, 0:VW], lhsT=af[h2][:, 256:384], rhs=vsrc(c, h),
                        start=first, stop=True), r=[af[h2], vtok[c], vext[c]], w=[pz])
                    op(ACT, lambda E, pz=pz, h2=h2: E.activation(out=Zb[:, h2, 0:VW], in_=pz[:, 0:VW],
                                                                 func=AF.Identity), r=[pz], w=[Zb])
                yield
                pu = npb()
                for h2 in range(2):
                    op(PE, lambda E, pu=pu, h2=h2, tt=tt: E.matmul(pu[:, h2 * VW:(h2 + 1) * VW], lhsT=tt[:, h2, :], rhs=Zb[:, h2, 0:VW],
                                                                    start=True, stop=True), r=[tt, Zb], w=[pu])
                op(ACT, lambda E, pu=pu: E.activation(out=nU[:, :, 0:VW], in_=pu[:, 0:2 * VW].rearrange("p (h v) -> p h v", h=2),
                                                      func=AF.Identity, scale=-1.0), r=[pu], w=[nU])
                yield
                for h2 in range(2):
                    lo = 64 * h2
                    h = 2 * ct + h2
                    py = npb()
                    ylo = 0 if stacked else lo
                    if not first:
                        op(PE, lambda E, py=py, lo=lo, ylo=ylo, kr=kr, e=e: E.matmul(
                            py[ylo:ylo + VW, 0:128], lhsT=Sb[e][lo:lo + 64, 0:VW], rhs=kr[lo:lo + 64, 1, :],
                            start=True, stop=False), r=[kr, Sb[e]], w=[py])
                    op(PE, lambda E, py=py, ylo=ylo, h=h, h2=h2, af=af, c=c, first=first: E.matmul(
                        py[ylo:ylo + VW, 0:128], lhsT=vsrc(c, h), rhs=af[h2][:, 384:512],
                        start=first, stop=False), r=[af[h2], vtok[c], vext[c]], w=[py])
                    op(PE, lambda E, py=py, ylo=ylo, h2=h2, af=af: E.matmul(
                        py[ylo:ylo + VW, 0:128], lhsT=nU[:, h2, 0:VW], rhs=af[h2][:, 128:256],
                        start=False, stop=True), r=[af[h2], nU], w=[py])
                    if stacked:
                        ys_ = yst[e]
                        op(ACT, lambda E, py=py, tsl=tsl, h2=h2, ys_=ys_: E.activation(out=ys_[:, h2, tsl], in_=py[:, 0:128],
                                                                                      func=AF.Identity), r=[py], w=[ys_])
                    else:
                        op(DVE, lambda E, py=py, tsl=tsl, lo=lo: E.tensor_tensor(
                            out=yT[lo:lo + 64, tsl], in0=py[lo:lo + 64, 0:128], in1=yT[lo:lo + 64, tsl], op=ALU.add),
                           r=[py, yT], w=[yT])
                yield
                pss = npb()
                for h2 in range(2):
                    lo = 64 * h2
                    h = 2 * ct + h2
                    op(PE, lambda E, pss=pss, lo=lo, h=h, kbtok=kbtok, c=c: E.matmul(
                        pss[lo:lo + 64, 0:VW], lhsT=kbtok[:, 0, lo:lo + 64], rhs=vsrc(c, h),
                        start=True, stop=False), r=[kbtok, vtok[c], vext[c]], w=[pss])
                    op(PE, lambda E, pss=pss, lo=lo, h2=h2, kbtok=kbtok: E.matmul(
                        pss[lo:lo + 64, 0:VW], lhsT=kbtok[:, 1, lo:lo + 64], rhs=nU[:, h2, 0:VW],
                        start=False, stop=True), r=[kbtok, nU], w=[pss])
                op(DVE, lambda E, pss=pss, wc=wc, e=e: E.scalar_tensor_tensor(
                    out=Sf[e][:, 0:VW], in0=Sf[e][:, 0:VW], scalar=wc, in1=pss[:, 0:VW], op0=ALU.mult, op1=ALU.add),
                   r=[pss, ei, Sf[e]], w=[Sf[e]])
                op(POOL, lambda E, e=e: E.tensor_copy(out=Sb[e][:, 0:VW], in_=Sf[e][:, 0:VW]), r=[Sf[e]], w=[Sb[e]])
                yield

            def chain(e, part):
                order_ = (0, 1) if e == 0 else (1, 0)
                if part == 0:
                    yield from prep(e, 0, order_[0])
                    yield from chain_init_and_first(e, order_[0])
                    return
                yield from serial(e, 1, order_[1])
                yield from chain_tail(e)

            def chain_init_and_first(e, c):
                op(POOL, lambda E, e=e: E.memset(Sf[e][:], 0.0), w=[Sf[e]])
                if stacked:
                    op(POOL, lambda E, e=e: E.tensor_add(out=Sf[e][:, 0:64], in0=identf[:, 0:64], in1=identf[:, 64:128]),
                       r=[identf, Sf[e]], w=[Sf[e]])
                op(POOL, lambda E, e=e: E.tensor_copy(out=Sb[e][:], in_=Sf[e][:]), r=[Sf[e]], w=[Sb[e]])
                yield from serial(e, 0, c)

            def chain_tail(e):
                if stacked:
                    dma(SP, ext_d[blk, e, 2 * ct:2 * ct + 2, :, :].rearrange("h k n -> (h k) n"), Sf[e][:], r=[Sf[e]],
                        w=[ext_d], sem=f"Sfo{e}")
                    dma(SP, yext_d[blk, e, ct, :, :], yst[e][:].rearrange("p h t -> p (h t)"), r=[yst[e]], w=[yext_d],
                        sem=f"yst{e}")
                elif state_out is not None and CUT not in ('tilde', 'aforms', 'tchain', 'nostate'):
                    pst = npb()
                    op(PE, lambda E, pst=pst, e=e: E.matmul(pst[0:64, 0:128], lhsT=Sf[e][:, 0:64], rhs=identf[:, :], start=True, stop=True),
                       r=[Sf[e], identf], w=[pst])
                    op(ACT, lambda E, pst=pst: E.activation(out=st_out[:].rearrange("v h k -> v (h k)"), in_=pst[0:64, 0:128],
                                                            func=AF.Identity), r=[pst], w=[st_out])
                    dma(SP, state_out[e, 2 * ct:2 * ct + 2, :, :].rearrange("h v k -> v h k"), st_out[:], r=[st_out], sem="st_out")
                yield

            def rr(gens):
                while gens:
                    for g_ in list(gens):
                        try:
                            next(g_)
                        except StopIteration:
                            gens.remove(g_)

            rr([chain(0, 0), chain(1, 0), prep(0, 1, 1), prep(1, 1, 0)])
            rr([chain(0, 1), chain(1, 1)])
            yield
            if CUT in ('tilde', 'aforms', 'tchain', 'serial'):
                return
            if stacked:
                pbn = npb()
                op(PE, lambda E, pbn=pbn: E.matmul(pbn[:, 0:256], lhsT=bo[:, 0, :], rhs=rkr[:], start=True, stop=True), r=[bo, rkr], w=[pbn])
                op(DVE, lambda E, pbn=pbn, ct=ct: E.tensor_tensor(out=bonst[:], in0=pbn[:, 0:256], in1=vT[:, ct, :], op=ALU.mult),
                   r=[pbn, vT], w=[bonst])
                dma(SP, post_d[blk, ct, 1, :, :], bonst[:], r=[bonst], w=[post_d], sem="bonst")
                return
            post_ct(ct, None, (yT, ybf, yc, sdv, gzT, oT), rkr)
        cgs = [ctgen(ct) for ct in range(KC)]
        next(cgs[0], None)
        for ct in range(KC):
            next(cgs[ct], None)
            if ct + 1 < KC:
                next(cgs[ct + 1], None)
            next(cgs[ct], None)
        if stacked:
            dma(SP, post_d[blk, :, 0, :, :].rearrange("c p t -> p c t"), gzT[:], r=[gzT], w=[post_d], sem="gzo")
            return
        if CUT:
            return
        wo = [load_wh(4, 0), load_wh(4, 1)]
        for c in range(2):
            y = out_proj_ln(0, x1rows[c * 128:(c + 1) * 128, :], oT, c * 128, oT, lambda nh: (wo[nh], wo[nh][:, :, :]), 1, v)
            dma(SP, yout_rows[c * 128:(c + 1) * 128, :], y[:], r=[y], sem=y.name)

    def post_ct(ct, bon_ap, bufs, rkr=None):
        yT, ybf, yc, sdv, gzT, oT = bufs
        if True:
            op(POOL, lambda E: E.tensor_copy(out=ybf[:], in_=yT[:]), r=[yT], w=[ybf])
            pm = npb()
            op(PE, lambda E, pm=pm: E.matmul(pm[:, 0:256], lhsT=bo[:, 1, :], rhs=ybf[:], start=True, stop=True), r=[bo, ybf], w=[pm])
            op(DVE, lambda E, pm=pm: E.tensor_tensor(out=yc[:], in0=yT[:], in1=pm[:, 0:256], op=ALU.subtract), r=[yT, pm], w=[yc])
            op(POOL, lambda E: E.tensor_mul(out=ybf[:], in0=yc[:], in1=yc[:]), r=[yc], w=[ybf])
            pv = npb()
            op(PE, lambda E, pv=pv: E.matmul(pv[:, 0:256], lhsT=bo[:, 1, :], rhs=ybf[:], start=True, stop=True), r=[bo, ybf], w=[pv])
            op(ACT, lambda E, pv=pv: E.activation(out=sdv[:], in_=pv[:, 0:256], func=AF.Ln, bias=consts[:, 1:2], scale=1.0),
               r=[pv, consts], w=[sdv])
            op(ACT, lambda E: E.activation(out=sdv[:], in_=sdv[:], func=AF.Exp, scale=-0.5), r=[sdv], w=[sdv])
            op(POOL, lambda E: E.tensor_mul(out=yc[:], in0=yc[:], in1=sdv[:]), r=[yc, sdv], w=[yc])
            op(ACT, lambda E, ct=ct: E.activation(out=yc[:], in_=yc[:], func=AF.Identity, scale=vecs[:, V_LG, ct:ct + 1],
                                                  bias=vecs[:, V_LB, ct:ct + 1]), r=[yc, vecs], w=[yc])
            if bon_ap is None:
                pbn = npb()
                op(PE, lambda E, pbn=pbn: E.matmul(pbn[:, 0:256], lhsT=bo[:, 0, :], rhs=rkr[:], start=True, stop=True), r=[bo, rkr], w=[pbn])
                op(DVE, lambda E, pbn=pbn, ct=ct: E.tensor_tensor(out=sdv[:], in0=pbn[:, 0:256], in1=vT[:, ct, :], op=ALU.mult),
                   r=[pbn, vT], w=[sdv])
                op(POOL, lambda E: E.tensor_add(out=yc[:], in0=yc[:], in1=sdv[:]), r=[yc, sdv], w=[yc])
            else:
                op(POOL, lambda E: E.tensor_add(out=yc[:], in0=yc[:], in1=bon_ap[0]), r=[yc, bon_ap[1]], w=[yc])
            op(POOL, lambda E, ct=ct: E.tensor_mul(out=oT[:, ct, :], in0=yc[:], in1=gzT[:, ct, :]), r=[yc, gzT], w=[oT])

    for s4 in range(NBLK):
        rwkv_block(s4, x1d[s4 * 256:(s4 + 1) * 256, :], [[x1d.sub(2 * s4)], [x1d.sub(2 * s4 + 1)]], 0,
                   o_yp[s4 * 256:(s4 + 1) * 256, :], o_state[s4])
    if stage == 3:
        return kb.finish(outs_final)

    kb.barrier()
    for t in range(8):
        ln_transpose(x1d[1024 + t * 128:1024 + (t + 1) * 128, :], [x1d.sub(8 + t)], modT[1], 1, u1T_s, 1 + t * 128, u1T_s)
    hin = kb.dram("hin", [2, D], BF16)
    hout = kb.dram("hout", [8, D], BF16)
    hsb = T(dx.h[0:8, 0:4, :].rearrange("p a b -> p (a b)"), "hsb")
    hsb.trk = dx.trk
    selh = sb("selh", [8, 2], BF16)
    selc = sb("selc", [64, 2, 4], F32)
    dma(POOL, selh[:], selh_d[:, :], w=[selh], sem="c3")
    dma(SP, selc[:], selc_d[:, :, :], w=[selc], sem="c3s")
    for w_, col in ((0, 1), (1, 1024)):
        dma(SP, hin[w_, :].rearrange("(kc p) -> p kc", p=128), u1T_s[:, :, col], r=[u1T_s], w=[hin], sem="hin",
            allow_slow_non_contiguous=True)
    kb.collective("AllGather", ALU.bypass, GROUPS, hin[:, :], hout[:, :], r=[hin], w=[hout], sem="ag2")
    dma(SP, hsb[:], hout[:, :], r=[hout], w=[hsb], sem="hsb")
    ph = npb()
    for kc in range(KC):
        op(PE, lambda E, kc=kc: E.matmul(ph[:, 2 * kc:2 * kc + 2], lhsT=hsb[0:8, kc * 128:(kc + 1) * 128], rhs=selh[0:8, :],
                                         start=True, stop=True), r=[hsb, selh], w=[ph])
    for w_, col in ((0, 0), (1, 1025)):
        op(DVE, lambda E, w_=w_, col=col: E.tensor_copy(out=u1T_s[:, :, col],
                                                        in_=ph[:, 0:16].rearrange("p (k w) -> p k w", w=2)[:, :, w_]),
           r=[ph], w=[u1T_s])
    for sblk in range(4):
        rwkv_block(sblk, None, None, 1, None, None, stacked=True)

    kb.barrier()
    kb.release(mark_blk)
    yT_x = sb("yT2", [128, 256], F32)
    ybf_x = sb("ybf2", [128, 256], BF16)
    yc_x = sb("yc2", [128, 256], F32)
    sdv_x = sb("sdv2", [128, 256], F32)
    gzT_x = sb("gzT2", [128, KC, 256], BF16)
    oT_x = sb("oT2", [128, KC, 256], BF16)
    EXTs = sb("EXTs", [64, 4, 8, 128], F32)
    QTt = sb("QTt", [64, 4, 8, 64], F32)
    Xab = [sb(f"Xab{i}", [64, 8, 128], F32) for i in range(2)]
    CE = sb("CE", [64, 4, 8, 128], F32)
    QcT = sb("QcT", [64, 4, 8, 64], F32)
    S0v = sb("S0v", [64, 8, 64], F32)
    Bab = [sb(f"Bab{i}", [64, 8, 64], F32) for i in range(2)]
    accS = sb("accS", [64, 8, 64], F32)
    FIN = sb("FIN", [128, 4, 2, 16, 64], BF16)
    yx = [sb(f"yx{i}", [128, 2, 512], BF16) for i in range(2)]
    pl = [sb(f"pl{i}", [128, 2, 256], BF16) for i in range(2)]
    cin = kb.dram("cin", [2, 16, 64, 128], F32)
    cout = kb.dram("cout", [4, 2, 16, 64, 128], F32)
    op(POOL, lambda E: E.memset(FIN[:], 0.0), w=[FIN])
    for j in range(4):
        for e in range(2):
            op(POOL, lambda E, j=j, e=e: E.tensor_copy(
                out=FIN[64:128, j, e, :, :], in_=ident[64:128, 64:128].unsqueeze(1).to_broadcast([64, 16, 64])),
               r=[ident, FIN], w=[FIN])

    def load_ext(e, hg):
        for j in range(4):
            dma(SP, EXTs[:, j, :, :], ext_d[j, e, hg * 8:(hg + 1) * 8, :, :].rearrange("h k n -> k h n"),
                r=[ext_d], w=[EXTs], sem="EXTs")
        for j in range(4):
            p = npb()
            for hh in range(8):
                op(PE, lambda E, p=p, j=j, hh=hh: E.matmul(p[0:64, hh * 64:(hh + 1) * 64], lhsT=EXTs[:, j, hh, 0:64],
                                                           rhs=identf[0:64, 0:64], start=True, stop=True), r=[EXTs, identf], w=[p])
            op(ACT, lambda E, p=p, j=j: E.activation(out=QTt[:, j, :, :].rearrange("p h k -> p (h k)"), in_=p[0:64, :],
                                                     func=AF.Identity), r=[p], w=[QTt])

    for e in range(2):
        order = [0, 1, 2, 3] if e == 0 else [3, 2, 1, 0]
        for hg in range(2):
            load_ext(e, hg)
            cur_ap = EXTs[:, order[0], :, :]
            cur_trk = EXTs
            for step, j in enumerate(order[1:]):
                xn = Xab[step % 2]
                for half in range(2):
                    p = npb()
                    for h4 in range(4):
                        hh = half * 4 + h4
                        op(PE, lambda E, p=p, h4=h4, hh=hh, j=j, cur_ap=cur_ap: E.matmul(
                            p[0:64, h4 * 128:(h4 + 1) * 128], lhsT=QTt[:, j, hh, :], rhs=cur_ap[:, hh, :], start=True, stop=True),
                           r=[QTt, cur_trk], w=[p])
                    pv3 = p[0:64, :].rearrange("p (h n) -> p h n", h=4)
                    op(ACT, lambda E, pv3=pv3, xn=xn, half=half: E.activation(out=xn[:, half * 4:half * 4 + 4, 0:64], in_=pv3[:, :, 0:64],
                                                                              func=AF.Identity), r=[p], w=[xn])
                    op(DVE, lambda E, pv3=pv3, xn=xn, half=half, j=j: E.tensor_tensor(
                        out=xn[:, half * 4:half * 4 + 4, 64:128], in0=pv3[:, :, 64:128], in1=EXTs[:, j, half * 4:half * 4 + 4, 64:128],
                        op=ALU.add), r=[p, EXTs], w=[xn])
                cur_ap, cur_trk = xn[:], xn
            dma(SP, cin[e, hg * 8:(hg + 1) * 8, :, :].rearrange("h k n -> k h n"), cur_ap, r=[cur_trk], w=[cin], sem="cin")
    kb.collective("AllGather", ALU.bypass, GROUPS, cin[:].rearrange("e h k n -> (e h k) n"),
                  cout[:].rearrange("r e h k n -> (r e h k) n"), r=[cin], w=[cout], sem="ag3")

    for e in range(2):
        order = [0, 1, 2, 3] if e == 0 else [3, 2, 1, 0]
        for hg in range(2):
            load_ext(e, hg)
            for r_ in range(4):
                dma(SP, CE[:, r_, :, :], cout[r_, e, hg * 8:(hg + 1) * 8, :, :].rearrange("h k n -> k h n"),
                    r=[cout], w=[CE], sem="CE")
            for r_ in range(4):
                p = npb()
                for hh in range(8):
                    op(PE, lambda E, p=p, r_=r_, hh=hh: E.matmul(p[0:64, hh * 64:(hh + 1) * 64], lhsT=CE[:, r_, hh, 0:64],
                                                                 rhs=identf[0:64, 0:64], start=True, stop=True), r=[CE, identf], w=[p])
                op(ACT, lambda E, p=p, r_=r_: E.activation(out=QcT[:, r_, :, :].rearrange("p h k -> p (h k)"), in_=p[0:64, :],
                                                           func=AF.Identity), r=[p], w=[QcT])
            dma(SP, S0v[:], st0_d[e, hg * 8:(hg + 1) * 8, :, :].rearrange("h v k -> v h k"), w=[S0v], sem="S0v")
            p = npb()
            for hh in range(8):
                op(PE, lambda E, p=p, hh=hh: E.matmul(p[0:64, hh * 64:(hh + 1) * 64], lhsT=S0v[:, hh, :], rhs=identf[0:64, 0:64],
                                                      start=True, stop=True), r=[S0v, identf], w=[p])
            bcur = Bab[0]
            op(ACT, lambda E, p=p, bcur=bcur: E.activation(out=bcur[:].rearrange("p h v -> p (h v)"), in_=p[0:64, :],
                                                           func=AF.Identity), r=[p], w=[bcur])
            op(DVE, lambda E, bcur=bcur, e=e: E.tensor_scalar_mul(out=accS[:], in0=bcur[:], scalar1=selc[:, e, 0:1]),
               r=[bcur, selc], w=[accS])
            for i in range(3):
                r_ = order[i]
                bn = Bab[(i + 1) % 2]
                p = npb()
                for hh in range(8):
                    op(PE, lambda E, p=p, hh=hh, r_=r_, bcur=bcur: E.matmul(
                        p[0:64, hh * 64:(hh + 1) * 64], lhsT=QcT[:, r_, hh, :], rhs=bcur[:, hh, :], start=True, stop=True),
                       r=[QcT, bcur], w=[p])
                op(DVE, lambda E, p=p, bn=bn, r_=r_: E.tensor_tensor(
                    out=bn[:], in0=p[0:64, :].rearrange("p (h v) -> p h v", h=8), in1=CE[:, r_, :, 64:128], op=ALU.add),
                   r=[p, CE], w=[bn])
                op(DVE, lambda E, bn=bn, e=e, i=i: E.scalar_tensor_tensor(
                    out=accS[:], in0=bn[:], scalar=selc[:, e, i + 1:i + 2], in1=accS[:], op0=ALU.mult, op1=ALU.add),
                   r=[bn, selc, accS], w=[accS])
                bcur = bn
            scur = accS
            for idx, j in enumerate(order):
                op(ACT, lambda E, scur=scur, j=j, e=e, hg=hg: E.activation(
                    out=FIN[0:64, j, e, hg * 8:(hg + 1) * 8, :], in_=scur[:], func=AF.Identity), r=[scur], w=[FIN])
                if idx < 3:
                    sn = Bab[idx % 2]
                    p = npb()
                    for hh in range(8):
                        op(PE, lambda E, p=p, hh=hh, j=j, scur=scur: E.matmul(
                            p[0:64, hh * 64:(hh + 1) * 64], lhsT=QTt[:, j, hh, :], rhs=scur[:, hh, :], start=True, stop=True),
                           r=[QTt, scur], w=[p])
                    op(DVE, lambda E, p=p, sn=sn, j=j: E.tensor_tensor(
                        out=sn[:], in0=p[0:64, :].rearrange("p (h v) -> p h v", h=8), in1=EXTs[:, j, :, 64:128], op=ALU.add),
                       r=[p, EXTs], w=[sn])
                    scur = sn

    nfx = 0
    for j in range(4):
        for ct in range(KC):
            yx_, pl_ = yx[nfx % 2], pl[nfx % 2]
            nfx += 1
            dma(SP, yx_[:], yext_d[j, :, ct, :, :].rearrange("e p n -> p e n"), r=[yext_d], w=[yx_], sem=yx_.name)
            dma(SP, pl_[:], post_d[j, ct, :, :, :].rearrange("w p t -> p w t"), r=[post_d], w=[pl_], sem=pl_.name)
            py = npb()
            for h2 in range(2):
                lo = 64 * h2
                for e in range(2):
                    op(PE, lambda E, py=py, lo=lo, h2=h2, e=e, j=j, ct=ct, yx_=yx_: E.matmul(
                        py[lo:lo + 64, 0:256], lhsT=FIN[:, j, e, 2 * ct + h2, :], rhs=yx_[:, e, h2 * 256:(h2 + 1) * 256],
                        start=(e == 0), stop=(e == 1)), r=[FIN, yx_], w=[py])
            op(ACT, lambda E, py=py: E.activation(out=yT_x[:], in_=py[:, 0:256], func=AF.Identity), r=[py], w=[yT_x])
            op(POOL, lambda E, ct=ct, pl_=pl_: E.tensor_copy(out=gzT_x[:, ct, :], in_=pl_[:, 0, :]), r=[pl_], w=[gzT_x])
            post_ct(ct, (pl_[:, 1, :], pl_), (yT_x, ybf_x, yc_x, sdv_x, gzT_x, oT_x))
        wo = [load_wh(4, 0), load_wh(4, 1)]
        for c in range(2):
            r0 = 1024 + j * 256 + c * 128
            y = out_proj_ln(0, x1d[r0:r0 + 128, :], oT_x, c * 128, oT_x, (lambda nh, wo=wo: (wo[nh], wo[nh][:, :, :])), 1, 1)
            dma(SP, o_ys[j * 256 + c * 128:j * 256 + (c + 1) * 128, :], y[:], r=[y], sem=y.name)

    return kb.finish(outs_final)


_NC_CACHE = {}


def _dft_tables():
    i = np.arange(128, dtype=np.float64)
    ang = 2 * np.pi * np.outer(i, i) / 128.0
    dftc = np.concatenate([np.cos(ang), -np.sin(ang)], 1) / np.sqrt(128.0)
    l = np.arange(256, dtype=np.float64)
    ang = 2 * np.pi * np.outer(l, l) / 256.0
    dftl = np.concatenate([np.cos(ang), np.sin(ang)], 1) / 16.0
    return dftc.astype(np.float32), dftl.astype(np.float32)


_DFTC, _DFTL256 = _dft_tables()


def _na_tables():
    half = np.arange(128)[:, None, None] // 64
    ck = np.arange(128)[:, None, None] % 64
    idx = np.arange(30)[None, :, None]
    cq = np.arange(64)[None, None, :]
    dr = 14 - idx + half + 0 * cq
    dc = np.clip(ck - cq + 15, 0, 30) + 0 * idx
    cstart = np.clip(cq - 8, 0, 48)
    col_in = (ck >= cstart) & (ck < cstart + 16)
    valid = (np.abs(dr) <= 7) & col_in
    dri = np.clip(dr + 7, 0, 14)
    qmask = np.zeros((4, 24, 1024), np.float32)
    for qd in range(4):
        for q in range(1024):
            r = 16 * qd + q // 64
            st = min(max(r - 4, 0), 56)
            for j in range(24):
                R = 16 * qd - 4 + j
                if not (st <= R < st + 8):
                    qmask[qd, j, q] = -1e30
    rowoh = (np.arange(1536)[None, :] // 64 == np.arange(24)[:, None]).astype(np.float32)
    return dri.astype(np.int64), dc.astype(np.int64), valid, qmask, rowoh


_BB_DR, _BB_DC, _BB_VALID, _QMASK, _ROWOH = _na_tables()
_DFTS_CACHE = {}


def _dfts(qd):
    if qd not in _DFTS_CACHE:
        l = np.arange(4096, dtype=np.int64)[:, None]
        m = (1024 * qd + np.arange(1024, dtype=np.int64))[None, :]
        ang = 2 * np.pi * ((l * m) % 4096).astype(np.float64) / 4096.0
        t = np.stack([np.cos(ang), np.sin(ang)], 1) / 64.0
        _DFTS_CACHE[qd] = np.ascontiguousarray(t.astype(np.float32).astype(ml_dtypes.bfloat16))
    return _DFTS_CACHE[qd]


def _scan_tables():
    i = np.arange(128)
    s_, t_ = i[:, None], i[None, :]
    us, ls = (s_ < t_).astype(np.float32), (s_ > t_).astype(np.float32)
    ui, li = (s_ <= t_).astype(np.float32), (s_ >= t_).astype(np.float32)
    tri = np.stack([-0.606531 * ui, -0.606531 * li], 1)
    mask4 = np.stack([np.concatenate([-us, ui, us, ui], 1), np.concatenate([-ls, li, ls, li], 1)], 1)
    namask = np.stack([np.concatenate([-ls, -ls], 1), np.concatenate([-us, -us], 1)], 1)
    blk = (s_ // 64 == t_ // 64).astype(np.float32)
    bo = np.stack([blk, blk / 64.0], 1)
    c = np.ascontiguousarray
    return c(tri.astype(np.float32)), c(mask4), c(namask), c(bo.astype(np.float32))


_TRI, _MASK4, _NAMASK, _BO = _scan_tables()


def _rw_vecs(inp):
    f = lambda a: np.asarray(a, dtype=np.float32)
    rows = [f(inp["rw_mu"])[0][i] for i in range(6)]
    rows += [f(inp["rw_k_k"])[0], f(inp["rw_k_a"])[0], f(inp["rw_r_k"])[0].reshape(-1), f(inp["rw_lnx_g"])[0],
             f(inp["rw_lnx_b"])[0], f(inp["rw_a0"])[0][0], f(inp["rw_a0"])[0][1], np.zeros(1024, np.float32)]
    v = np.stack(rows, 0)
    return np.ascontiguousarray(v.reshape(14, KC, 128).transpose(2, 0, 1))


def _prep_inputs(inp):
    f = lambda a: np.ascontiguousarray(np.asarray(a, dtype=np.float32))
    x_prompt = f(inp["x_prompt"])
    c = f(inp["c"])
    c_ctx = f(inp["c_ctx"])
    x_sample = f(inp["x_sample"])
    cache_k = f(inp["cache_k"])
    cache_v = f(inp["cache_v"])
    rpb = f(inp["ev_rpb"])[0]
    state_rwkv = f(inp["state_rwkv"])
    na_bias = np.where(_BB_VALID[None], rpb[:, _BB_DR, _BB_DC], np.float32(-1e30)).astype(np.float32).reshape(8, 128, 30 * 64)
    ada_b = f(inp["ada_b"])
    common = {
        "ada_w": f(inp["ada_w"]),
        "ada_b": ada_b,
        "ada_bT": np.ascontiguousarray(ada_b.reshape(2, 24, 128).transpose(0, 2, 1)),
        "ev_w_in": f(inp["ev_w_in"])[0],
        "ev_w_fnet": f(inp["ev_w_fnet"])[0],
        "ev_w_out": f(inp["ev_w_out"])[0],
        "post_ln_g": f(inp["post_ln_g"]),
        "post_ln_b": f(inp["post_ln_b"]),
        "rw_w_rkvz": f(inp["rw_w_rkvz"])[0],
        "rw_w1": f(inp["rw_w1"])[0], "rw_w2": f(inp["rw_w2"])[0],
        "rw_a1": f(inp["rw_a1"])[0], "rw_a2": f(inp["rw_a2"])[0],
        "rw_g1": f(inp["rw_g1"])[0], "rw_g2": f(inp["rw_g2"])[0],
        "rw_w0": f(inp["rw_w0"])[0], "rw_w_out": f(inp["rw_w_out"])[0],
        "rw_vecs": _rw_vecs(inp),
        "tri": _TRI, "mask4": _MASK4, "namask": _NAMASK, "blockones": _BO,
        "dftc": _DFTC,
        "dftl256": _DFTL256,
    }
    maps = []
    for core in range(NCORES):
        b = core // 4
        cv = np.stack([c_ctx, c[b]], 0)
        m = dict(common)
        m["xp"] = x_prompt[4 * core:4 * core + 4].reshape(1024, D)
        qd = core % 4
        win = np.zeros((24, 64, D), np.float32)
        r0 = 16 * qd - 4
        lo_, hi_ = max(r0, 0), min(r0 + 24, 64)
        win[lo_ - r0:hi_ - r0] = x_sample[b].reshape(64, 64, D)[lo_:hi_]
        m["xs_win"] = win.reshape(1536, D)
        m["cache_k"] = cache_k[b, 0].reshape(512, 512)
        m["cache_v"] = cache_v[b, 0].reshape(512, 512)
        m["na_bias"] = na_bias
        m["na_qmask"] = _QMASK[qd]
        m["na_rowoh"] = _ROWOH
        m["dfts"] = _dfts(qd)
        m["state0"] = state_rwkv[b, 0]
        selc = np.zeros((64, 2, 4), np.float32)
        selc[:, 0, qd] = 1.0
        selc[:, 1, 3 - qd] = 1.0
        m["selc"] = selc
        selh = np.zeros((8, 2), np.float32)
        if qd > 0:
            selh[2 * (qd - 1) + 1, 0] = 1.0
        if qd < 3:
            selh[2 * (qd + 1), 1] = 1.0
        m["selh"] = selh
        m["cvT"] = np.ascontiguousarray(cv.reshape(2, KC, 128).transpose(2, 1, 0))
        maps.append(m)
    return maps


def kernel(_stage=99, _raw=False, **inputs):
    if _stage not in _NC_CACHE:
        _NC_CACHE[_stage] = build(_stage)
    nc = _NC_CACHE[_stage]
    maps = _prep_inputs(inputs)
    res = run_bass_kernel_spmd(nc, maps, core_ids=list(range(NCORES)))
    R = res.results
    if _raw:
        return R
    y_prompt = np.concatenate([R[c]["o_yp"].reshape(4, 256, D) for c in range(NCORES)], 0)
    y_sample = np.concatenate([R[c]["o_ys"] for c in range(NCORES)], 0).reshape(2, 4096, D)
    new_k = np.concatenate([R[c]["o_newk"].reshape(4, 1, 256, 8, 64) for c in range(NCORES)], 0)
    new_v = np.concatenate([R[c]["o_newv"].reshape(4, 1, 256, 8, 64) for c in range(NCORES)], 0)
    new_state = np.concatenate([R[c]["o_state"].reshape(4, 1, 2, 16, 64, 64) for c in range(NCORES)], 0)
    return (y_prompt, y_sample, new_k, new_v, new_state)
```

```python
import os
import numpy as np
import ml_dtypes
from contextlib import ExitStack
import concourse.bass as bass
import concourse.mybir as mybir
from concourse.bass_utils import run_bass_kernel_spmd

F32 = mybir.dt.float32
BF16 = mybir.dt.bfloat16
AF = mybir.ActivationFunctionType
ALU = mybir.AluOpType
AX = mybir.AxisListType

NCORES = 8
D = 1024
KC = 8
LN_EPS = 1e-6
ALPHA = 4 ** 0.25
PE, ACT, DVE, POOL, SP = "pe", "act", "dve", "pool", "sp"


class Trk:
    __slots__ = ("w", "r", "name")

    def __init__(self, name=""):
        self.w = None
        self.r = {}
        self.name = name


class T:
    def __init__(self, h, name):
        self.h = h
        self.name = name
        self.trk = Trk(name)
        self.subs = {}

    def sub(self, key):
        if key not in self.subs:
            self.subs[key] = Trk(f"{self.name}.{key}")
        return self.subs[key]

    def __getitem__(self, k):
        return self.h[k]


def _trk(x):
    return x.trk if isinstance(x, T) else x


class KB:
    def __init__(self):
        self.nc = bass.Bass("TRN2", target_bir_lowering=False)
        self.es = ExitStack()
        self.q = {e: [] for e in (PE, ACT, DVE, POOL, SP)}
        self.cnt = {e: 0 for e in self.q}
        self.waited = {e: {} for e in self.q}
        self.sems = {}
        self.dcnt = {}
        for e in self.q:
            self.sems[e] = self.es.enter_context(self.nc.semaphore("s_" + e))
        self.n_ops = 0

    def sb(self, name, shape, dt):
        return T(self.es.enter_context(self.nc.sbuf_tensor(name, list(shape), dt)), name)

    def ps(self, name, shape, dt):
        return T(self.es.enter_context(self.nc.psum_tensor(name, list(shape), dt)), name)

    def dram(self, name, shape, dt, kind="Internal"):
        h = self.nc.dram_tensor(name, list(shape), dt, kind=kind)
        return T(h.ap(), name)

    def init_arena(self, nbytes):
        self.arena = self.es.enter_context(self.nc.sbuf_tensor("arena", [128, nbytes // 2], BF16))
        self.a_off = 0
        self.a_size = nbytes

    def alloc(self, name, shape, dt):
        esz = 4 if dt == F32 else 2
        n = int(np.prod(shape[1:]))
        nb = (n * esz + 63) // 64 * 64
        off = self.a_off
        self.a_off += nb
        assert self.a_off <= self.a_size, (name, self.a_off, self.a_size)
        v = self.arena[0:shape[0], off // 2:off // 2 + n * esz // 2]
        if dt == F32:
            v = v.bitcast(F32)
        if len(shape) > 2:
            names = [f"d{i}" for i in range(len(shape) - 1)]
            v = v.rearrange("p (" + " ".join(names) + ") -> p " + " ".join(names),
                            **{nm: int(sz) for nm, sz in zip(names[1:], shape[2:])})
        return T(v, name)

    def mark(self):
        return self.a_off

    def release(self, mark):
        self.a_off = mark

    def barrier(self):
        snap = [(k, v) for k, v in self.cnt.items() if v > 0] + [(k, v) for k, v in self.dcnt.items() if v > 0]
        sems = self.sems
        for e in self.q:
            waits = []
            for k, v in snap:
                if k == e and e in (PE, SP):
                    continue
                if self.waited[e].get(k, 0) >= v:
                    continue
                self.waited[e][k] = v
                waits.append((k, v))

            def run(E, waits=waits):
                for k, v in waits:
                    E.wait_ge(sems[k], v)
            self.q[e].append(run)

    def dsem(self, key):
        if key not in self.sems:
            self.sems[key] = self.es.enter_context(self.nc.semaphore("d_" + key))
            self.dcnt[key] = 0
        return key

    def _waits(self, eng, r, w):
        need = {}

        def add(ev):
            if ev is None:
                return
            k, v = ev
            if need.get(k, 0) < v:
                need[k] = v
        for t in r:
            add(t.w)
        for t in w:
            add(t.w)
            for k, v in t.r.items():
                add((k, v))
        out = []
        for k, v in need.items():
            if k == eng and eng == PE:
                continue
            if self.waited[eng].get(k, 0) >= v:
                continue
            self.waited[eng][k] = v
            out.append((k, v))
        return out

    def _commit(self, ev, r, w):
        for t in w:
            t.w = ev
            t.r = {}
        for t in r:
            if t.r.get(ev[0], 0) < ev[1]:
                t.r[ev[0]] = ev[1]

    def op(self, eng, fn, r=(), w=()):
        r = [_trk(x) for x in r]
        w = [_trk(x) for x in w]
        waits = self._waits(eng, r, w)
        self.cnt[eng] += 1
        ev = (eng, self.cnt[eng])
        sems = self.sems

        def run(E, waits=waits, fn=fn, s=sems[eng]):
            for k, v in waits:
                E.wait_ge(sems[k], v)
            fn(E).then_inc(s, 1)
        self.q[eng].append(run)
        self._commit(ev, r, w)
        self.n_ops += 1

    def dma(self, eng, out, in_, r=(), w=(), sem=None, **kw):
        r = [_trk(x) for x in r]
        w = [_trk(x) for x in w]
        key = self.dsem(sem)
        waits = self._waits(eng, r, w)
        self.dcnt[key] += 16
        ev = (key, self.dcnt[key])
        sems = self.sems

        def run(E, waits=waits, s=sems[key]):
            for k, v in waits:
                E.wait_ge(sems[k], v)
            E.dma_start(out=out, in_=in_, **kw).then_inc(s, 16)
        self.q[eng].append(run)
        self._commit(ev, r, w)
        self.n_ops += 1

    def collective(self, kind, op, groups, in_ap, out_ap, r, w, sem):
        r = [_trk(x) for x in r]
        w = [_trk(x) for x in w]
        key = self.dsem(sem)
        waits = self._waits(POOL, r, w)
        self.dcnt[key] += 1
        ev = (key, self.dcnt[key])
        sems = self.sems

        def run(E, waits=waits, s=sems[key]):
            for k, v in waits:
                E.wait_ge(sems[k], v)
            E.collective_compute(kind, op, replica_groups=groups, ins=[in_ap], outs=[out_ap]).then_inc(s)
        self.q[POOL].append(run)
        self._commit(ev, r, w)

    def finish(self, final_trks):
        final = [_trk(x) for x in final_trks]
        waits = self._waits(SP, final, final)
        waits += [(k, v) for k, v in self.dcnt.items() if v > 0]
        sems = self.sems

        def run(E):
            for k, v in waits:
                E.wait_ge(sems[k], v)
        self.q[SP].append(run)
        block = self.es.enter_context(self.nc.Block())
        q = self.q

        @block.sync
        def _(E):
            for f in q[SP]:
                f(E)

        @block.scalar
        def _(E):
            for f in q[ACT]:
                f(E)

        @block.vector
        def _(E):
            for f in q[DVE]:
                f(E)

        @block.gpsimd
        def _(E):
            for f in q[POOL]:
                f(E)

        @block.tensor
        def _(E):
            for f in q[PE]:
                f(E)
        self.es.close()
        return self.nc


def build(stage=99):
    kb = KB()
    nc = kb.nc
    op, dma = kb.op, kb.dma
    kb.init_arena(207 * 1024)
    sb = kb.alloc

    def din(name, shape, dt=F32):
        return nc.dram_tensor(name, list(shape), dt, kind="ExternalInput").ap()

    def dout(name, shape, dt=F32):
        return T(nc.dram_tensor(name, list(shape), dt, kind="ExternalOutput").ap(), name)

    xp = din("xp", [1024, D])
    cvT = din("cvT", [128, KC, 2])
    ada_w = din("ada_w", [2, D, 3 * D])
    ada_bT = din("ada_bT", [2, 128, 24])
    ada_b = din("ada_b", [2, 3 * D])
    post_g = din("post_ln_g", [2, D])
    post_b = din("post_ln_b", [2, D])
    w_in = din("ev_w_in", [D, 3 * D])
    w_fnet = din("ev_w_fnet", [4, 128, 128])
    w_out0 = din("ev_w_out", [D, D])
    dftc_d = din("dftc", [128, 256])
    dftl_d = din("dftl256", [256, 512])
    rkvz_d = din("rw_w_rkvz", [4, D, D])
    w1_d = din("rw_w1", [2, D, 64])
    w2_d = din("rw_w2", [2, 64, D])
    a1_d = din("rw_a1", [2, D, 64])
    a2_d = din("rw_a2", [2, 64, D])
    g1_d = din("rw_g1", [D, 128])
    g2_d = din("rw_g2", [128, D])
    w0_d = din("rw_w0", [2, D])
    wout1_d = din("rw_w_out", [D, D])
    vecs_d = din("rw_vecs", [128, 14, KC])
    tri_d = din("tri", [128, 2, 128])
    mask4_d = din("mask4", [128, 2, 512])
    namask_d = din("namask", [128, 2, 256])
    bo_d = din("blockones", [128, 2, 128])
    xs_win = din("xs_win", [1536, D])
    ck_d = din("cache_k", [512, 512])
    cv_d = din("cache_v", [512, 512])
    bb_d = din("na_bias", [8, 128, 30 * 64])
    qmask_d = din("na_qmask", [24, 1024])
    rowoh_d = din("na_rowoh", [24, 1536])
    dfts_d = din("dfts", [4096, 2, 1024], BF16)
    st0_d = din("state0", [2, 16, 64, 64])
    selc_d = din("selc", [64, 2, 4])
    selh_d = din("selh", [8, 2])
    o_ys = dout("o_ys", [1024, D])
    o_yp = dout("o_yp", [1024, D])
    o_state = dout("o_state", [4, 2, 16, 64, 64])
    o_newk = dout("o_newk", [1024, 512])
    o_newv = dout("o_newv", [1024, 512])
    outs_final = [o_newk, o_newv]

    def dump(name, tile_ap, shape, src_trk):
        o = dout("dbg_" + name, shape)
        dma(POOL, o[:], tile_ap, r=(src_trk if isinstance(src_trk, (list, tuple)) else [src_trk]), sem="dbg_" + name)

    w_in_bf = kb.dram("w_in_bf", [D, 3 * D], BF16)
    wbf_d = [kb.dram(f"wbf_d{i}", [D, D], BF16) for i in range(5)]

    pbs = [kb.ps(f"pb{i}", [128, 512], F32) for i in range(6)]
    ptrs = [kb.ps(f"ptr{i}", [128, D], BF16) for i in range(2)]
    pctr = [0, 0]

    def npb():
        pctr[0] += 1
        return pbs[pctr[0] % 6]

    def nptr():
        pctr[1] += 1
        return ptrs[pctr[1] % 2]

    ident = sb("ident", [128, 128], BF16)
    consts = sb("consts", [128, 4], F32)
    ones_bf = sb("ones_bf", [128, 128], BF16)
    ones_row = sb("ones_row", [1, 128], F32)
    row_stage = sb("row_stage", [1, 512], F32)
    modT = [sb(f"modT{l}", [128, 2, 16], F32) for l in range(2)]
    gate1 = [None, sb("gate1_1", [128, 2, D], F32)]
    lng = sb("lng", [128, D], F32)
    lnb = sb("lnb", [128, D], F32)
    xt = [sb(f"xt{i}", [128, D], F32) for i in range(2)]
    xn = [sb("xn0", [128, D], BF16)] * 2
    stats = [sb(f"stats{i}", [128, 2, 6], F32) for i in range(2)]
    mv = [sb(f"mv{i}", [128, 4], F32) for i in range(2)]
    ybuf = [sb(f"ybuf{i}", [128, D], F32) for i in range(2)]
    mark_base = kb.mark()
    gate1[0] = sb("gate1_0", [128, 2, D], F32)
    wout0 = sb("wout0", [128, KC, D], BF16)
    dftc = sb("dftc_sb", [128, 256], BF16)
    wf_sb = sb("wf_sb", [128, 4, 128], BF16)
    csw = sb("csw", [128, 4, 256], BF16)
    mark_l0 = kb.mark()
    kv_out = [sb(f"kv_out{i}", [128, 512], F32) for i in range(2)]
    w_in_sb = sb("w_in_sb", [128, KC, 3 * D], BF16)
    mark_ph = kb.mark()

    op(POOL, lambda E: E.memset(ident[:], 0.0), w=[ident])
    op(POOL, lambda E: E.affine_select(out=ident[:], in_=ident[:], pattern=[[-1, 128]], compare_op=ALU.not_equal,
                                       fill=1.0, base=0, channel_multiplier=1), r=[ident], w=[ident])
    for i, val in enumerate((LN_EPS, 64e-5, 1e-12, 1.0)):
        op(POOL, lambda E, i=i, val=val: E.memset(consts[:, i:i + 1], val), w=[consts])
    op(POOL, lambda E: E.memset(ones_bf[:], 1.0), w=[ones_bf])
    op(POOL, lambda E: E.memset(ones_row[:], 1.0), w=[ones_row])

    def bcast_row(dst, dst_ap, src_row_ap, n):
        for c0 in range(0, n, 512):
            cw = min(512, n - c0)
            dma(SP, row_stage[0:1, 0:cw], src_row_ap[:, c0:c0 + cw], w=[row_stage], sem="row_stage")
            p = npb()
            op(PE, lambda E, cw=cw, p=p: E.matmul(p[:, 0:cw], lhsT=ones_row[0:1, :], rhs=row_stage[0:1, 0:cw],
                                                  start=True, stop=True), r=[ones_row, row_stage], w=[p])
            op(ACT, lambda E, c0=c0, cw=cw, p=p: E.activation(out=dst_ap[:, c0:c0 + cw], in_=p[:, 0:cw], func=AF.Identity),
               r=[p], w=[dst])

    cv_sb = sb("cv_sb", [128, KC, 2], F32)
    sc = sb("sc", [128, KC, 2], BF16)
    sc_rep = sb("sc_rep", [128, KC, 2, 128], BF16)
    adab_sb = sb("adab_sb", [128, 2, 24], F32)
    gate_b = sb("gate_b", [128, D], F32)
    adaw = [sb(f"adaw{i}", [128, KC, 512], BF16) for i in range(2)]
    dma(SP, cv_sb[:], cvT[:, :, :], w=[cv_sb], sem="c_cv")
    op(ACT, lambda E: E.activation(out=sc[:], in_=cv_sb[:], func=AF.Silu), r=[cv_sb], w=[sc])
    for kc in range(KC):
        for v in range(2):
            op(DVE, lambda E, kc=kc, v=v: E.tensor_copy(out=sc_rep[:, kc, v, :],
                                                         in_=sc[:, kc, v:v + 1].to_broadcast([128, 128])),
               r=[sc], w=[sc_rep])
    dma(SP, adab_sb[:], ada_bT.rearrange("l p j -> p l j"), w=[adab_sb], sem="c_adab")
    bcast_row(lng, lng[:], post_g[0:1, :], D)
    bcast_row(lnb, lnb[:], post_b[0:1, :], D)
    for l in range(2):
        bcast_row(gate_b, gate_b[:], ada_b[l:l + 1, 2 * D:3 * D], D)
        ps_modT = npb()
        for nch in range(6):
            wb = adaw[(l * 6 + nch) % 2]
            dma(POOL, wb[:], ada_w[l, :, nch * 512:(nch + 1) * 512].rearrange("(kc p) n -> p kc n", p=128),
                w=[wb], sem=wb.name)
            if nch < 4:
                for jj in range(4):
                    j = nch * 4 + jj
                    for kc in range(KC):
                        op(PE, lambda E, kc=kc, jj=jj, j=j, wb=wb, ps_modT=ps_modT: E.matmul(
                            ps_modT[:, 2 * j:2 * j + 2], lhsT=wb[:, kc, jj * 128:(jj + 1) * 128], rhs=sc[:, kc, :],
                            start=(kc == 0), stop=(kc == KC - 1)), r=[wb, sc], w=[ps_modT])
                if nch == 3:
                    for v in range(2):
                        op(DVE, lambda E, l=l, v=v, ps_modT=ps_modT: E.tensor_tensor(
                            out=modT[l][:, v, :], in0=ps_modT[:, 0:32].rearrange("p (j v) -> p v j", v=2)[:, v, :],
                            in1=adab_sb[:, l, 0:16], op=ALU.add), r=[ps_modT, adab_sb], w=[modT[l]])
                    op(DVE, lambda E, l=l: E.tensor_scalar_add(out=modT[l][:, :, 8:16], in0=modT[l][:, :, 8:16],
                                                               scalar1=1.0), r=[modT[l]], w=[modT[l]])
            else:
                for v in range(2):
                    ps_mod = npb()
                    for kc in range(KC):
                        op(PE, lambda E, kc=kc, v=v, wb=wb, ps_mod=ps_mod: E.matmul(
                            ps_mod[:, :], lhsT=sc_rep[:, kc, v, :], rhs=wb[:, kc, :],
                            start=(kc == 0), stop=(kc == KC - 1)), r=[wb, sc_rep], w=[ps_mod])
                    c0 = (nch - 4) * 512
                    op(DVE, lambda E, l=l, v=v, c0=c0, ps_mod=ps_mod: E.scalar_tensor_tensor(
                        out=gate1[l][:, v, c0:c0 + 512], in0=ps_mod[:, :], scalar=1.0,
                        in1=gate_b[:, c0:c0 + 512], op0=ALU.add, op1=ALU.add),
                       r=[ps_mod, gate_b], w=[gate1[l]])
        if l == 0:
            for nch in range(6):
                dma(POOL, w_in_sb[:, :, nch * 512:(nch + 1) * 512],
                    w_in[:, nch * 512:(nch + 1) * 512].rearrange("(kc p) n -> p kc n", p=128),
                    w=[w_in_sb.sub(nch)], sem=f"w_in{nch}")
            dma(POOL, dftc[:], dftc_d[:, :], w=[dftc], sem="c_dftc")
            dma(POOL, wf_sb[:], w_fnet.rearrange("g c e -> c g e"), w=[wf_sb], sem="c_wf")
            dma(POOL, wout0[:], w_out0.rearrange("(kc p) n -> p kc n", p=128), w=[wout0], sem="wout0")
            dma(POOL, w_in_bf[:, :], w_in[:, :], w=[w_in_bf], sem="wcast0")
            for i5 in range(5):
                dma(POOL, wbf_d[i5][:, :], (wout1_d if i5 == 4 else rkvz_d[i5]), w=[wbf_d[i5]], sem=f"wcast{1 + i5}")
    for half in range(2):
        p = npb()
        for gg in range(2):
            g = half * 2 + gg
            for cs in range(2):
                op(PE, lambda E, p=p, g=g, gg=gg, cs=cs: E.matmul(
                    p[:, gg * 256 + cs * 128:gg * 256 + (cs + 1) * 128], lhsT=dftc[:, cs * 128:(cs + 1) * 128],
                    rhs=wf_sb[:, g, :], start=True, stop=True), r=[dftc, wf_sb], w=[p])
        op(ACT, lambda E, p=p, half=half: E.activation(
            out=csw[:, half * 2:half * 2 + 2, :].rearrange("p g e -> p (g e)"), in_=p[:, :], func=AF.Identity),
           r=[p], w=[csw])

    if stage == 0:
        dump("modT0", modT[0][:], [128, 2, 16], modT[0])
        dump("gate1_0", gate1[0][:], [128, 2, D], gate1[0])
        dump("gate1_1", gate1[1][:], [128, 2, D], gate1[1])
        return kb.finish(outs_final)
    kb.barrier()
    kb.release(mark_ph)

    lctr = [0]

    def ln_stats(x, st, m, eps_col=0):
        for h in range(2):
            op(DVE, lambda E, h=h: E.bn_stats(out=st[:, h, :], in_=x[:, h * 512:(h + 1) * 512]), r=[x], w=[st])
        op(DVE, lambda E: E.bn_aggr(out=m[:, 0:2], in_=st[:]), r=[st], w=[m])
        op(ACT, lambda E: E.activation(out=m[:, 2:3], in_=m[:, 1:2], func=AF.Sqrt, bias=consts[:, eps_col:eps_col + 1],
                                       scale=1.0), r=[m, consts], w=[m])
        op(DVE, lambda E: E.reciprocal(out=m[:, 2:3], in_=m[:, 2:3]), r=[m], w=[m])
        op(DVE, lambda E: E.scalar_tensor_tensor(out=m[:, 3:4], in0=m[:, 0:1], scalar=-1.0, in1=m[:, 2:3],
                                                 op0=ALU.mult, op1=ALU.mult), r=[m], w=[m])

    def ln_transpose(x_ap, x_deps, mod, v, uT, col0, utrk):
        i = lctr[0] % 2
        lctr[0] += 1
        x, xnb, st, m = xt[i], xn[i], stats[i], mv[i]
        pst = nptr()
        dma(SP, x[:], x_ap, r=x_deps, w=[x], sem=x.name)
        ln_stats(x, st, m)
        op(ACT, lambda E: E.activation(out=xnb[:], in_=x[:], func=AF.Identity, scale=m[:, 2:3], bias=m[:, 3:4]),
           r=[x, m], w=[xnb])
        for kc in range(KC):
            op(PE, lambda E, kc=kc: E.transpose(pst[:, kc * 128:(kc + 1) * 128], xnb[:, kc * 128:(kc + 1) * 128],
                                                ident[:]), r=[xnb, ident], w=[pst])
        for kc in range(KC):
            op(DVE, lambda E, kc=kc: E.tensor_scalar(
                out=uT[:, kc, col0:col0 + 128], in0=pst[:, kc * 128:(kc + 1) * 128],
                scalar1=mod[:, v, 8 + kc:9 + kc], scalar2=mod[:, v, kc:kc + 1], op0=ALU.mult, op1=ALU.add),
               r=[pst, mod], w=[utrk])

    uT_p = sb("uT_p", [128, KC, 1024], BF16)
    ocat = uT_p
    vtok_p = sb("vtok_p", [128, 8, 512], BF16)
    aT = sb("aT", [128, 4, 1024], BF16)
    sza = sb("sza", [128, 4, 1024], BF16)
    qT = sb("qT", [128, 4, 1024], BF16)
    kT = sb("kT", [128, 4, 1024], BF16)
    szb = sb("szb", [128, 4, 1024], BF16)
    A1 = sb("A1", [128, 8, 4, 256], BF16)
    dftl = sb("dftl_sb", [128, 2, 512], BF16)
    pTs = [sb(f"pT{i}", [128, 512], BF16) for i in range(2)]
    recs = [sb(f"rec{i}", [128, 256], F32) for i in range(2)]
    dma(POOL, dftl[:], dftl_d.rearrange("(lt p) m -> p lt m", p=128), w=[dftl], sem="const3")
    for t in range(8):
        ln_transpose(xp[t * 128:(t + 1) * 128, :], [], modT[0], 0, uT_p, t * 128, uT_p.sub(t // 4))

    n_kv = 0
    for t in range(8):
        for which, c0, od in ((0, 1536, o_newk), (1, 2048, o_newv)):
            pa = npb()
            ko = kv_out[n_kv % 2]
            n_kv += 1
            for kc in range(KC):
                op(PE, lambda E, kc=kc, t=t, c0=c0, pa=pa: E.matmul(
                    pa[:, :], lhsT=uT_p[:, kc, t * 128:(t + 1) * 128], rhs=w_in_sb[:, kc, c0:c0 + 512],
                    start=(kc == 0), stop=(kc == KC - 1)),
                   r=[uT_p.sub(t // 4), w_in_sb.sub(c0 // 512)], w=[pa])
            op(ACT, lambda E, pa=pa, ko=ko: E.activation(out=ko[:], in_=pa[:, :], func=AF.Identity), r=[pa], w=[ko])
            if which == 1:
                op(POOL, lambda E, ko=ko, t=t: E.tensor_copy(out=vtok_p[:, t, :], in_=ko[:]),
                   r=[ko], w=[vtok_p.sub(t)])
            dma(SP, od[t * 128:(t + 1) * 128, :], ko[:], r=[ko], sem=ko.name)

    for blk in range(2):
        c0 = blk * 512
        for j in list(range(0, 16)) + list(range(20, 24)):
            p = npb()
            for kc in range(KC):
                op(PE, lambda E, kc=kc, j=j, c0=c0, p=p: E.matmul(
                    p[:, :], lhsT=w_in_sb[:, kc, j * 128:(j + 1) * 128], rhs=uT_p[:, kc, c0:c0 + 512],
                    start=(kc == 0), stop=(kc == KC - 1)), r=[w_in_sb.sub(j // 4), uT_p.sub(blk)], w=[p])
            grp, idx = j // 4, j % 4
            if grp == 0:
                op(DVE, lambda E, p=p, idx=idx, c0=c0: E.tensor_copy(out=aT[:, idx, c0:c0 + 512], in_=p[:, :]),
                   r=[p], w=[aT.sub(blk)])
            elif grp == 1:
                op(ACT, lambda E, p=p, idx=idx, c0=c0: E.activation(out=sza[:, idx, c0:c0 + 512], in_=p[:, :], func=AF.Silu),
                   r=[p], w=[sza.sub(blk)])
            elif grp == 2:
                op(DVE, lambda E, p=p, idx=idx, c0=c0: E.tensor_scalar_mul(out=qT[:, idx, c0:c0 + 512], in0=p[:, :],
                                                                            scalar1=0.125), r=[p], w=[qT.sub(blk)])
            elif grp == 3:
                op(DVE, lambda E, p=p, idx=idx, c0=c0: E.tensor_copy(out=kT[:, idx, c0:c0 + 512], in_=p[:, :]),
                   r=[p], w=[kT.sub(blk)])
            else:
                op(ACT, lambda E, p=p, idx=idx, c0=c0: E.activation(out=szb[:, idx, c0:c0 + 512], in_=p[:, :], func=AF.Silu),
                   r=[p], w=[szb.sub(blk)])

    for t in range(8):
        for half in range(2):
            p = npb()
            for gg in range(2):
                g = half * 2 + gg
                op(PE, lambda E, p=p, g=g, gg=gg, t=t: E.matmul(
                    p[:, gg * 256:(gg + 1) * 256], lhsT=aT[:, g, t * 128:(t + 1) * 128], rhs=csw[:, g, :],
                    start=True, stop=True), r=[aT.sub(t // 4), csw], w=[p])
            eng = ACT if half == 0 else DVE
            if eng == ACT:
                op(ACT, lambda E, p=p, t=t, half=half: E.activation(
                    out=A1[:, t, half * 2:half * 2 + 2, :].rearrange("p g e -> p (g e)"), in_=p[:, :], func=AF.Identity),
                   r=[p], w=[A1.sub(t)])
            else:
                op(DVE, lambda E, p=p, t=t, half=half: E.tensor_copy(
                    out=A1[:, t, half * 2:half * 2 + 2, :].rearrange("p g e -> p (g e)"), in_=p[:, :]),
                   r=[p], w=[A1.sub(t)])
    for s in range(4):
        for g in range(4):
            p = npb()
            n = 0
            for lt in range(2):
                t = 2 * s + lt
                for cs in range(2):
                    op(PE, lambda E, p=p, t=t, g=g, lt=lt, cs=cs, n=n: E.matmul(
                        p[:, 0:256], lhsT=A1[:, t, g, cs * 128:(cs + 1) * 128], rhs=dftl[:, lt, cs * 256:(cs + 1) * 256],
                        start=(n == 0), stop=(n == 3)), r=[A1.sub(t), dftl], w=[p])
                    n += 1
            op(DVE, lambda E, p=p, g=g, s=s: E.tensor_tensor(
                out=ocat[:, g, s * 256:(s + 1) * 256], in0=p[:, 0:256], in1=sza[:, g, s * 256:(s + 1) * 256], op=ALU.mult),
               r=[p, sza.sub(s // 2)], w=[ocat.sub(s // 2)])

    na = 0
    for s in range(4):
        q0 = s * 256
        for hp in range(4):
            po = npb()
            for e2 in range(2):
                h = 2 * hp + e2
                lo = 64 * e2
                pss = npb()
                pT = pTs[na % 2]
                na += 1
                for kc2 in range(2):
                    op(PE, lambda E, pss=pss, kc2=kc2, lo=lo, hp=hp, q0=q0: E.matmul(
                        pss[:, kc2 * 256:(kc2 + 1) * 256], lhsT=kT[lo:lo + 64, hp, q0 + kc2 * 128:q0 + (kc2 + 1) * 128],
                        rhs=qT[lo:lo + 64, hp, q0:q0 + 256], start=True, stop=True),
                       r=[kT.sub(s // 2), qT.sub(s // 2)], w=[pss])
                op(ACT, lambda E, pss=pss, pT=pT: E.activation(out=pT[:], in_=pss[:, :], func=AF.Exp), r=[pss], w=[pT])
                for kc2 in range(2):
                    op(PE, lambda E, po=po, kc2=kc2, lo=lo, h=h, s=s, pT=pT: E.matmul(
                        po[lo:lo + 64, 0:256], lhsT=vtok_p[:, 2 * s + kc2, h * 64:(h + 1) * 64],
                        rhs=pT[:, kc2 * 256:(kc2 + 1) * 256], start=(kc2 == 0), stop=(kc2 == 1)),
                       r=[vtok_p.sub(2 * s + kc2), pT], w=[po])
                for kc2 in range(2):
                    op(PE, lambda E, po=po, kc2=kc2, lo=lo, pT=pT: E.matmul(
                        po[lo:lo + 64, 256:512], lhsT=ones_bf[:, 0:64],
                        rhs=pT[:, kc2 * 256:(kc2 + 1) * 256], start=(kc2 == 0), stop=(kc2 == 1)),
                       r=[ones_bf, pT], w=[po])
            rec = recs[(s * 4 + hp) % 2]
            op(DVE, lambda E, po=po, rec=rec: E.reciprocal(out=rec[:], in_=po[:, 256:512]), r=[po], w=[rec])
            op(POOL, lambda E, rec=rec, hp=hp, q0=q0: E.tensor_mul(out=rec[:], in0=rec[:], in1=szb[:, hp, q0:q0 + 256]),
               r=[rec, szb.sub(s // 2)], w=[rec])
            op(DVE, lambda E, po=po, rec=rec, hp=hp, q0=q0: E.tensor_tensor(
                out=ocat[:, 4 + hp, q0:q0 + 256], in0=po[:, 0:256], in1=rec[:], op=ALU.mult),
               r=[po, rec], w=[ocat.sub(s // 2)])

    if stage == 2:
        x1d = dout("dbg_x1", [2048, D])
    else:
        x1d = kb.dram("x1d", [2048, D], F32)

    def w0fn(nh):
        return wout0, wout0[:, :, nh * 512:(nh + 1) * 512]

    def out_proj_ln(t_glob, x_ap, oc, col0, octrk, wfn, layer, v):
        i = lctr[0] % 2
        lctr[0] += 1
        x, st, m, y = xt[i], stats[i], mv[i], ybuf[i]
        dma(SP, x[:], x_ap, w=[x], sem=x.name)
        for nh in range(2):
            p = npb()
            wt_, wap_ = wfn(nh)
            for kc in range(KC):
                op(PE, lambda E, kc=kc, nh=nh, p=p, wap_=wap_: E.matmul(
                    p[:, :], lhsT=oc[:, kc, col0:col0 + 128], rhs=wap_[:, kc, :],
                    start=(kc == 0), stop=(kc == KC - 1)), r=[octrk, wt_], w=[p])
            op(DVE, lambda E, nh=nh, p=p: E.tensor_tensor(
                out=y[:, nh * 512:(nh + 1) * 512], in0=p[:, :], in1=gate1[layer][:, v, nh * 512:(nh + 1) * 512],
                op=ALU.mult), r=[p, gate1[layer]], w=[y])
        op(DVE, lambda E: E.scalar_tensor_tensor(out=y[:], in0=x[:], scalar=ALPHA, in1=y[:], op0=ALU.mult, op1=ALU.add),
           r=[x, y], w=[y])
        ln_stats(y, st, m)
        op(ACT, lambda E: E.activation(out=y[:], in_=y[:], func=AF.Identity, scale=m[:, 2:3], bias=m[:, 3:4]),
           r=[y, m], w=[y])
        op(POOL, lambda E: E.tensor_mul(out=y[:], in0=y[:], in1=lng[:]), r=[y, lng], w=[y])
        op(POOL, lambda E: E.tensor_add(out=y[:], in0=y[:], in1=lnb[:]), r=[y, lnb], w=[y])
        return y

    for t in range(8):
        y = out_proj_ln(t, xp[t * 128:(t + 1) * 128, :], ocat, t * 128, ocat.sub(t // 4), w0fn, 0, 0)
        dma(SP, x1d[t * 128:(t + 1) * 128, :], y[:], r=[y], w=[x1d.sub(t)], sem=y.name)
    kb.barrier()
    kb.release(mark_l0)
    GROUPS = [[0, 1, 2, 3], [4, 5, 6, 7]]
    gin = [kb.dram(f"gin{i}", [512, 1024], BF16) for i in range(2)]
    gout = [kb.dram(f"gout{i}", [2048, 1024], BF16) for i in range(2)]
    ocs = sb("ocs", [128, KC, 1024], BF16)
    sza_s = sb("sza_s", [128, 4, 1024], BF16)
    szb_s = sb("szb_s", [128, 4, 1024], BF16)
    qTa = sb("qTa", [128, 8, 1024], BF16)
    kTa = sb("kTa", [128, 8, 1536], BF16)
    vwin = sb("vwin", [128, 12, 512], BF16)
    mark_s1 = kb.mark()
    wch = [sb(f"wch{i}", [128, KC, 512], BF16) for i in range(2)]
    uT_s = sb("uT_s", [128, KC, 1536], BF16)
    aTb = sb("aTb", [128, 4, 512], BF16)
    a1st = [sb(f"a1st{i}", [128, 1024], BF16) for i in range(2)]
    for h in range(8):
        dma(POOL, qTa[64:88, h, :], qmask_d[:, :], w=[qTa.sub("aug")], sem="aug_q")
        dma(POOL, kTa[64:88, h, :], rowoh_d[:, :], w=[kTa.sub("aug")], sem="aug_k")
    for t in range(12):
        ln_transpose(xs_win[t * 128:(t + 1) * 128, :], [], modT[0], 1, uT_s, t * 128, uT_s.sub(t // 4))
    SCUT = os.environ.get('SCUT', '')
    if SCUT == 'ln':
        return kb.finish(outs_final)
    wcc = [0]

    def load_wch(nch):
        wb = wch[wcc[0] % 2]
        wcc[0] += 1
        dma(SP, wb[:], w_in_bf[:, nch * 512:(nch + 1) * 512].rearrange("(kc p) n -> p kc n", p=128), r=[w_in_bf], w=[wb],
            sem=wb.name)
        return wb

    def fm4(wb, wblk, evac):
        for jj in range(4):
            p = npb()
            for kc in range(KC):
                op(PE, lambda E, kc=kc, jj=jj, p=p: E.matmul(
                    p[:, :], lhsT=wb[:, kc, jj * 128:(jj + 1) * 128], rhs=uT_s[:, kc, wblk * 512:(wblk + 1) * 512],
                    start=(kc == 0), stop=(kc == KC - 1)), r=[wb, uT_s.sub(wblk)], w=[p])
            evac(jj, p)

    wb = load_wch(0)
    na1 = 0
    for blk in range(2):
        for jj in range(4):
            p = npb()
            for kc in range(KC):
                op(PE, lambda E, kc=kc, jj=jj, p=p, blk=blk, wb=wb: E.matmul(
                    p[:, :], lhsT=wb[:, kc, jj * 128:(jj + 1) * 128], rhs=uT_s[:, kc, 256 + blk * 512:256 + (blk + 1) * 512],
                    start=(kc == 0), stop=(kc == KC - 1)), r=[wb, uT_s.sub(0), uT_s.sub(1), uT_s.sub(2)], w=[p])
            op(DVE, lambda E, jj=jj, p=p: E.tensor_copy(out=aTb[:, jj, :], in_=p[:, :]), r=[p], w=[aTb])
        for tt_ in range(4):
            stg = a1st[na1 % 2]
            na1 += 1
            for half in range(2):
                p = npb()
                for gg in range(2):
                    g = half * 2 + gg
                    op(PE, lambda E, p=p, g=g, gg=gg, tt_=tt_: E.matmul(
                        p[:, gg * 256:(gg + 1) * 256], lhsT=aTb[:, g, tt_ * 128:(tt_ + 1) * 128], rhs=csw[:, g, :],
                        start=True, stop=True), r=[aTb, csw], w=[p])
                op(ACT, lambda E, p=p, half=half, stg=stg: E.activation(out=stg[:, half * 512:(half + 1) * 512], in_=p[:, :],
                                                                        func=AF.Identity), r=[p], w=[stg])
            row0 = tt_ * 128
            dma(SP, gin[blk][row0:row0 + 128, :], stg[:], r=[stg], w=[gin[blk].sub(row0)], sem=stg.name)
        kb.collective("AllGather", ALU.bypass, GROUPS, gin[blk][:, :], gout[blk][:, :],
                      r=[gin[blk].sub(r0) for r0 in range(0, 512, 128)] + ([gout[0]] if blk else []),
                      w=[gout[blk]], sem=f"ag1_{blk}")

    if SCUT == 'a1':
        dump('aTb', aTb[:], [128, 4, 512], [aTb])
        dump('csw', csw[:], [128, 4, 256], [csw])
        dump('gin0', gin[0][:, :], [512, 1024], [gin[0].sub(r0) for r0 in range(0, 512, 128)])
        return kb.finish(outs_final)

    def own_blocks(fn):
        for blk in range(2):
            fn(blk)

    wb = load_wch(1)
    for blk in range(2):
        for jj in range(4):
            p = npb()
            for kc in range(KC):
                op(PE, lambda E, kc=kc, jj=jj, p=p, blk=blk, wb=wb: E.matmul(
                    p[:, :], lhsT=wb[:, kc, jj * 128:(jj + 1) * 128], rhs=uT_s[:, kc, 256 + blk * 512:256 + (blk + 1) * 512],
                    start=(kc == 0), stop=(kc == KC - 1)), r=[wb, uT_s.sub(0), uT_s.sub(1), uT_s.sub(2)], w=[p])
            op(ACT, lambda E, jj=jj, p=p, blk=blk: E.activation(out=sza_s[:, jj, blk * 512:(blk + 1) * 512], in_=p[:, :],
                                                                func=AF.Silu), r=[p], w=[sza_s])
    wb = load_wch(2)
    for blk in range(2):
        for h in range(8):
            p = npb()
            for kc in range(KC):
                op(PE, lambda E, kc=kc, h=h, p=p, blk=blk, wb=wb: E.matmul(
                    p[0:64, :], lhsT=wb[:, kc, h * 64:(h + 1) * 64], rhs=uT_s[:, kc, 256 + blk * 512:256 + (blk + 1) * 512],
                    start=(kc == 0), stop=(kc == KC - 1)), r=[wb, uT_s.sub(0), uT_s.sub(1), uT_s.sub(2)], w=[p])
            op(DVE, lambda E, h=h, p=p, blk=blk: E.tensor_scalar_mul(out=qTa[0:64, h, blk * 512:(blk + 1) * 512],
                                                                      in0=p[0:64, :], scalar1=0.125), r=[p], w=[qTa.sub(h)])
    wb = load_wch(3)
    for wblk in range(3):
        for h in range(8):
            p = npb()
            for kc in range(KC):
                op(PE, lambda E, kc=kc, h=h, p=p, wblk=wblk, wb=wb: E.matmul(
                    p[0:64, :], lhsT=wb[:, kc, h * 64:(h + 1) * 64], rhs=uT_s[:, kc, wblk * 512:(wblk + 1) * 512],
                    start=(kc == 0), stop=(kc == KC - 1)), r=[wb, uT_s.sub(wblk)], w=[p])
            op(ACT, lambda E, h=h, p=p, wblk=wblk: E.activation(out=kTa[0:64, h, wblk * 512:(wblk + 1) * 512], in_=p[0:64, :],
                                                                func=AF.Identity), r=[p], w=[kTa.sub(h)])
    wb = load_wch(4)
    for t in range(12):
        p = npb()
        for kc in range(KC):
            op(PE, lambda E, kc=kc, t=t, p=p, wb=wb: E.matmul(
                p[:, :], lhsT=uT_s[:, kc, t * 128:(t + 1) * 128], rhs=wb[:, kc, :],
                start=(kc == 0), stop=(kc == KC - 1)), r=[wb, uT_s.sub(t // 4)], w=[p])
        op(ACT if t % 2 else DVE, (lambda E, t=t, p=p: E.activation(out=vwin[:, t, :], in_=p[:, :], func=AF.Identity)) if t % 2
           else (lambda E, t=t, p=p: E.tensor_copy(out=vwin[:, t, :], in_=p[:, :])), r=[p], w=[vwin])
    wb = load_wch(5)
    for blk in range(2):
        for jj in range(4):
            p = npb()
            for kc in range(KC):
                op(PE, lambda E, kc=kc, jj=jj, p=p, blk=blk, wb=wb: E.matmul(
                    p[:, :], lhsT=wb[:, kc, jj * 128:(jj + 1) * 128], rhs=uT_s[:, kc, 256 + blk * 512:256 + (blk + 1) * 512],
                    start=(kc == 0), stop=(kc == KC - 1)), r=[wb, uT_s.sub(0), uT_s.sub(1), uT_s.sub(2)], w=[p])
            op(ACT, lambda E, jj=jj, p=p, blk=blk: E.activation(out=szb_s[:, jj, blk * 512:(blk + 1) * 512], in_=p[:, :],
                                                                func=AF.Silu), r=[p], w=[szb_s])
    if SCUT == 'proj':
        dump('gin0', gin[0][:, :], [512, 1024], [gin[0].sub(r0) for r0 in range(0, 512, 128)])
        dump('csw', csw[:], [128, 4, 256], [csw])
        dump('aTb', aTb[:], [128, 4, 512], [aTb])
        return kb.finish(outs_final)
    kb.barrier()
    kb.release(mark_s1)

    BBt = [sb(f"BBt{i}", [128, 30 * 64], BF16) for i in range(2)]
    kc_tok = sb("kc_tok", [128, 4, 512], BF16)
    vctx = sb("vctx", [128, 4, 512], BF16)
    kcT = sb("kcT", [64, 8, 512], BF16)
    pTn = [sb(f"pTn{i}", [128, 512], BF16) for i in range(3)]
    recn = sb("recn", [128, 512], F32)
    a1g = [sb(f"a1g{i}", [128, 4, 256], BF16) for i in range(2)]
    dfs = [sb(f"dfs{i}", [128, 2, 512], BF16) for i in range(2)]
    dma(POOL, kc_tok[:], ck_d.rearrange("(t p) n -> p t n", p=128), w=[kc_tok], sem="ctxk")
    dma(POOL, vctx[:], cv_d.rearrange("(t p) n -> p t n", p=128), w=[vctx], sem="ctxv")
    for h in range(8):
        pt = nptr()
        for t4 in range(4):
            op(PE, lambda E, pt=pt, t4=t4, h=h: E.transpose(pt[0:64, t4 * 128:(t4 + 1) * 128], kc_tok[:, t4, h * 64:(h + 1) * 64],
                                                            ident[:]), r=[kc_tok, ident], w=[pt])
        op(DVE, lambda E, pt=pt, h=h: E.tensor_copy(out=kcT[:, h, :], in_=pt[0:64, 0:512]), r=[pt], w=[kcT])
    npT = 0
    for hp in range(4):
        bbs = []
        for h2 in range(2):
            bt = BBt[h2]
            dma(POOL, bt[:], bb_d[2 * hp + h2], w=[bt], sem=bt.name)
            bbs.append(bt)
        for j in range(2):
            po, pd = pbs[0], pbs[1]
            for h2 in range(2):
                h = 2 * hp + h2
                lo = 64 * h2
                units = [("loc", i) for i in ((range(0, 10)) if j == 0 else range(2, 12))] + [("ctx", t4) for t4 in range(4)]
                for ui, (kind, i) in enumerate(units):
                    pss = pbs[2 + (npT % 4)]
                    pT = pTn[npT % 3]
                    npT += 1
                    if kind == "loc":
                        idx0 = 18 - 2 * i + 8 * j
                        op(PE, lambda E, pss=pss, h=h, i=i, j=j: E.matmul(
                            pss[:, :], lhsT=kTa[0:88, h, i * 128:(i + 1) * 128], rhs=qTa[0:88, h, j * 512:(j + 1) * 512],
                            start=True, stop=False), r=[kTa.sub(h), kTa.sub("aug"), qTa.sub(h), qTa.sub("aug")], w=[pss])
                        op(PE, lambda E, pss=pss, h2=h2, idx0=idx0: E.matmul(
                            pss[:, :], lhsT=ident[:], rhs=bbs[h2][:, idx0 * 64:idx0 * 64 + 512], start=False, stop=True),
                           r=[ident, bbs[h2]], w=[pss])
                        vl = vwin[:, i, h * 64:(h + 1) * 64]
                        vtr = vwin
                    else:
                        op(PE, lambda E, pss=pss, h=h, i=i, j=j: E.matmul(
                            pss[:, :], lhsT=kcT[0:64, h, i * 128:(i + 1) * 128], rhs=qTa[0:64, h, j * 512:(j + 1) * 512],
                            start=True, stop=True), r=[kcT, qTa.sub(h)], w=[pss])
                        vl = vctx[:, i, h * 64:(h + 1) * 64]
                        vtr = vctx
                    op(ACT, lambda E, pss=pss, pT=pT: E.activation(out=pT[:], in_=pss[:, :], func=AF.Exp), r=[pss], w=[pT])
                    op(PE, lambda E, po=po, lo=lo, vl=vl, pT=pT, ui=ui: E.matmul(
                        po[lo:lo + 64, :], lhsT=vl, rhs=pT[:], start=(ui == 0), stop=(ui == len(units) - 1)),
                       r=[vtr, pT], w=[po])
                    op(PE, lambda E, pd=pd, lo=lo, pT=pT, ui=ui: E.matmul(
                        pd[lo:lo + 64, :], lhsT=ones_bf[:, 0:64], rhs=pT[:], start=(ui == 0), stop=(ui == len(units) - 1)),
                       r=[ones_bf, pT], w=[pd])
            op(DVE, lambda E, pd=pd: E.reciprocal(out=recn[:], in_=pd[:, :]), r=[pd], w=[recn])
            op(POOL, lambda E, hp=hp, j=j: E.tensor_mul(out=recn[:], in0=recn[:], in1=szb_s[:, hp, j * 512:(j + 1) * 512]),
               r=[recn, szb_s], w=[recn])
            op(DVE, lambda E, po=po, hp=hp, j=j: E.tensor_tensor(out=ocs[:, 4 + hp, j * 512:(j + 1) * 512], in0=po[:, :], in1=recn[:],
                                                                 op=ALU.mult), r=[po, recn], w=[ocs.sub(j)])

    if SCUT == 'na':
        dump('gin0', gin[0][:, :], [512, 1024], [gin[0].sub(r0) for r0 in range(0, 512, 128)])
        dump('csw', csw[:], [128, 4, 256], [csw])
        return kb.finish(outs_final)
    nst = 0
    for mb in range(2):
        for lt in range(32):
            ag, df = a1g[nst % 2], dfs[nst % 2]
            nst += 1
            gsrc = gout[(lt % 8) // 4]
            grow = (lt // 8) * 512 + (lt % 4) * 128
            dma(SP, ag[:], gsrc[grow:grow + 128, :].rearrange("p (g e) -> p g e", g=4), r=[gsrc], w=[ag], sem=ag.name)
            dma(SP, df[:], dfts_d[lt * 128:(lt + 1) * 128, :, mb * 512:(mb + 1) * 512], w=[df], sem=df.name)
            for g in range(4):
                for cs in range(2):
                    op(PE, lambda E, g=g, cs=cs, ag=ag, df=df, lt=lt: E.matmul(
                        pbs[g][:, :], lhsT=ag[:, g, cs * 128:(cs + 1) * 128], rhs=df[:, cs, :],
                        start=(lt == 0 and cs == 0), stop=(lt == 31 and cs == 1)), r=[ag, df], w=[pbs[g]])
        for g in range(4):
            op(DVE, lambda E, g=g, mb=mb: E.tensor_tensor(out=ocs[:, g, mb * 512:(mb + 1) * 512], in0=pbs[g][:, :],
                                                          in1=sza_s[:, g, mb * 512:(mb + 1) * 512], op=ALU.mult),
               r=[pbs[g], sza_s], w=[ocs.sub(mb)])

    if SCUT == 'fnet':
        return kb.finish(outs_final)
    if stage == 2:
        dump('ocs', ocs[:], [128, KC, 1024], [ocs.sub(0), ocs.sub(1)])
        dump('gout0', gout[0][:, :], [2048, 1024], [gout[0]])
        dump('gin0', gin[0][:, :], [512, 1024], [gin[0].sub(r0) for r0 in range(0, 512, 128)])
        dump('sza', sza_s[:], [128, 4, 1024], [sza_s])
    for t in range(8):
        y = out_proj_ln(t, xs_win[256 + t * 128:256 + (t + 1) * 128, :], ocs, t * 128, ocs.sub(t // 4), w0fn, 0, 1)
        dma(SP, x1d[1024 + t * 128:1024 + (t + 1) * 128, :], y[:], r=[y], w=[x1d.sub(8 + t)], sem=y.name)

    if stage == 2:
        return kb.finish(outs_final)

    kb.barrier()
    kb.release(mark_base)
    NLEV = 2
    CUT = os.environ.get('L1CUT', '')
    NBLK = int(os.environ.get('L1NBLK', '4'))
    wbuf = [sb(f"wbuf{i}", [128, KC, 512], BF16) for i in range(2)]
    w1cat = sb("w1cat", [128, KC, 128], BF16)
    a1cat = sb("a1cat", [128, KC, 128], BF16)
    g1s = sb("g1s", [128, KC, 128], BF16)
    w2cat = sb("w2cat", [128, D], BF16)
    a2cat = sb("a2cat", [128, D], BF16)
    g2s = sb("g2s", [128, D], BF16)
    w0b = sb("w0b", [128, 2, D], F32)
    vecs = sb("vecs", [128, 16, KC], F32)
    tri = sb("tri", [128, 2, 128], F32)
    mask4 = sb("mask4", [128, 2, 512], BF16)
    namask = sb("namask", [128, 2, 256], BF16)
    bo = sb("bo", [128, 2, 128], BF16)
    identf = sb("identf", [128, 128], F32)
    dma(SP, vecs[:, 0:14, :], vecs_d[:, :, :], w=[vecs], sem="c_vecs")
    dma(SP, tri[:], tri_d[:, :, :], w=[tri], sem="c_tri")
    dma(POOL, mask4[:], mask4_d[:, :, :], w=[mask4], sem="c_mask4")
    dma(POOL, namask[:], namask_d[:, :, :], w=[namask], sem="c_namask")
    dma(POOL, bo[:], bo_d[:, :, :], w=[bo], sem="c_bo")
    for e in range(2):
        dma(POOL, w1cat[:, :, e * 64:(e + 1) * 64], w1_d[e].rearrange("(kc p) r -> p kc r", p=128), w=[w1cat], sem="c_w1")
        dma(POOL, a1cat[:, :, e * 64:(e + 1) * 64], a1_d[e].rearrange("(kc p) r -> p kc r", p=128), w=[a1cat], sem="c_a1")
        dma(POOL, w2cat[64 * e:64 * e + 64, :], w2_d[e], w=[w2cat], sem="c_w2")
        dma(POOL, a2cat[64 * e:64 * e + 64, :], a2_d[e], w=[a2cat], sem="c_a2")
        bcast_row(w0b, w0b[:, e, :], w0_d[e:e + 1, :], D)
    dma(POOL, g1s[:], g1_d.rearrange("(kc p) r -> p kc r", p=128), w=[g1s], sem="c_g1")
    dma(POOL, g2s[:], g2_d[:, :], w=[g2s], sem="c_g2")
    bcast_row(lng, lng[:], post_g[1:2, :], D)
    bcast_row(lnb, lnb[:], post_b[1:2, :], D)
    op(POOL, lambda E: E.tensor_copy(out=identf[:], in_=ident[:]), r=[ident], w=[identf])
    V_MU, V_KK, V_KA, V_RK, V_LG, V_LB, V_A0, V_OMKA = 0, 6, 7, 8, 9, 10, 11, 13
    op(DVE, lambda E: E.tensor_scalar(out=vecs[:, V_OMKA, :], in0=vecs[:, V_KA, :], scalar1=-1.0, scalar2=1.0,
                                      op0=ALU.mult, op1=ALU.add), r=[vecs], w=[vecs])
    op(DVE, lambda E: E.tensor_scalar_mul(out=vecs[:, V_RK, :], in0=vecs[:, V_RK, :], scalar1=0.5), r=[vecs], w=[vecs])
    op(DVE, lambda E: E.tensor_scalar_mul(out=vecs[:, 14:16, :], in0=vecs[:, V_A0:V_A0 + 2, :], scalar1=-1.0), r=[vecs], w=[vecs])

    mark_blk = kb.mark()
    u1T_s = sb("u1T_s", [128, KC, 1026], BF16)
    u1T = [T(u1T_s.h[:, :, i * 258:(i + 1) * 258], f"u1T{i}") for i in range(2)]
    vext = [sb(f"vext{c}", [128, 16, 128], BF16) for c in range(2)]
    yst = [sb(f"yst{i}", [128, 2, 256], BF16) for i in range(2)]
    bonst = sb("bonst", [128, 256], BF16)
    ext_d = kb.dram("ext_d", [4, 2, 16, 64, 128], F32)
    yext_d = kb.dram("yext_d", [4, 2, 8, 128, 512], BF16)
    post_d = kb.dram("post_d", [4, 8, 2, 128, 256], BF16)
    dx = sb("dx", [128, KC, 256], BF16)
    xl = sb("xl", [128, KC, 256], BF16)
    hwT = sb("hwT", [128, 256], BF16)
    haT = sb("haT", [128, 256], BF16)
    hgT = sb("hgT", [128, 256], BF16)
    vtok = [sb(f"vtok{c}", [128, D], BF16) for c in range(2)]
    vT = sb("vT", [128, KC, 256], BF16)
    krawT = sb("krawT", [128, KC, 256], BF16)
    rT = sb("rT", [128, KC, 256], BF16)
    gzT = sb("gzT", [128, KC, 256], BF16)
    sig = [[sb(f"sig{c}{e}", [128, D], F32) for e in range(2)] for c in range(2)]
    oT = sb("oT", [128, KC, 256], BF16)
    a_t = [sb(f"a_t{e}", [128, 256], F32) for e in range(2)]
    kk0 = sb("kk0", [128, 256], F32)
    kap = sb("kap", [128, 256], F32)
    kd_t = [sb(f"kd_t{e}", [128, 256], F32) for e in range(2)]
    b_t = [sb(f"b_t{e}", [128, 256], F32) for e in range(2)]
    sqb = sb("sqb", [128, 256], BF16)
    rkr2 = [sb(f"rkr{i}", [128, 256], BF16) for i in range(2)]
    gtmp = kk0
    Einc = [sb(f"Einc{i}", [128, 130], F32) for i in range(4)]
    Enin = [sb(f"Enin{i}", [128, 128], F32) for i in range(4)]
    KR = [sb(f"KR{i}", [128, 2, 128], BF16) for i in range(4)]
    KBt = [sb(f"KB{i}", [128, 2, 128], BF16) for i in range(4)]
    KBbar2 = [sb(f"KBbar{e}", [128, 2, 128], BF16) for e in range(4)]
    KBtok = [sb(f"KBtok{i}", [128, 2, 128], BF16) for i in range(4)]
    AFM = [[sb(f"AFM{i}{h}", [128, 512], BF16) for h in range(2)] for i in range(4)]
    nA = [sb(f"nA{i}", [128, 2, 128], BF16) for i in range(4)]
    T02 = [sb(f"T0{e}", [128, 2, 128], BF16) for e in range(4)]
    BA22 = [sb(f"BA2{e}", [128, 2, 2, 128], BF16) for e in range(4)]
    TA2 = [sb(f"TA{e}", [128, 2, 2, 128], BF16) for e in range(4)]
    T12 = [T(TA2[e].h[:, 0, :, :], f"T1{e}") for e in range(4)]
    A42 = [T(TA2[e].h[:, 1, :, :], f"A4{e}") for e in range(4)]
    TT = [sb(f"TT{i}", [128, 2, 128], BF16) for i in range(4)]
    Zb2 = [sb(f"Zb{e}", [128, 2, 128], BF16) for e in range(2)]
    nU2 = [sb(f"nU{e}", [128, 2, 128], BF16) for e in range(2)]
    Sf = [sb(f"Sf{e}", [128, 128], F32) for e in range(2)]
    Sb = [sb(f"Sb{e}", [128, 128], BF16) for e in range(2)]
    yT = sb("yT", [128, 256], F32)
    ybf = sb("ybf", [128, 256], BF16)
    yc = sb("yc", [128, 256], F32)
    sdv = sb("sdv", [128, 256], F32)
    st_out = sb("st_out", [64, 2, 64], F32)
    for i in range(4):
        op(POOL, lambda E, i=i: E.memset(Einc[i][:], 1.0), w=[Einc[i]])
    for i in range(2):
        op(POOL, lambda E, i=i: E.memset(u1T[i][:], 0.0), w=[u1T[i]])
        op(POOL, lambda E, i=i: E.memset(vext[i][:], 0.0), w=[vext[i]])

    wctr = [0]

    def load_wh(i, half):
        wb = wbuf[wctr[0] % 2]
        wctr[0] += 1
        dma(SP, wb[:], wbf_d[i][:, half * 512:(half + 1) * 512].rearrange("(kc p) n -> p kc n", p=128), r=[wbf_d[i]], w=[wb],
            sem=wb.name)
        return wb

    def fm_proj(wi, rhs_fn, rtrk, evac):
        for hw in range(2):
            wb = load_wh(wi, hw)
            for cp in (2 * hw, 2 * hw + 1):
                p = npb()
                for half in range(2):
                    ct = cp * 2 + half
                    for kc in range(KC):
                        op(PE, lambda E, p=p, half=half, ct=ct, kc=kc, wb=wb: E.matmul(
                            p[:, half * 256:(half + 1) * 256], lhsT=wb[:, kc, (ct % 4) * 128:(ct % 4 + 1) * 128], rhs=rhs_fn(kc),
                            start=(kc == 0), stop=(kc == KC - 1)), r=[wb] + rtrk, w=[p])
                evac(cp, p)

    def rwkv_block(blk, x1rows, x1deps, v, yout_rows, state_out, stacked=False):
        VW = 128 if stacked else 64
        if stacked:
            ub = T(u1T_s.h[:, :, blk * 256:blk * 256 + 258], "ubv")
            ub.trk = u1T_s.trk
        else:
            ub = u1T[blk % 2]
            for c in range(2):
                ln_transpose(x1rows[c * 128:(c + 1) * 128, :], x1deps[c], modT[1], v, ub, 1 + c * 128, ub)

        def vsrc(c, h):
            return vext[c][:, h, :] if stacked else vtok[c][:, h * 64:(h + 1) * 64]
        op(DVE, lambda E: E.tensor_tensor(out=dx[:], in0=ub[:, :, 0:256], in1=ub[:, :, 2:258], op=ALU.add),
           r=[ub], w=[dx])
        op(DVE, lambda E: E.scalar_tensor_tensor(out=dx[:], in0=dx[:], scalar=0.5, in1=ub[:, :, 1:257],
                                                 op0=ALU.mult, op1=ALU.subtract), r=[ub, dx], w=[dx])

        def lerp(i):
            for kc in range(KC):
                op(DVE, lambda E, kc=kc: E.scalar_tensor_tensor(
                    out=xl[:, kc, :], in0=dx[:, kc, :], scalar=vecs[:, V_MU + i, kc:kc + 1], in1=ub[:, kc, 1:257],
                    op0=ALU.mult, op1=ALU.add), r=[dx, ub, vecs], w=[xl])

        def hidden(wcat, hT, func):
            p = npb()
            for kc in range(KC):
                op(PE, lambda E, kc=kc: E.matmul(p[:, 0:256], lhsT=wcat[:, kc, :], rhs=xl[:, kc, :],
                                                 start=(kc == 0), stop=(kc == KC - 1)), r=[wcat, xl], w=[p])
            op(ACT, lambda E: E.activation(out=hT[:], in_=p[:, 0:256], func=func), r=[p], w=[hT])

        lerp(1)
        hidden(w1cat, hwT, AF.Tanh)
        lerp(4)
        hidden(a1cat, haT, AF.Identity)
        lerp(5)
        hidden(g1s, hgT, AF.Sigmoid)
        lerp(3)
        for nh in range(2):
            wv = load_wh(2, nh)
            for c in range(2):
                p = npb()
                for kc in range(KC):
                    op(PE, lambda E, kc=kc, c=c, nh=nh, p=p, wv=wv: E.matmul(
                        p[:, :], lhsT=xl[:, kc, c * 128:(c + 1) * 128], rhs=wv[:, kc, :],
                        start=(kc == 0), stop=(kc == KC - 1)), r=[xl, wv], w=[p])
                op(ACT, lambda E, c=c, nh=nh, p=p: E.activation(out=vtok[c][:, nh * 512:(nh + 1) * 512], in_=p[:, :],
                                                                func=AF.Identity), r=[p], w=[vtok[c]])
        for c in range(2):
            pt = nptr()
            for kc in range(KC):
                op(PE, lambda E, kc=kc, c=c, pt=pt: E.transpose(pt[:, kc * 128:(kc + 1) * 128],
                                                                vtok[c][:, kc * 128:(kc + 1) * 128], ident[:]),
                   r=[vtok[c], ident], w=[pt])
            op(DVE, lambda E, c=c, pt=pt: E.tensor_copy(out=vT[:, :, c * 128:(c + 1) * 128],
                                                        in_=pt[:, :].rearrange("p (k t) -> p k t", k=KC)),
               r=[pt], w=[vT])
            if stacked:
                op(POOL, lambda E, c=c: E.tensor_copy(out=vext[c][:, :, 64:128],
                                                      in_=vtok[c][:, :].rearrange("p (h d) -> p h d", h=16)),
                   r=[vtok[c]], w=[vext[c]])
        lerp(2)
        fm_proj(1, lambda kc: xl[:, kc, :], [xl],
                lambda cp, p: op(DVE, lambda E: E.tensor_copy(
                    out=krawT[:, 2 * cp:2 * cp + 2, :], in_=p[:, :].rearrange("p (k t) -> p k t", k=2)),
                    r=[p], w=[krawT]))
        lerp(0)
        fm_proj(0, lambda kc: xl[:, kc, :], [xl],
                lambda cp, p: op(ACT, lambda E: E.activation(
                    out=rT[:, 2 * cp:2 * cp + 2, :], in_=p[:, :].rearrange("p (k t) -> p k t", k=2), func=AF.Identity),
                    r=[p], w=[rT]))
        fm_proj(3, lambda kc: ub[:, kc, 1:257], [ub],
                lambda cp, p: op(ACT, lambda E: E.activation(
                    out=gzT[:, 2 * cp:2 * cp + 2, :], in_=p[:, :].rearrange("p (k t) -> p k t", k=2), func=AF.Silu),
                    r=[p], w=[gzT]))
        for c in range(2):
            for e in range(2):
                for nh in range(2):
                    p = npb()
                    op(PE, lambda E, c=c, e=e, nh=nh, p=p: E.matmul(
                        p[:, :], lhsT=hwT[64 * e:64 * e + 64, c * 128:(c + 1) * 128],
                        rhs=w2cat[64 * e:64 * e + 64, nh * 512:(nh + 1) * 512], start=True, stop=True),
                       r=[hwT, w2cat], w=[p])
                    op(DVE, lambda E, c=c, e=e, nh=nh, p=p: E.tensor_tensor(
                        out=sig[c][e][:, nh * 512:(nh + 1) * 512], in0=p[:, :], in1=w0b[:, e, nh * 512:(nh + 1) * 512],
                        op=ALU.add), r=[p, w0b], w=[sig[c][e]])
                op(ACT, lambda E, c=c, e=e: E.activation(out=sig[c][e][:], in_=sig[c][e][:], func=AF.Sigmoid),
                   r=[sig[c][e]], w=[sig[c][e]])

        if CUT == 'proj':
            return
        pair = [0]
        def ctgen(ct):
            cs_ = slice(ct * 128, (ct + 1) * 128)
            rkr = rkr2[ct % 2]
            p = npb()
            op(PE, lambda E, p=p, cs_=cs_: E.matmul(p[:, 0:256], lhsT=g2s[:, cs_], rhs=hgT[:], start=True, stop=True),
               r=[g2s, hgT], w=[p])
            op(DVE, lambda E, p=p, ct=ct: E.tensor_tensor(out=gzT[:, ct, :], in0=p[:, 0:256], in1=gzT[:, ct, :], op=ALU.mult),
               r=[p, gzT], w=[gzT])
            if CUT == 'c1':
                return
            for e in range(2):
                p = npb()
                op(PE, lambda E, p=p, e=e, cs_=cs_: E.matmul(
                    p[:, 0:256], lhsT=a2cat[64 * e:64 * e + 64, cs_], rhs=haT[64 * e:64 * e + 64, :],
                    start=True, stop=True), r=[a2cat, haT], w=[p])
                op(ACT, lambda E, p=p, e=e, ct=ct: E.activation(
                    out=a_t[e][:], in_=p[:, 0:256], func=AF.Exp,
                    bias=vecs[:, 14 + e, ct:ct + 1], scale=-1.0), r=[p, vecs], w=[a_t[e]])
                op(ACT, lambda E, e=e: E.activation(out=a_t[e][:], in_=a_t[e][:], func=AF.Ln, bias=consts[:, 3:4], scale=1.0),
                   r=[a_t[e], consts], w=[a_t[e]])
                op(ACT, lambda E, e=e: E.activation(out=a_t[e][:], in_=a_t[e][:], func=AF.Exp, scale=-1.0),
                   r=[a_t[e]], w=[a_t[e]])
            if CUT == 'c2':
                return
            op(DVE, lambda E, ct=ct: E.tensor_scalar_mul(out=kk0[:], in0=krawT[:, ct, :], scalar1=vecs[:, V_KK, ct:ct + 1]),
               r=[krawT, vecs], w=[kk0])
            op(POOL, lambda E: E.tensor_mul(out=sqb[:], in0=kk0[:], in1=kk0[:]), r=[kk0], w=[sqb])
            p = npb()
            op(PE, lambda E, p=p: E.matmul(p[:, 0:256], lhsT=bo[:, 0, :], rhs=sqb[:], start=True, stop=True),
               r=[bo, sqb], w=[p])
            op(ACT, lambda E, p=p: E.activation(out=kap[:], in_=p[:, 0:256], func=AF.Ln, bias=consts[:, 2:3], scale=1.0),
               r=[p, consts], w=[kap])
            op(ACT, lambda E: E.activation(out=kap[:], in_=kap[:], func=AF.Exp, scale=-0.5), r=[kap], w=[kap])
            op(POOL, lambda E: E.tensor_mul(out=kap[:], in0=kap[:], in1=kk0[:]), r=[kap, kk0], w=[kap])
            if CUT == 'c3':
                return
            for e in range(2):
                op(DVE, lambda E, e=e, ct=ct: E.tensor_scalar(
                    out=kd_t[e][:], in0=a_t[e][:], scalar1=vecs[:, V_KA, ct:ct + 1], scalar2=vecs[:, V_OMKA, ct:ct + 1],
                    op0=ALU.mult, op1=ALU.add), r=[a_t[e], vecs], w=[kd_t[e]])
                op(POOL, lambda E, e=e, ct=ct: E.tensor_mul(out=kd_t[e][:], in0=kd_t[e][:], in1=krawT[:, ct, :]),
                   r=[kd_t[e], krawT], w=[kd_t[e]])
                op(POOL, lambda E, e=e: E.tensor_mul(out=b_t[e][:], in0=kap[:], in1=a_t[e][:]), r=[kap, a_t[e]], w=[b_t[e]])
            if CUT == 'c4':
                return
            op(POOL, lambda E: E.tensor_add(out=gtmp[:], in0=kd_t[0][:], in1=kd_t[1][:]), r=[kd_t[0], kd_t[1]], w=[gtmp])
            op(POOL, lambda E, ct=ct: E.tensor_mul(out=gtmp[:], in0=gtmp[:], in1=rT[:, ct, :]), r=[gtmp, rT], w=[gtmp])
            op(DVE, lambda E, ct=ct: E.tensor_scalar_mul(out=rkr[:], in0=gtmp[:], scalar1=vecs[:, V_RK, ct:ct + 1]),
               r=[gtmp, vecs], w=[rkr])

            if CUT == 'ctprep':
                return
            yield
            op(POOL, lambda E: E.memset(yT[:], 0.0), w=[yT])

            def prep(e, ci, c):
                i = 2 * e + ci
                KBbar, T0, BA2, T1, A4 = KBbar2[i], T02[i], BA22[i], T12[i], A42[i]
                Zb, nU = Zb2[e], nU2[e]
                tsl = slice(c * 128, (c + 1) * 128)
                ei, en, kr, kbt, kbtok, af, na_, tt = Einc[i], Enin[i], KR[i], KBt[i], KBtok[i], AFM[i], nA[i], TT[i]
                ex = ei[:, 0:128] if e == 0 else ei[:, 2:130]
                wc = ei[:, 128:129] if e == 0 else ei[:, 1:2]
                p = npb()
                op(PE, lambda E, p=p, c=c, e=e, cs_=cs_: E.matmul(p[:, 0:128], lhsT=sig[c][e][:, cs_], rhs=tri[:, e, :],
                                                                  start=True, stop=True), r=[sig[c][e], tri], w=[p])
                op(ACT, lambda E, p=p, ei=ei: E.activation(out=ei[:, 1:129], in_=p[:, 0:128], func=AF.Exp), r=[p], w=[ei])
                op(ACT, lambda E, p=p, en=en: E.activation(out=en[:], in_=p[:, 0:128], func=AF.Exp, scale=-1.0), r=[p], w=[en])
                yield
                op(DVE, lambda E, kr=kr, ex=ex, tsl=tsl: E.tensor_tensor(out=kr[:, 0, :], in0=kap[:, tsl], in1=ex, op=ALU.mult),
                   r=[kap, ei], w=[kr])
                op(POOL, lambda E, kr=kr, ei=ei, tsl=tsl, ct=ct: E.tensor_mul(out=kr[:, 1, :], in0=rT[:, ct, tsl], in1=ei[:, 1:129]),
                   r=[rT, ei], w=[kr])
                op(DVE, lambda E, kbt=kbt, en=en, tsl=tsl, e=e: E.tensor_tensor(out=kbt[:, 0, :], in0=kd_t[e][:, tsl], in1=en[:], op=ALU.mult),
                   r=[kd_t[e], en], w=[kbt])
                op(POOL, lambda E, kbt=kbt, en=en, tsl=tsl, e=e: E.tensor_mul(out=kbt[:, 1, :], in0=b_t[e][:, tsl], in1=en[:]),
                   r=[b_t[e], en], w=[kbt])
                op(ACT, lambda E, kbt=kbt, wc=wc: E.activation(out=KBbar[:], in_=kbt[:], func=AF.Identity, scale=wc),
                   r=[kbt, ei], w=[KBbar])
                yield
                pt = nptr()
                for j in range(2):
                    op(PE, lambda E, pt=pt, j=j: E.transpose(pt[:, j * 128:(j + 1) * 128], KBbar[:, j, :], ident[:]),
                       r=[KBbar, ident], w=[pt])
                op(DVE, lambda E, pt=pt, kbtok=kbtok: E.tensor_copy(
                    out=kbtok[:], in_=pt[:, 0:256].rearrange("p (j c) -> p j c", j=2)), r=[pt], w=[kbtok])
                yield
                if CUT == 'tilde':
                    return
                for h2 in range(2):
                    lo = 64 * h2
                    pa = npb()
                    pn = npb()
                    op(PE, lambda E, pa=pa, lo=lo, kbt=kbt, kr=kr: E.matmul(
                        pa[:, 0:256], lhsT=kbt[lo:lo + 64, 1, :], rhs=kr[lo:lo + 64, :, :], start=True, stop=True),
                       r=[kbt, kr], w=[pa])
                    op(PE, lambda E, pa=pa, lo=lo, kbt=kbt, kr=kr: E.matmul(
                        pa[:, 256:512], lhsT=kbt[lo:lo + 64, 0, :], rhs=kr[lo:lo + 64, :, :], start=True, stop=True),
                       r=[kbt, kr], w=[pa])
                    op(PE, lambda E, pn=pn, lo=lo, h2=h2, kbt=kbt, kr=kr: E.matmul(
                        pn[:, 0:128], lhsT=kr[lo:lo + 64, 0, :], rhs=kbt[lo:lo + 64, 1, :],
                        start=True, stop=True), r=[kbt, kr], w=[pn])
                    op(DVE, lambda E, pa=pa, h2=h2, af=af, e=e: E.tensor_tensor(
                        out=af[h2][:], in0=pa[:, :], in1=mask4[:, e, :], op=ALU.mult), r=[pa, mask4], w=[af[h2]])
                    op(DVE, lambda E, pn=pn, na_=na_, e=e, h2=h2: E.tensor_tensor(
                        out=na_[:, h2, :], in0=pn[:, 0:128], in1=namask[:, e, 0:128], op=ALU.mult),
                       r=[pn, namask], w=[na_])
                yield
                if CUT == 'aforms':
                    return
                for h2 in range(2):
                    op(POOL, lambda E, h2=h2, af=af: E.tensor_add(out=T0[:, h2, :], in0=ident[:], in1=af[h2][:, 0:128]),
                       r=[ident, af[h2]], w=[T0])
                p1 = npb()
                for h2 in range(2):
                    op(PE, lambda E, p1=p1, h2=h2, af=af, na_=na_: E.matmul(
                        p1[:, h2 * 256:h2 * 256 + 128], lhsT=na_[:, h2, :], rhs=af[h2][:, 0:128], start=True, stop=True),
                       r=[na_, af[h2]], w=[p1])
                    op(PE, lambda E, p1=p1, h2=h2, af=af, na_=na_: E.matmul(
                        p1[:, h2 * 256 + 128:h2 * 256 + 256], lhsT=af[h2][:, 0:128], rhs=na_[:, h2, :], start=True, stop=True),
                       r=[na_, af[h2]], w=[p1])
                op(ACT, lambda E, p1=p1: E.activation(out=BA2[:].rearrange("p h j t -> p (h j t)"), in_=p1[:, :],
                                                      func=AF.Identity), r=[p1], w=[BA2])
                yield
                p2 = npb()
                for h2 in range(2):
                    op(PE, lambda E, p2=p2, h2=h2: E.matmul(p2[:, h2 * 128:(h2 + 1) * 128], lhsT=ident[:], rhs=T0[:, h2, :],
                                                            start=True, stop=False), r=[ident, T0], w=[p2])
                    op(PE, lambda E, p2=p2, h2=h2: E.matmul(p2[:, h2 * 128:(h2 + 1) * 128], lhsT=BA2[:, h2, 1, :], rhs=T0[:, h2, :],
                                                            start=False, stop=True), r=[BA2, T0], w=[p2])
                    op(PE, lambda E, p2=p2, h2=h2: E.matmul(p2[:, 256 + h2 * 128:256 + (h2 + 1) * 128], lhsT=BA2[:, h2, 0, :],
                                                            rhs=BA2[:, h2, 1, :], start=True, stop=True), r=[BA2], w=[p2])
                ta_ = TA2[i]
                op(ACT, lambda E, p2=p2, ta_=ta_: E.activation(out=ta_[:].rearrange("p a h t -> p (a h t)"), in_=p2[:, :],
                                                               func=AF.Identity), r=[p2], w=[T1, A4])
                yield
                p3 = npb()
                for h2 in range(2):
                    op(PE, lambda E, p3=p3, h2=h2: E.matmul(p3[:, h2 * 128:(h2 + 1) * 128], lhsT=ident[:], rhs=T1[:, h2, :],
                                                            start=True, stop=False), r=[ident, T1], w=[p3])
                    op(PE, lambda E, p3=p3, h2=h2: E.matmul(p3[:, h2 * 128:(h2 + 1) * 128], lhsT=A4[:, h2, :], rhs=T1[:, h2, :],
                                                            start=False, stop=True), r=[A4, T1], w=[p3])
                op(ACT, lambda E, p3=p3, tt=tt: E.activation(out=tt[:].rearrange("p h t -> p (h t)"), in_=p3[:, 0:256],
                                                             func=AF.Identity), r=[p3], w=[tt])
                yield
                yield

            def serial(e, ci, c):
                i = 2 * e + ci
                KBbar, T0, BA2, T1, A4 = KBbar2[i], T02[i], BA22[i], T12[i], A42[i]
                Zb, nU = Zb2[e], nU2[e]
                tsl = slice(c * 128, (c + 1) * 128)
                ei, en, kr, kbt, kbtok, af, na_, tt = Einc[i], Enin[i], KR[i], KBt[i], KBtok[i], AFM[i], nA[i], TT[i]
                ex = ei[:, 0:128] if e == 0 else ei[:, 2:130]
                wc = ei[:, 128:129] if e == 0 else ei[:, 1:2]
                if CUT == 'tchain':
                    return
                first = (ci == 0) and not stacked
                for h2 in range(2):
                    lo = 64 * h2
                    h = 2 * ct + h2
                    pz = npb()
                    if not first:
                        op(PE, lambda E, pz=pz, h2=h2, lo=lo, kr=kr, e=e: E.matmul(
                            pz[:, 0:VW], lhsT=kr[lo:lo + 64, 0, :], rhs=Sb[e][lo:lo + 64, 0:VW],
                            start=True, stop=False), r=[kr, Sb[e]], w=[pz])
                    op(PE, lambda E, pz=pz, h2=h2, h=h, af=af, c=c, first=first: E.matmul(
                        pz[:, 0:VW], lhsT=af[h2][:, 256:384], rhs=vsrc(c, h),
                        start=first, stop=True), r=[af[h2], vtok[c], vext[c]], w=[pz])
                    op(ACT, lambda E, pz=pz, h2=h2: E.activation(out=Zb[:, h2, 0:VW], in_=pz[:, 0:VW],
                                                                 func=AF.Identity), r=[pz], w=[Zb])
                yield
                pu = npb()
                for h2 in range(2):
                    op(PE, lambda E, pu=pu, h2=h2, tt=tt: E.matmul(pu[:, h2 * VW:(h2 + 1) * VW], lhsT=tt[:, h2, :], rhs=Zb[:, h2, 0:VW],
                                                                    start=True, stop=True), r=[tt, Zb], w=[pu])
                op(ACT, lambda E, pu=pu: E.activation(out=nU[:, :, 0:VW], in_=pu[:, 0:2 * VW].rearrange("p (h v) -> p h v", h=2),
                                                      func=AF.Identity, scale=-1.0), r=[pu], w=[nU])
                yield
                for h2 in range(2):
                    lo = 64 * h2
                    h = 2 * ct + h2
                    py = npb()
                    ylo = 0 if stacked else lo
                    if not first:
                        op(PE, lambda E, py=py, lo=lo, ylo=ylo, kr=kr, e=e: E.matmul(
                            py[ylo:ylo + VW, 0:128], lhsT=Sb[e][lo:lo + 64, 0:VW], rhs=kr[lo:lo + 64, 1, :],
                            start=True, stop=False), r=[kr, Sb[e]], w=[py])
                    op(PE, lambda E, py=py, ylo=ylo, h=h, h2=h2, af=af, c=c, first=first: E.matmul(
                        py[ylo:ylo + VW, 0:128], lhsT=vsrc(c, h), rhs=af[h2][:, 384:512],
                        start=first, stop=False), r=[af[h2], vtok[c], vext[c]], w=[py])
                    op(PE, lambda E, py=py, ylo=ylo, h2=h2, af=af: E.matmul(
                        py[ylo:ylo + VW, 0:128], lhsT=nU[:, h2, 0:VW], rhs=af[h2][:, 128:256],
                        start=False, stop=True), r=[af[h2], nU], w=[py])
                    if stacked:
                        ys_ = yst[e]
                        op(ACT, lambda E, py=py, tsl=tsl, h2=h2, ys_=ys_: E.activation(out=ys_[:, h2, tsl], in_=py[:, 0:128],
                                                                                      func=AF.Identity), r=[py], w=[ys_])
                    else:
                        op(DVE, lambda E, py=py, tsl=tsl, lo=lo: E.tensor_tensor(
                            out=yT[lo:lo + 64, tsl], in0=py[lo:lo + 64, 0:128], in1=yT[lo:lo + 64, tsl], op=ALU.add),
                           r=[py, yT], w=[yT])
                yield
                pss = npb()
                for h2 in range(2):
                    lo = 64 * h2
                    h = 2 * ct + h2
                    op(PE, lambda E, pss=pss, lo=lo, h=h, kbtok=kbtok, c=c: E.matmul(
                        pss[lo:lo + 64, 0:VW], lhsT=kbtok[:, 0, lo:lo + 64], rhs=vsrc(c, h),
                        start=True, stop=False), r=[kbtok, vtok[c], vext[c]], w=[pss])
                    op(PE, lambda E, pss=pss, lo=lo, h2=h2, kbtok=kbtok: E.matmul(
                        pss[lo:lo + 64, 0:VW], lhsT=kbtok[:, 1, lo:lo + 64], rhs=nU[:, h2, 0:VW],
                        start=False, stop=True), r=[kbtok, nU], w=[pss])
                op(DVE, lambda E, pss=pss, wc=wc, e=e: E.scalar_tensor_tensor(
                    out=Sf[e][:, 0:VW], in0=Sf[e][:, 0:VW], scalar=wc, in1=pss[:, 0:VW], op0=ALU.mult, op1=ALU.add),
                   r=[pss, ei, Sf[e]], w=[Sf[e]])
                op(POOL, lambda E, e=e: E.tensor_copy(out=Sb[e][:, 0:VW], in_=Sf[e][:, 0:VW]), r=[Sf[e]], w=[Sb[e]])
                yield

            def chain(e, part):
                order_ = (0, 1) if e == 0 else (1, 0)
                if part == 0:
                    yield from prep(e, 0, order_[0])
                    yield from chain_init_and_first(e, order_[0])
                    return
                yield from serial(e, 1, order_[1])
                yield from chain_tail(e)

            def chain_init_and_first(e, c):
                op(POOL, lambda E, e=e: E.memset(Sf[e][:], 0.0), w=[Sf[e]])
                if stacked:
                    op(POOL, lambda E, e=e: E.tensor_add(out=Sf[e][:, 0:64], in0=identf[:, 0:64], in1=identf[:, 64:128]),
                       r=[identf, Sf[e]], w=[Sf[e]])
                op(POOL, lambda E, e=e: E.tensor_copy(out=Sb[e][:], in_=Sf[e][:]), r=[Sf[e]], w=[Sb[e]])
                yield from serial(e, 0, c)

            def chain_tail(e):
                if stacked:
                    dma(SP, ext_d[blk, e, 2 * ct:2 * ct + 2, :, :].rearrange("h k n -> (h k) n"), Sf[e][:], r=[Sf[e]],
                        w=[ext_d], sem=f"Sfo{e}")
                    dma(SP, yext_d[blk, e, ct, :, :], yst[e][:].rearrange("p h t -> p (h t)"), r=[yst[e]], w=[yext_d],
                        sem=f"yst{e}")
                elif state_out is not None and CUT not in ('tilde', 'aforms', 'tchain', 'nostate'):
                    pst = npb()
                    op(PE, lambda E, pst=pst, e=e: E.matmul(pst[0:64, 0:128], lhsT=Sf[e][:, 0:64], rhs=identf[:, :], start=True, stop=True),
                       r=[Sf[e], identf], w=[pst])
                    op(ACT, lambda E, pst=pst: E.activation(out=st_out[:].rearrange("v h k -> v (h k)"), in_=pst[0:64, 0:128],
                                                            func=AF.Identity), r=[pst], w=[st_out])
                    dma(SP, state_out[e, 2 * ct:2 * ct + 2, :, :].rearrange("h v k -> v h k"), st_out[:], r=[st_out], sem="st_out")
                yield

            def rr(gens):
                while gens:
                    for g_ in list(gens):
                        try:
                            next(g_)
                        except StopIteration:
                            gens.remove(g_)

            rr([chain(0, 0), chain(1, 0), prep(0, 1, 1), prep(1, 1, 0)])
            rr([chain(0, 1), chain(1, 1)])
            yield
            if CUT in ('tilde', 'aforms', 'tchain', 'serial'):
                return
            if stacked:
                pbn = npb()
                op(PE, lambda E, pbn=pbn: E.matmul(pbn[:, 0:256], lhsT=bo[:, 0, :], rhs=rkr[:], start=True, stop=True), r=[bo, rkr], w=[pbn])
                op(DVE, lambda E, pbn=pbn, ct=ct: E.tensor_tensor(out=bonst[:], in0=pbn[:, 0:256], in1=vT[:, ct, :], op=ALU.mult),
                   r=[pbn, vT], w=[bonst])
                dma(SP, post_d[blk, ct, 1, :, :], bonst[:], r=[bonst], w=[post_d], sem="bonst")
                return
            post_ct(ct, None, (yT, ybf, yc, sdv, gzT, oT), rkr)
        cgs = [ctgen(ct) for ct in range(KC)]
        next(cgs[0], None)
        for ct in range(KC):
            next(cgs[ct], None)
            if ct + 1 < KC:
                next(cgs[ct + 1], None)
            next(cgs[ct], None)
        if stacked:
            dma(SP, post_d[blk, :, 0, :, :].rearrange("c p t -> p c t"), gzT[:], r=[gzT], w=[post_d], sem="gzo")
            return
        if CUT:
            return
        wo = [load_wh(4, 0), load_wh(4, 1)]
        for c in range(2):
            y = out_proj_ln(0, x1rows[c * 128:(c + 1) * 128, :], oT, c * 128, oT, lambda nh: (wo[nh], wo[nh][:, :, :]), 1, v)
            dma(SP, yout_rows[c * 128:(c + 1) * 128, :], y[:], r=[y], sem=y.name)

    def post_ct(ct, bon_ap, bufs, rkr=None):
        yT, ybf, yc, sdv, gzT, oT = bufs
        if True:
            op(POOL, lambda E: E.tensor_copy(out=ybf[:], in_=yT[:]), r=[yT], w=[ybf])
            pm = npb()
            op(PE, lambda E, pm=pm: E.matmul(pm[:, 0:256], lhsT=bo[:, 1, :], rhs=ybf[:], start=True, stop=True), r=[bo, ybf], w=[pm])
            op(DVE, lambda E, pm=pm: E.tensor_tensor(out=yc[:], in0=yT[:], in1=pm[:, 0:256], op=ALU.subtract), r=[yT, pm], w=[yc])
            op(POOL, lambda E: E.tensor_mul(out=ybf[:], in0=yc[:], in1=yc[:]), r=[yc], w=[ybf])
            pv = npb()
            op(PE, lambda E, pv=pv: E.matmul(pv[:, 0:256], lhsT=bo[:, 1, :], rhs=ybf[:], start=True, stop=True), r=[bo, ybf], w=[pv])
            op(ACT, lambda E, pv=pv: E.activation(out=sdv[:], in_=pv[:, 0:256], func=AF.Ln, bias=consts[:, 1:2], scale=1.0),
               r=[pv, consts], w=[sdv])
            op(ACT, lambda E: E.activation(out=sdv[:], in_=sdv[:], func=AF.Exp, scale=-0.5), r=[sdv], w=[sdv])
            op(POOL, lambda E: E.tensor_mul(out=yc[:], in0=yc[:], in1=sdv[:]), r=[yc, sdv], w=[yc])
            op(ACT, lambda E, ct=ct: E.activation(out=yc[:], in_=yc[:], func=AF.Identity, scale=vecs[:, V_LG, ct:ct + 1],
                                                  bias=vecs[:, V_LB, ct:ct + 1]), r=[yc, vecs], w=[yc])
            if bon_ap is None:
                pbn = npb()
                op(PE, lambda E, pbn=pbn: E.matmul(pbn[:, 0:256], lhsT=bo[:, 0, :], rhs=rkr[:], start=True, stop=True), r=[bo, rkr], w=[pbn])
                op(DVE, lambda E, pbn=pbn, ct=ct: E.tensor_tensor(out=sdv[:], in0=pbn[:, 0:256], in1=vT[:, ct, :], op=ALU.mult),
                   r=[pbn, vT], w=[sdv])
                op(POOL, lambda E: E.tensor_add(out=yc[:], in0=yc[:], in1=sdv[:]), r=[yc, sdv], w=[yc])
            else:
                op(POOL, lambda E: E.tensor_add(out=yc[:], in0=yc[:], in1=bon_ap[0]), r=[yc, bon_ap[1]], w=[yc])
            op(POOL, lambda E, ct=ct: E.tensor_mul(out=oT[:, ct, :], in0=yc[:], in1=gzT[:, ct, :]), r=[yc, gzT], w=[oT])

    for s4 in range(NBLK):
        rwkv_block(s4, x1d[s4 * 256:(s4 + 1) * 256, :], [[x1d.sub(2 * s4)], [x1d.sub(2 * s4 + 1)]], 0,
                   o_yp[s4 * 256:(s4 + 1) * 256, :], o_state[s4])
    if stage == 3:
        return kb.finish(outs_final)

    kb.barrier()
    for t in range(8):
        ln_transpose(x1d[1024 + t * 128:1024 + (t + 1) * 128, :], [x1d.sub(8 + t)], modT[1], 1, u1T_s, 1 + t * 128, u1T_s)
    hin = kb.dram("hin", [2, D], BF16)
    hout = kb.dram("hout", [8, D], BF16)
    hsb = T(dx.h[0:8, 0:4, :].rearrange("p a b -> p (a b)"), "hsb")
    hsb.trk = dx.trk
    selh = sb("selh", [8, 2], BF16)
    selc = sb("selc", [64, 2, 4], F32)
    dma(POOL, selh[:], selh_d[:, :], w=[selh], sem="c3")
    dma(SP, selc[:], selc_d[:, :, :], w=[selc], sem="c3s")
    for w_, col in ((0, 1), (1, 1024)):
        dma(SP, hin[w_, :].rearrange("(kc p) -> p kc", p=128), u1T_s[:, :, col], r=[u1T_s], w=[hin], sem="hin",
            allow_slow_non_contiguous=True)
    kb.collective("AllGather", ALU.bypass, GROUPS, hin[:, :], hout[:, :], r=[hin], w=[hout], sem="ag2")
    dma(SP, hsb[:], hout[:, :], r=[hout], w=[hsb], sem="hsb")
    ph = npb()
    for kc in range(KC):
        op(PE, lambda E, kc=kc: E.matmul(ph[:, 2 * kc:2 * kc + 2], lhsT=hsb[0:8, kc * 128:(kc + 1) * 128], rhs=selh[0:8, :],
                                         start=True, stop=True), r=[hsb, selh], w=[ph])
    for w_, col in ((0, 0), (1, 1025)):
        op(DVE, lambda E, w_=w_, col=col: E.tensor_copy(out=u1T_s[:, :, col],
                                                        in_=ph[:, 0:16].rearrange("p (k w) -> p k w", w=2)[:, :, w_]),
           r=[ph], w=[u1T_s])
    for sblk in range(4):
        rwkv_block(sblk, None, None, 1, None, None, stacked=True)

    kb.barrier()
    kb.release(mark_blk)
    yT_x = sb("yT2", [128, 256], F32)
    ybf_x = sb("ybf2", [128, 256], BF16)
    yc_x = sb("yc2", [128, 256], F32)
    sdv_x = sb("sdv2", [128, 256], F32)
    gzT_x = sb("gzT2", [128, KC, 256], BF16)
    oT_x = sb("oT2", [128, KC, 256], BF16)
    EXTs = sb("EXTs", [64, 4, 8, 128], F32)
    QTt = sb("QTt", [64, 4, 8, 64], F32)
    Xab = [sb(f"Xab{i}", [64, 8, 128], F32) for i in range(2)]
    CE = sb("CE", [64, 4, 8, 128], F32)
    QcT = sb("QcT", [64, 4, 8, 64], F32)
    S0v = sb("S0v", [64, 8, 64], F32)
    Bab = [sb(f"Bab{i}", [64, 8, 64], F32) for i in range(2)]
    accS = sb("accS", [64, 8, 64], F32)
    FIN = sb("FIN", [128, 4, 2, 16, 64], BF16)
    yx = [sb(f"yx{i}", [128, 2, 512], BF16) for i in range(2)]
    pl = [sb(f"pl{i}", [128, 2, 256], BF16) for i in range(2)]
    cin = kb.dram("cin", [2, 16, 64, 128], F32)
    cout = kb.dram("cout", [4, 2, 16, 64, 128], F32)
    op(POOL, lambda E: E.memset(FIN[:], 0.0), w=[FIN])
    for j in range(4):
        for e in range(2):
            op(POOL, lambda E, j=j, e=e: E.tensor_copy(
                out=FIN[64:128, j, e, :, :], in_=ident[64:128, 64:128].unsqueeze(1).to_broadcast([64, 16, 64])),
               r=[ident, FIN], w=[FIN])

    def load_ext(e, hg):
        for j in range(4):
            dma(SP, EXTs[:, j, :, :], ext_d[j, e, hg * 8:(hg + 1) * 8, :, :].rearrange("h k n -> k h n"),
                r=[ext_d], w=[EXTs], sem="EXTs")
        for j in range(4):
            p = npb()
            for hh in range(8):
                op(PE, lambda E, p=p, j=j, hh=hh: E.matmul(p[0:64, hh * 64:(hh + 1) * 64], lhsT=EXTs[:, j, hh, 0:64],
                                                           rhs=identf[0:64, 0:64], start=True, stop=True), r=[EXTs, identf], w=[p])
            op(ACT, lambda E, p=p, j=j: E.activation(out=QTt[:, j, :, :].rearrange("p h k -> p (h k)"), in_=p[0:64, :],
                                                     func=AF.Identity), r=[p], w=[QTt])

    for e in range(2):
        order = [0, 1, 2, 3] if e == 0 else [3, 2, 1, 0]
        for hg in range(2):
            load_ext(e, hg)
            cur_ap = EXTs[:, order[0], :, :]
            cur_trk = EXTs
            for step, j in enumerate(order[1:]):
                xn = Xab[step % 2]
                for half in range(2):
                    p = npb()
                    for h4 in range(4):
                        hh = half * 4 + h4
                        op(PE, lambda E, p=p, h4=h4, hh=hh, j=j, cur_ap=cur_ap: E.matmul(
                            p[0:64, h4 * 128:(h4 + 1) * 128], lhsT=QTt[:, j, hh, :], rhs=cur_ap[:, hh, :], start=True, stop=True),
                           r=[QTt, cur_trk], w=[p])
                    pv3 = p[0:64, :].rearrange("p (h n) -> p h n", h=4)
                    op(ACT, lambda E, pv3=pv3, xn=xn, half=half: E.activation(out=xn[:, half * 4:half * 4 + 4, 0:64], in_=pv3[:, :, 0:64],
                                                                              func=AF.Identity), r=[p], w=[xn])
                    op(DVE, lambda E, pv3=pv3, xn=xn, half=half, j=j: E.tensor_tensor(
                        out=xn[:, half * 4:half * 4 + 4, 64:128], in0=pv3[:, :, 64:128], in1=EXTs[:, j, half * 4:half * 4 + 4, 64:128],
                        op=ALU.add), r=[p, EXTs], w=[xn])
                cur_ap, cur_trk = xn[:], xn
            dma(SP, cin[e, hg * 8:(hg + 1) * 8, :, :].rearrange("h k n -> k h n"), cur_ap, r=[cur_trk], w=[cin], sem="cin")
    kb.collective("AllGather", ALU.bypass, GROUPS, cin[:].rearrange("e h k n -> (e h k) n"),
                  cout[:].rearrange("r e h k n -> (r e h k) n"), r=[cin], w=[cout], sem="ag3")

    for e in range(2):
        order = [0, 1, 2, 3] if e == 0 else [3, 2, 1, 0]
        for hg in range(2):
            load_ext(e, hg)
            for r_ in range(4):
                dma(SP, CE[:, r_, :, :], cout[r_, e, hg * 8:(hg + 1) * 8, :, :].rearrange("h k n -> k h n"),
                    r=[cout], w=[CE], sem="CE")
            for r_ in range(4):
                p = npb()
                for hh in range(8):
                    op(PE, lambda E, p=p, r_=r_, hh=hh: E.matmul(p[0:64, hh * 64:(hh + 1) * 64], lhsT=CE[:, r_, hh, 0:64],
                                                                 rhs=identf[0:64, 0:64], start=True, stop=True), r=[CE, identf], w=[p])
                op(ACT, lambda E, p=p, r_=r_: E.activation(out=QcT[:, r_, :, :].rearrange("p h k -> p (h k)"), in_=p[0:64, :],
                                                           func=AF.Identity), r=[p], w=[QcT])
            dma(SP, S0v[:], st0_d[e, hg * 8:(hg + 1) * 8, :, :].rearrange("h v k -> v h k"), w=[S0v], sem="S0v")
            p = npb()
            for hh in range(8):
                op(PE, lambda E, p=p, hh=hh: E.matmul(p[0:64, hh * 64:(hh + 1) * 64], lhsT=S0v[:, hh, :], rhs=identf[0:64, 0:64],
                                                      start=True, stop=True), r=[S0v, identf], w=[p])
            bcur = Bab[0]
            op(ACT, lambda E, p=p, bcur=bcur: E.activation(out=bcur[:].rearrange("p h v -> p (h v)"), in_=p[0:64, :],
                                                           func=AF.Identity), r=[p], w=[bcur])
            op(DVE, lambda E, bcur=bcur, e=e: E.tensor_scalar_mul(out=accS[:], in0=bcur[:], scalar1=selc[:, e, 0:1]),
               r=[bcur, selc], w=[accS])
            for i in range(3):
                r_ = order[i]
                bn = Bab[(i + 1) % 2]
                p = npb()
                for hh in range(8):
                    op(PE, lambda E, p=p, hh=hh, r_=r_, bcur=bcur: E.matmul(
                        p[0:64, hh * 64:(hh + 1) * 64], lhsT=QcT[:, r_, hh, :], rhs=bcur[:, hh, :], start=True, stop=True),
                       r=[QcT, bcur], w=[p])
                op(DVE, lambda E, p=p, bn=bn, r_=r_: E.tensor_tensor(
                    out=bn[:], in0=p[0:64, :].rearrange("p (h v) -> p h v", h=8), in1=CE[:, r_, :, 64:128], op=ALU.add),
                   r=[p, CE], w=[bn])
                op(DVE, lambda E, bn=bn, e=e, i=i: E.scalar_tensor_tensor(
                    out=accS[:], in0=bn[:], scalar=selc[:, e, i + 1:i + 2], in1=accS[:], op0=ALU.mult, op1=ALU.add),
                   r=[bn, selc, accS], w=[accS])
                bcur = bn
            scur = accS
            for idx, j in enumerate(order):
                op(ACT, lambda E, scur=scur, j=j, e=e, hg=hg: E.activation(
                    out=FIN[0:64, j, e, hg * 8:(hg + 1) * 8, :], in_=scur[:], func=AF.Identity), r=[scur], w=[FIN])
                if idx < 3:
                    sn = Bab[idx % 2]
                    p = npb()
                    for hh in range(8):
                        op(PE, lambda E, p=p, hh=hh, j=j, scur=scur: E.matmul(
                            p[0:64, hh * 64:(hh + 1) * 64], lhsT=QTt[:, j, hh, :], rhs=scur[:, hh, :], start=True, stop=True),
                           r=[QTt, scur], w=[p])
                    op(DVE, lambda E, p=p, sn=sn, j=j: E.tensor_tensor(
                        out=sn[:], in0=p[0:64, :].rearrange("p (h v) -> p h v", h=8), in1=EXTs[:, j, :, 64:128], op=ALU.add),
                       r=[p, EXTs], w=[sn])
                    scur = sn

    nfx = 0
    for j in range(4):
        for ct in range(KC):
            yx_, pl_ = yx[nfx % 2], pl[nfx % 2]
            nfx += 1
            dma(SP, yx_[:], yext_d[j, :, ct, :, :].rearrange("e p n -> p e n"), r=[yext_d], w=[yx_], sem=yx_.name)
            dma(SP, pl_[:], post_d[j, ct, :, :, :].rearrange("w p t -> p w t"), r=[post_d], w=[pl_], sem=pl_.name)
            py = npb()
            for h2 in range(2):
                lo = 64 * h2
                for e in range(2):
                    op(PE, lambda E, py=py, lo=lo, h2=h2, e=e, j=j, ct=ct, yx_=yx_: E.matmul(
                        py[lo:lo + 64, 0:256], lhsT=FIN[:, j, e, 2 * ct + h2, :], rhs=yx_[:, e, h2 * 256:(h2 + 1) * 256],
                        start=(e == 0), stop=(e == 1)), r=[FIN, yx_], w=[py])
            op(ACT, lambda E, py=py: E.activation(out=yT_x[:], in_=py[:, 0:256], func=AF.Identity), r=[py], w=[yT_x])
            op(POOL, lambda E, ct=ct, pl_=pl_: E.tensor_copy(out=gzT_x[:, ct, :], in_=pl_[:, 0, :]), r=[pl_], w=[gzT_x])
            post_ct(ct, (pl_[:, 1, :], pl_), (yT_x, ybf_x, yc_x, sdv_x, gzT_x, oT_x))
        wo = [load_wh(4, 0), load_wh(4, 1)]
        for c in range(2):
            r0 = 1024 + j * 256 + c * 128
            y = out_proj_ln(0, x1d[r0:r0 + 128, :], oT_x, c * 128, oT_x, (lambda nh, wo=wo: (wo[nh], wo[nh][:, :, :])), 1, 1)
            dma(SP, o_ys[j * 256 + c * 128:j * 256 + (c + 1) * 128, :], y[:], r=[y], sem=y.name)

    return kb.finish(outs_final)


_NC_CACHE = {}


def _dft_tables():
    i = np.arange(128, dtype=np.float64)
    ang = 2 * np.pi * np.outer(i, i) / 128.0
    dftc = np.concatenate([np.cos(ang), -np.sin(ang)], 1) / np.sqrt(128.0)
    l = np.arange(256, dtype=np.float64)
    ang = 2 * np.pi * np.outer(l, l) / 256.0
    dftl = np.concatenate([np.cos(ang), np.sin(ang)], 1) / 16.0
    return dftc.astype(np.float32), dftl.astype(np.float32)


_DFTC, _DFTL256 = _dft_tables()


def _na_tables():
    half = np.arange(128)[:, None, None] // 64
    ck = np.arange(128)[:, None, None] % 64
    idx = np.arange(30)[None, :, None]
    cq = np.arange(64)[None, None, :]
    dr = 14 - idx + half + 0 * cq
    dc = np.clip(ck - cq + 15, 0, 30) + 0 * idx
    cstart = np.clip(cq - 8, 0, 48)
    col_in = (ck >= cstart) & (ck < cstart + 16)
    valid = (np.abs(dr) <= 7) & col_in
    dri = np.clip(dr + 7, 0, 14)
    qmask = np.zeros((4, 24, 1024), np.float32)
    for qd in range(4):
        for q in range(1024):
            r = 16 * qd + q // 64
            st = min(max(r - 4, 0), 56)
            for j in range(24):
                R = 16 * qd - 4 + j
                if not (st <= R < st + 8):
                    qmask[qd, j, q] = -1e30
    rowoh = (np.arange(1536)[None, :] // 64 == np.arange(24)[:, None]).astype(np.float32)
    return dri.astype(np.int64), dc.astype(np.int64), valid, qmask, rowoh


_BB_DR, _BB_DC, _BB_VALID, _QMASK, _ROWOH = _na_tables()
_DFTS_CACHE = {}


def _dfts(qd):
    if qd not in _DFTS_CACHE:
        l = np.arange(4096, dtype=np.int64)[:, None]
        m = (1024 * qd + np.arange(1024, dtype=np.int64))[None, :]
        ang = 2 * np.pi * ((l * m) % 4096).astype(np.float64) / 4096.0
        t = np.stack([np.cos(ang), np.sin(ang)], 1) / 64.0
        _DFTS_CACHE[qd] = np.ascontiguousarray(t.astype(np.float32).astype(ml_dtypes.bfloat16))
    return _DFTS_CACHE[qd]


def _scan_tables():
    i = np.arange(128)
    s_, t_ = i[:, None], i[None, :]
    us, ls = (s_ < t_).astype(np.float32), (s_ > t_).astype(np.float32)
    ui, li = (s_ <= t_).astype(np.float32), (s_ >= t_).astype(np.float32)
    tri = np.stack([-0.606531 * ui, -0.606531 * li], 1)
    mask4 = np.stack([np.concatenate([-us, ui, us, ui], 1), np.concatenate([-ls, li, ls, li], 1)], 1)
    namask = np.stack([np.concatenate([-ls, -ls], 1), np.concatenate([-us, -us], 1)], 1)
    blk = (s_ // 64 == t_ // 64).astype(np.float32)
    bo = np.stack([blk, blk / 64.0], 1)
    c = np.ascontiguousarray
    return c(tri.astype(np.float32)), c(mask4), c(namask), c(bo.astype(np.float32))


_TRI, _MASK4, _NAMASK, _BO = _scan_tables()


def _rw_vecs(inp):
    f = lambda a: np.asarray(a, dtype=np.float32)
    rows = [f(inp["rw_mu"])[0][i] for i in range(6)]
    rows += [f(inp["rw_k_k"])[0], f(inp["rw_k_a"])[0], f(inp["rw_r_k"])[0].reshape(-1), f(inp["rw_lnx_g"])[0],
             f(inp["rw_lnx_b"])[0], f(inp["rw_a0"])[0][0], f(inp["rw_a0"])[0][1], np.zeros(1024, np.float32)]
    v = np.stack(rows, 0)
    return np.ascontiguousarray(v.reshape(14, KC, 128).transpose(2, 0, 1))


def _prep_inputs(inp):
    f = lambda a: np.ascontiguousarray(np.asarray(a, dtype=np.float32))
    x_prompt = f(inp["x_prompt"])
    c = f(inp["c"])
    c_ctx = f(inp["c_ctx"])
    x_sample = f(inp["x_sample"])
    cache_k = f(inp["cache_k"])
    cache_v = f(inp["cache_v"])
    rpb = f(inp["ev_rpb"])[0]
    state_rwkv = f(inp["state_rwkv"])
    na_bias = np.where(_BB_VALID[None], rpb[:, _BB_DR, _BB_DC], np.float32(-1e30)).astype(np.float32).reshape(8, 128, 30 * 64)
    ada_b = f(inp["ada_b"])
    common = {
        "ada_w": f(inp["ada_w"]),
        "ada_b": ada_b,
        "ada_bT": np.ascontiguousarray(ada_b.reshape(2, 24, 128).transpose(0, 2, 1)),
        "ev_w_in": f(inp["ev_w_in"])[0],
        "ev_w_fnet": f(inp["ev_w_fnet"])[0],
        "ev_w_out": f(inp["ev_w_out"])[0],
        "post_ln_g": f(inp["post_ln_g"]),
        "post_ln_b": f(inp["post_ln_b"]),
        "rw_w_rkvz": f(inp["rw_w_rkvz"])[0],
        "rw_w1": f(inp["rw_w1"])[0], "rw_w2": f(inp["rw_w2"])[0],
        "rw_a1": f(inp["rw_a1"])[0], "rw_a2": f(inp["rw_a2"])[0],
        "rw_g1": f(inp["rw_g1"])[0], "rw_g2": f(inp["rw_g2"])[0],
        "rw_w0": f(inp["rw_w0"])[0], "rw_w_out": f(inp["rw_w_out"])[0],
        "rw_vecs": _rw_vecs(inp),
        "tri": _TRI, "mask4": _MASK4, "namask": _NAMASK, "blockones": _BO,
        "dftc": _DFTC,
        "dftl256": _DFTL256,
    }
    maps = []
    for core in range(NCORES):
        b = core // 4
        cv = np.stack([c_ctx, c[b]], 0)
        m = dict(common)
        m["xp"] = x_prompt[4 * core:4 * core + 4].reshape(1024, D)
        qd = core % 4
        win = np.zeros((24, 64, D), np.float32)
        r0 = 16 * qd - 4
        lo_, hi_ = max(r0, 0), min(r0 + 24, 64)
        win[lo_ - r0:hi_ - r0] = x_sample[b].reshape(64, 64, D)[lo_:hi_]
        m["xs_win"] = win.reshape(1536, D)
        m["cache_k"] = cache_k[b, 0].reshape(512, 512)
        m["cache_v"] = cache_v[b, 0].reshape(512, 512)
        m["na_bias"] = na_bias
        m["na_qmask"] = _QMASK[qd]
        m["na_rowoh"] = _ROWOH
        m["dfts"] = _dfts(qd)
        m["state0"] = state_rwkv[b, 0]
        selc = np.zeros((64, 2, 4), np.float32)
        selc[:, 0, qd] = 1.0
        selc[:, 1, 3 - qd] = 1.0
        m["selc"] = selc
        selh = np.zeros((8, 2), np.float32)
        if qd > 0:
            selh[2 * (qd - 1) + 1, 0] = 1.0
        if qd < 3:
            selh[2 * (qd + 1), 1] = 1.0
        m["selh"] = selh
        m["cvT"] = np.ascontiguousarray(cv.reshape(2, KC, 128).transpose(2, 1, 0))
        maps.append(m)
    return maps


def kernel(_stage=99, _raw=False, **inputs):
    if _stage not in _NC_CACHE:
        _NC_CACHE[_stage] = build(_stage)
    nc = _NC_CACHE[_stage]
    maps = _prep_inputs(inputs)
    res = run_bass_kernel_spmd(nc, maps, core_ids=list(range(NCORES)))
    R = res.results
    if _raw:
        return R
    y_prompt = np.concatenate([R[c]["o_yp"].reshape(4, 256, D) for c in range(NCORES)], 0)
    y_sample = np.concatenate([R[c]["o_ys"] for c in range(NCORES)], 0).reshape(2, 4096, D)
    new_k = np.concatenate([R[c]["o_newk"].reshape(4, 1, 256, 8, 64) for c in range(NCORES)], 0)
    new_v = np.concatenate([R[c]["o_newv"].reshape(4, 1, 256, 8, 64) for c in range(NCORES)], 0)
    new_state = np.concatenate([R[c]["o_state"].reshape(4, 1, 2, 16, 64, 64) for c in range(NCORES)], 0)
    return (y_prompt, y_sample, new_k, new_v, new_state)
```
